# Optimizing a Trainium2 kernel written in Bass

```python
import jax, jax.numpy as jnp
from jax import lax
import numpy as np

D_MODEL = 1024
BATCH = 8
SEQ = 4096
DEPTH = 2

N_BRANCH = 4
BRANCH_W = D_MODEL // 4
FOURIER_GROUPS = 4
FOURIER_GW = BRANCH_W // FOURIER_GROUPS
CONV_WIDTH = 31
CONV_PAD = CONV_WIDTH // 2
LN_EPS = 1e-5
HEAD_DIM = 64
HEADS_PER_GROUP = BRANCH_W // HEAD_DIM
DILATED_CFG = ((128, 1), (512, 4), (2048, 16))
N_ATT_GROUPS = len(DILATED_CFG)
ATT_QKV_W = N_ATT_GROUPS * HEADS_PER_GROUP * HEAD_DIM
ROPE_THETA = 10000.0
NEG_BIG = -1e30
POOL_SIZES = (2, 4, 8, 16)
POOL_GROUPS = len(POOL_SIZES)
POOL_GW = BRANCH_W // POOL_GROUPS
NORM_EPS = 1e-6

IN_A = BRANCH_W
IN_B = 2 * BRANCH_W
IN_C = 3 * ATT_QKV_W
IN_D = BRANCH_W
IN_VALUE = IN_A + IN_B + IN_C + IN_D
D_IN = IN_VALUE + N_BRANCH * BRANCH_W
SPLIT_POINTS = (IN_A, IN_A + IN_B, IN_A + IN_B + IN_C, IN_VALUE)

kernel_name = "hybrid_parallel_gated_mixer_encoder"


def rms_norm(x, g):
    xf = x.astype(jnp.float32)
    y = xf * lax.rsqrt(jnp.mean(xf * xf, axis=-1, keepdims=True) + NORM_EPS)
    return (y * g.astype(jnp.float32)).astype(x.dtype)


def layer_norm(x, g, b):
    xf = x.astype(jnp.float32)
    mu = jnp.mean(xf, axis=-1, keepdims=True)
    var = jnp.mean(jnp.square(xf - mu), axis=-1, keepdims=True)
    y = (xf - mu) * lax.rsqrt(var + LN_EPS)
    return (y * g.astype(jnp.float32) + b.astype(jnp.float32)).astype(x.dtype)


def rope_tables(seq):
    inv = 1.0 / (ROPE_THETA ** (jnp.arange(0, HEAD_DIM, 2, dtype=jnp.float32) / HEAD_DIM))
    ang = jnp.arange(seq, dtype=jnp.float32)[:, None] * inv[None, :]
    return jnp.cos(ang), jnp.sin(ang)


def apply_rope(t, cos, sin):
    half = t.shape[-1] // 2
    t1 = t[..., :half].astype(jnp.float32)
    t2 = t[..., half:].astype(jnp.float32)
    c = cos[None, :, None, :]
    s = sin[None, :, None, :]
    return jnp.concatenate([t1 * c - t2 * s, t2 * c + t1 * s], axis=-1).astype(t.dtype)


def fourier_mix(u, w_lin):
    B, S, _ = u.shape
    ug = u.astype(jnp.float32).reshape(B, S, FOURIER_GROUPS, FOURIER_GW)
    f = jnp.fft.fft2(ug, axes=(1, 3), norm="ortho").real
    f = f.astype(u.dtype).reshape(B, S, BRANCH_W)
    return f @ w_lin


def conformer_conv(u2, conv_w, conv_b, ln_g, ln_b, w_pw):
    a, g = jnp.split(u2, 2, axis=-1)
    u = a * jax.nn.sigmoid(g)
    y = lax.conv_general_dilated(
        u, conv_w.astype(u.dtype)[:, None, :], window_strides=(1,),
        padding=[(CONV_PAD, CONV_PAD)], dimension_numbers=("NWC", "WIO", "NWC"),
        feature_group_count=BRANCH_W) + conv_b
    y = jax.nn.silu(layer_norm(y, ln_g, ln_b))
    return y @ w_pw


def band_attention(q, k, v, half):
    N, L, H, dh = q.shape
    blk = half
    nb = -(-L // blk)
    Lp = nb * blk
    pad = Lp - L
    q = jnp.pad(q, ((0, 0), (0, pad), (0, 0), (0, 0)))
    k = jnp.pad(k, ((0, 0), (blk, pad + blk), (0, 0), (0, 0)))
    v = jnp.pad(v, ((0, 0), (blk, pad + blk), (0, 0), (0, 0)))
    qb = q.reshape(N, nb, blk, H, dh)
    kb = k.reshape(N, nb + 2, blk, H, dh)
    vb = v.reshape(N, nb + 2, blk, H, dh)
    kw = jnp.concatenate([kb[:, :-2], kb[:, 1:-1], kb[:, 2:]], axis=2)
    vw = jnp.concatenate([vb[:, :-2], vb[:, 1:-1], vb[:, 2:]], axis=2)
    s = jnp.einsum("nbqhd,nbkhd->nbhqk", qb, kw).astype(jnp.float32) * (HEAD_DIM ** -0.5)
    qi = jnp.arange(nb)[:, None, None] * blk + jnp.arange(blk)[None, :, None]
    kj = (jnp.arange(nb)[:, None, None] - 1) * blk + jnp.arange(3 * blk)[None, None, :]
    valid = (jnp.abs(qi - kj) <= half) & (kj >= 0) & (kj < L)
    s = jnp.where(valid[None, :, None], s, NEG_BIG)
    m = jnp.max(s, axis=-1, keepdims=True)
    p = jnp.exp(s - m)
    den = jnp.sum(p, axis=-1, keepdims=True)
    o = jnp.einsum("nbhqk,nbkhd->nbqhd", (p / den).astype(v.dtype), vw)
    lse = (m + jnp.log(den))[..., 0]
    o = o.reshape(N, Lp, H, dh)[:, :L]
    lse = lse.transpose(0, 1, 3, 2).reshape(N, Lp, H)[:, :L]
    return o, lse


def dilated_window_attention(q, k, v, dil, half):
    B, S, H, dh = q.shape
    L = S // dil

    def to_res(t):
        return t.reshape(B, L, dil, H, dh).transpose(0, 2, 1, 3, 4).reshape(B * dil, L, H, dh)

    o, lse = band_attention(to_res(q), to_res(k), to_res(v), half)
    o = o.reshape(B, dil, L, H, dh).transpose(0, 2, 1, 3, 4).reshape(B, S, H, dh)
    lse = lse.reshape(B, dil, L, H).transpose(0, 2, 1, 3).reshape(B, S, H)
    return o, lse


def dilated_mixture(qkv, cos, sin):
    B, S, _ = qkv.shape
    qkv = qkv.reshape(B, S, 3, N_ATT_GROUPS * HEADS_PER_GROUP, HEAD_DIM)
    q = apply_rope(qkv[:, :, 0], cos, sin).reshape(B, S, N_ATT_GROUPS, HEADS_PER_GROUP, HEAD_DIM)
    k = apply_rope(qkv[:, :, 1], cos, sin).reshape(B, S, N_ATT_GROUPS, HEADS_PER_GROUP, HEAD_DIM)
    v = qkv[:, :, 2].reshape(B, S, N_ATT_GROUPS, HEADS_PER_GROUP, HEAD_DIM)
    outs, lses = [], []
    for g, (window, dil) in enumerate(DILATED_CFG):
        o, l = dilated_window_attention(q[:, :, g], k[:, :, g], v[:, :, g], dil, window // (2 * dil))
        outs.append(o)
        lses.append(l)
    alpha = jax.nn.softmax(jnp.stack(lses, axis=0), axis=0)
    out = outs[0] * alpha[0][..., None].astype(outs[0].dtype)
    for g in range(1, N_ATT_GROUPS):
        out = out + outs[g] * alpha[g][..., None].astype(outs[g].dtype)
    return out.reshape(B, S, BRANCH_W)


def multiscale_pool(u, w_pool, pool_scale):
    B, S, _ = u.shape
    ug = u.astype(jnp.float32).reshape(B, S, POOL_GROUPS, POOL_GW)
    c = lax.cumsum(ug, axis=1)
    c = jnp.pad(c, ((0, 0), (1, 0), (0, 0), (0, 0)))
    pos = jnp.arange(S)
    outs = []
    for gi, size in enumerate(POOL_SIZES):
        lo = jnp.clip(pos - size // 2, 0, S - 1)
        hi = jnp.clip(pos + size - 1 - size // 2, 0, S - 1)
        cg = c[:, :, gi]
        win_sum = cg[:, hi + 1] - cg[:, lo]
        cnt = (hi - lo + 1).astype(jnp.float32)[None, :, None]
        outs.append(win_sum / cnt - ug[:, :, gi])
    pooled = jnp.stack(outs, axis=2).astype(u.dtype)
    y = jnp.einsum("bsgc,gcd->bsgd", pooled, w_pool).reshape(B, S, BRANCH_W)
    return y * pool_scale


def hybrid_layer(x, cos, sin, norm_g, w_in, w_fourier, conv_w, conv_b, conv_ln_g, conv_ln_b,
                 w_pw, w_pool, pool_scale, w_branch, w_gate, b_gate, w_out):
    B, S, _ = x.shape
    h = rms_norm(x, norm_g)
    z = h @ w_in
    u_a, u_b, u_c, u_d, u_gate = jnp.split(z, SPLIT_POINTS, axis=-1)
    gate_paths = jax.nn.silu(u_gate).reshape(B, S, N_BRANCH, BRANCH_W)
    branch_outs = (
        fourier_mix(u_a, w_fourier),
        conformer_conv(u_b, conv_w, conv_b, conv_ln_g, conv_ln_b, w_pw),
        dilated_mixture(u_c, cos, sin),
        multiscale_pool(u_d, w_pool, pool_scale),
    )
    merged = None
    for n, y in enumerate(branch_outs):
        y_n = (y * gate_paths[:, :, n]) @ w_branch[n]
        mg = jax.nn.sigmoid(h @ w_gate[n] + b_gate[n])
        merged = mg * y_n if merged is None else merged + mg * y_n
    return x + merged @ w_out


def setup_inputs(seed: int = 0) -> dict:
    key = jax.random.key(seed)
    ks = jax.random.split(key, 16)
    f32 = jnp.float32

    def nrm(k, shape, fan_in):
        return jax.random.normal(k, shape, f32) * (fan_in ** -0.5)

    return {
        "x": jax.random.normal(ks[0], (BATCH, SEQ, D_MODEL), f32),
        "norm_g": 1.0 + 0.05 * jax.random.normal(ks[1], (DEPTH, D_MODEL), f32),
        "w_in": nrm(ks[2], (DEPTH, D_MODEL, D_IN), D_MODEL),
        "w_fourier": nrm(ks[3], (DEPTH, BRANCH_W, BRANCH_W), BRANCH_W),
        "conv_w": nrm(ks[4], (DEPTH, CONV_WIDTH, BRANCH_W), CONV_WIDTH),
        "conv_b": 0.02 * jax.random.normal(ks[5], (DEPTH, BRANCH_W), f32),
        "conv_ln_g": 1.0 + 0.05 * jax.random.normal(ks[6], (DEPTH, BRANCH_W), f32),
        "conv_ln_b": 0.02 * jax.random.normal(ks[7], (DEPTH, BRANCH_W), f32),
        "w_pw": nrm(ks[8], (DEPTH, BRANCH_W, BRANCH_W), BRANCH_W),
        "w_pool": nrm(ks[9], (DEPTH, POOL_GROUPS, POOL_GW, POOL_GW), POOL_GW),
        "pool_scale": 1.0 + 0.1 * jax.random.normal(ks[10], (DEPTH, BRANCH_W), f32),
        "w_branch": nrm(ks[11], (DEPTH, N_BRANCH, BRANCH_W, D_MODEL), BRANCH_W),
        "w_gate": nrm(ks[12], (DEPTH, N_BRANCH, D_MODEL, D_MODEL), D_MODEL),
        "b_gate": 0.1 * jax.random.normal(ks[13], (DEPTH, N_BRANCH, D_MODEL), f32),
        "w_out": nrm(ks[14], (DEPTH, D_MODEL, D_MODEL), D_MODEL),
        "final_g": 1.0 + 0.05 * jax.random.normal(ks[15], (D_MODEL,), f32),
    }


def reference(x, norm_g, w_in, w_fourier, conv_w, conv_b, conv_ln_g, conv_ln_b, w_pw,
              w_pool, pool_scale, w_branch, w_gate, b_gate, w_out, final_g):
    cos, sin = rope_tables(x.shape[1])
    for l in range(DEPTH):
        x = hybrid_layer(x, cos, sin, norm_g[l], w_in[l], w_fourier[l], conv_w[l], conv_b[l],
                         conv_ln_g[l], conv_ln_b[l], w_pw[l], w_pool[l], pool_scale[l],
                         w_branch[l], w_gate[l], b_gate[l], w_out[l])
    return rms_norm(x, final_g)
```

```python
import bisect
import numpy as np
import concourse.bass as bass
import concourse.mybir as mybir

F32 = mybir.dt.float32
BF16 = mybir.dt.bfloat16
ALU = mybir.AluOpType
AF = mybir.ActivationFunctionType
AX = mybir.AxisListType

ENGS = ("pe", "act", "dve", "pool", "sp")


class Buf:
    __slots__ = ("name", "last_w", "readers")

    def __init__(self, name):
        self.name = name
        self.last_w = []
        self.readers = {}


class Op:
    __slots__ = ("eng", "fn", "deps", "gidx", "eidx", "marked", "dma", "cum", "tag")

    def __init__(self, eng, fn, gidx):
        self.eng = eng
        self.fn = fn
        self.deps = []
        self.gidx = gidx
        self.eidx = -1
        self.marked = False
        self.dma = None
        self.tag = None
        self.cum = 0


class Prog:
    def __init__(self, nc, n_dma_sems_sp=24, n_dma_sems_pool=16):
        self.nc = nc
        self.ops = []
        self.eops = {e: [] for e in ENGS}
        self.ndma = {"sp": n_dma_sems_sp, "pool": n_dma_sems_pool, "act": 16}
        self.dma_rr = {"sp": 0, "pool": 0, "act": 0}
        self.dma_cnt = {}
        self.dma_last = {}
        self.bufs = {}
        self.epoch = []
        import os as _os
        self.tagging = bool(_os.environ.get('KTAG'))

    def buf(self, name):
        b = self.bufs.get(name)
        if b is None:
            b = Buf(name)
            b.last_w = list(self.epoch)
            self.bufs[name] = b
        return b

    def barrier(self):
        toks = []
        for e in ENGS:
            if self.eops[e]:
                o = self.eops[e][-1]
                if o.dma is None:
                    toks.append(("op", o))
                else:
                    for q in reversed(self.eops[e]):
                        if q.dma is None:
                            toks.append(("op", q))
                            break
        for key, t in self.dma_last.items():
            toks.append(t)
        self.epoch = toks
        for b in self.bufs.values():
            b.last_w = list(toks)
            b.readers = {}

    def _mkop(self, eng, fn, reads, writes):
        op = Op(eng, fn, len(self.ops))
        if self.tagging:
            import sys as _sys
            fr = _sys._getframe(2)
            while fr is not None and fr.f_code.co_name not in ("build_program", "norm_tile", "gate_tile"):
                fr = fr.f_back
            op.tag = "L%d" % fr.f_lineno if fr is not None else "?"
        op.eidx = len(self.eops[eng])
        self.ops.append(op)
        self.eops[eng].append(op)
        deps = op.deps
        for b in reads:
            for t in b.last_w:
                deps.append((t, "RAW"))
            if b.name.startswith("bank"):
                for k, t in b.readers.items():
                    if k != eng:
                        deps.append((t, "RAW"))
        for b in writes:
            for t in b.last_w:
                deps.append((t, "WAW"))
            for t in b.readers.values():
                deps.append((t, "WAR"))
        return op

    def _commit(self, op, tok, reads, writes):
        for b in writes:
            b.last_w = [tok]
            b.readers = {}
        for b in reads:
            if tok[0] == "op":
                b.readers[op.eng] = tok
            else:
                b.readers[("dma", tok[1], tok[2])] = tok

    def op(self, eng, fn, reads=(), writes=()):
        reads = [self.buf(x) if isinstance(x, str) else x for x in reads]
        writes = [self.buf(x) if isinstance(x, str) else x for x in writes]
        op = self._mkop(eng, fn, reads, writes)
        tok = ("op", op)
        self._commit(op, tok, reads, writes)
        return op

    def dma(self, eng, out, in_, reads=(), writes=(), **kw):
        reads = [self.buf(x) if isinstance(x, str) else x for x in reads]
        writes = [self.buf(x) if isinstance(x, str) else x for x in writes]

        def fn(e, out=out, in_=in_, kw=kw):
            return e.dma_start(out=out, in_=in_, **kw)

        op = self._mkop(eng, fn, reads, writes)
        k = self.dma_rr[eng]
        self.dma_rr[eng] = (k + 1) % self.ndma[eng]
        key = (eng, k)
        prev = self.dma_last.get(key)
        if prev is not None:
            op.deps.append((prev, "SEM"))
        cnt = self.dma_cnt.get(key, 0) + 1
        self.dma_cnt[key] = cnt
        tok = ("dma", key, 16 * cnt)
        self.dma_last[key] = tok
        op.dma = (key, 16 * cnt)
        self._commit(op, tok, reads, writes)
        return op

    def emit(self):
        nc = self.nc
        for op in self.ops:
            for (t, kind) in op.deps:
                if t[0] != "op":
                    continue
                p = t[1]
                if p.eng == op.eng and p.eng == "pe":
                    continue
                if kind != "WAR":
                    p.marked = True
        marked_idx = {e: [o.eidx for o in self.eops[e] if o.marked] for e in ENGS}
        resolved = {}
        for op in self.ops:
            for (t, kind) in op.deps:
                if t[0] != "op" or kind != "WAR":
                    continue
                p = t[1]
                if p.eng == op.eng and p.eng == "pe":
                    continue
                if p.marked:
                    continue
                lst = marked_idx[p.eng]
                i = bisect.bisect_left(lst, p.eidx)
                ok = False
                if i < len(lst):
                    q = self.eops[p.eng][lst[i]]
                    if q.gidx < op.gidx:
                        resolved[(id(op), id(p))] = q
                        ok = True
                if not ok:
                    p.marked = True
                    bisect.insort(lst, p.eidx)
        for e in ENGS:
            c = 0
            for o in self.eops[e]:
                if o.marked:
                    c += 1
                o.cum = c
        sems = {}
        ctx = []
        for e in ENGS:
            g = nc.semaphore("s_" + e)
            sems[e] = g.__enter__()
            ctx.append(g)
        dsem = {}
        for key in self.dma_cnt:
            g = nc.semaphore("d_%s_%d" % key)
            dsem[key] = g.__enter__()
            ctx.append(g)
        self.n_waits = 0

        def run_engine(ename, eng):
            waited = {}
            for o in self.eops[ename]:
                need = {}
                for (t, kind) in o.deps:
                    if t[0] == "op":
                        p = t[1]
                        if p.eng == ename and ename == "pe":
                            continue
                        if not p.marked:
                            p = resolved[(id(o), id(p))]
                        key = ("e", p.eng)
                        val = p.cum
                    else:
                        key = ("d", t[1])
                        val = t[2]
                    if val > need.get(key, 0):
                        need[key] = val
                for key, val in need.items():
                    if val > waited.get(key, 0):
                        waited[key] = val
                        s = sems[key[1]] if key[0] == "e" else dsem[key[1]]
                        eng.wait_ge(s, val)
                        self.n_waits += 1
                ins = o.fn(eng)
                if o.tag is not None:
                    ins.annotate(o.tag)
                if o.dma is not None:
                    ins.then_inc(dsem[o.dma[0]], 16)
                elif o.marked:
                    ins.then_inc(sems[ename], 1)
            if ename == "sp":
                for key, cnt in self.dma_cnt.items():
                    if 16 * cnt > waited.get(("d", key), 0):
                        eng.wait_ge(dsem[key], 16 * cnt)

        with nc.Block() as block:
            @block.tensor
            def _(eng):
                run_engine("pe", eng)

            @block.scalar
            def _(eng):
                run_engine("act", eng)

            @block.vector
            def _(eng):
                run_engine("dve", eng)

            @block.gpsimd
            def _(eng):
                run_engine("pool", eng)

            @block.sync
            def _(eng):
                run_engine("sp", eng)
        for g in reversed(ctx):
            g.__exit__(None, None, None)
S = 4096
D = 1024
NBLK = 8
NT = 32
KPAD = 1024
DIL = (1, 4, 16)
NORM_EPS = 1e-6
LN_EPS = 1e-5
NV = 128
V_CB, V_LG, V_LB, V_PS, V_BG, V_IS, V_CW = 0, 2, 4, 6, 8, 40, 42


class Arena:
    def __init__(self, nc, nbytes):
        self.nbytes = nbytes
        self.h16 = nc.alloc_sbuf_tensor("arena", [128, nbytes // 2], BF16)
        self.h32 = self.h16.bitcast(F32)
        self.off = 0

    def reset(self):
        self.off = 0

    def alloc(self, free_shape, dt):
        es = 2 if dt == BF16 else 4
        n = 1
        for s in free_shape:
            n *= s
        nb = (n * es + 63) // 64 * 64
        assert self.off + nb <= self.nbytes, ("arena overflow", self.off, nb, self.nbytes)
        h = self.h16 if dt == BF16 else self.h32
        ps = self.nbytes // es
        dims = [[ps, 128]]
        st = n
        for s in free_shape:
            st //= s
            dims.append([st, s])
        ap = bass.AP(h, self.off // es, dims)
        self.off += nb
        return ap


def build_program(nc, n_layers=2, dbg=False, stop=None):
    P = Prog(nc)
    skind = "ExternalOutput" if dbg else "Internal"

    def din(name, shape, dt=F32):
        return nc.dram_tensor(name, list(shape), dt, kind="ExternalInput").ap()

    L = 2
    x_d = din("x", [S, D])
    w_in_d = din("w_in", [L, D, 4352])
    w_fourier_d = din("w_fourier", [L, 256, 256])
    w_pw_d = din("w_pw", [L, 256, 256])
    w_pool_d = din("w_pool", [L, 4, 64, 64])
    w_branch_d = din("w_branch", [L, 4, 256, 1024])
    w_gate_d = din("w_gate", [L, 4, D, D])
    w_out_d = din("w_out", [L, D, D])
    gvec_h = nc.dram_tensor("gvec", [L + 1, D], F32, kind="ExternalInput")
    vecs_d = din("vecs", [L, 128, NV])
    ident_d = din("ident", [128, 128], BF16)
    masks_d = din("masks", [128, 3, 512], BF16)
    ropec_d = din("ropec", [128, S])
    ropes_d = din("ropes", [128, S])
    dft_d = din("dft", [4, 16, 128, 2, 2, 512], BF16)
    dmid_d = din("dmid", [128, 32], BF16)
    cs_d = din("cs", [128, 2, 512], BF16)
    edge_d = din("edge", [128, 2, 16])
    out_d = nc.dram_tensor("out", [S, D], F32, kind="ExternalOutput").ap()
    yg_d = nc.dram_tensor("yg_scr", [1024, S], BF16, kind=skind).ap()
    mT_d = nc.dram_tensor("mT_scr", [1024, S], BF16, kind=skind).ap()
    xmid_d = nc.dram_tensor("xmid_scr", [S, D], F32, kind=skind).ap()
    hdbg_d = nc.dram_tensor("hT_dbg", [128, 8, S], BF16, kind="ExternalOutput").ap() if dbg else None

    hT = nc.alloc_sbuf_tensor("hT", [128, 8, S], BF16)
    ident = nc.alloc_sbuf_tensor("identsb", [128, 128], BF16)
    masks = nc.alloc_sbuf_tensor("maskssb", [128, 3, 512], BF16)
    gb = nc.alloc_sbuf_tensor("gb", [128, D], F32)
    vecs = nc.alloc_sbuf_tensor("vecssb", [128, NV], F32)
    onesm = nc.alloc_sbuf_tensor("onesm", [128, 128], F32)
    arena = Arena(nc, 138752)
    banks = [nc.alloc_psum_tensor("bank%d" % i, [128, 512], F32) for i in range(8)]
    bank_rr = [0]

    def pb():
        i = bank_rr[0]
        bank_rr[0] = (i + 1) % 8
        return banks[i], "bank%d" % i

    uid = [0]

    def nm(s):
        uid[0] += 1
        return "%s#%d" % (s, uid[0])

    class Rot:
        def __init__(self, name, n, shape, dt):
            self.tiles = [(arena.alloc(shape, dt), nm(name)) for _ in range(n)]
            self.i = 0

        def next(self):
            t = self.tiles[self.i]
            self.i = (self.i + 1) % len(self.tiles)
            return t

    def mm(out, lhsT, rhs, start, stop, reads, writes, tp=None):
        if tp is None:
            P.op("pe", lambda e: e.matmul(out, lhsT=lhsT, rhs=rhs, start=start, stop=stop), reads, writes)
        else:
            P.op("pe", lambda e: e.matmul(out, lhsT=lhsT, rhs=rhs, start=start, stop=stop, tile_position=tp), reads, writes)

    def act(out, in_, func, reads, writes, bias=None, scale=None, accum=None):
        kw = {}
        if bias is not None:
            kw["bias"] = bias
        if scale is not None:
            kw["scale"] = scale
        if accum is not None:
            kw["accum_out"] = accum
        P.op("act", lambda e: e.activation(out=out, in_=in_, func=func, **kw), reads, writes)

    def tt(eng, out, in0, in1, op, reads, writes):
        P.op(eng, lambda e: e.tensor_tensor(out=out, in0=in0, in1=in1, op=op), reads, writes)

    def ts(eng, out, in0, s1, s2, op0, op1, reads, writes):
        if s2 is None:
            P.op(eng, lambda e: e.tensor_scalar(out=out, in0=in0, scalar1=s1, scalar2=None, op0=op0), reads, writes)
        else:
            P.op(eng, lambda e: e.tensor_scalar(out=out, in0=in0, scalar1=s1, scalar2=s2, op0=op0, op1=op1), reads, writes)

    def stt(eng, out, in0, scalar, in1, op0, op1, reads, writes):
        P.op(eng, lambda e: e.scalar_tensor_tensor(out=out, in0=in0, scalar=scalar, in1=in1, op0=op0, op1=op1), reads, writes)

    def cp(eng, out, in_, reads, writes):
        if eng == "act":
            act(out, in_, AF.Copy, reads, writes)
        else:
            P.op(eng, lambda e: e.tensor_copy(out=out, in_=in_), reads, writes)

    def memset(eng, ap, val, writes):
        P.op(eng, lambda e: e.memset(ap, val), (), writes)

    def blkname(b):
        return "hT_b%d" % b

    def wcols(l, c0, n):
        return w_in_d[l].rearrange("(kc p) n -> p kc n", p=128)[:, :, c0:c0 + n]

    def load_w(ap_sb, dram_ap, name):
        P.dma("pool", ap_sb, dram_ap, writes=[name])

    P.dma("sp", ident[:, :], ident_d, writes=["ident"])
    P.dma("sp", masks[:, :, :], masks_d, writes=["masks"])
    memset("pool", onesm[:, :], 1.0 / 256.0, ["onesm"])

    def load_gb(idx):
        src = bass.AP(gvec_h, idx * D, [[0, 128], [1, D]])
        P.dma("sp", gb[:, :], src, writes=["gb"])

    stat = nc.alloc_sbuf_tensor("stat4", [128, 16], F32)

    def norm_stage1(xn, xn_name, tile, rots, last):
        sl_ = tile % 4
        st = stat[:, sl_ * 4:sl_ * 4 + 4]
        sn = ["stat%d_%d" % (sl_, k) for k in range(4)]
        junk, jn = rots["junk"].next()
        memset("pool", st[:, 0:1], 0.0, [sn[0]])
        act(junk, xn, AF.Square, [xn_name], [jn, sn[0]], accum=st[:, 0:1])
        ts("dve", st[:, 1:2], st[:, 0:1], 1.0 / D, NORM_EPS, ALU.mult, ALU.add, [sn[0]], [sn[1]])
        act(st[:, 2:3], st[:, 1:2], AF.Sqrt, [sn[1]], [sn[2]])
        P.op("dve", lambda e: e.reciprocal(out=st[:, 3:4], in_=st[:, 2:3]), [sn[2]], [sn[3]])
        if last:
            o, on = rots["ofin"].next()
            stt("dve", o, xn, st[:, 3:4], gb[:, :], ALU.mult, ALU.mult, [xn_name, sn[3], "gb"], [on])
            P.dma("pool", out_d[tile * 128:(tile + 1) * 128, :], o, reads=[on])
            return None
        h, hn = rots["h"].next()
        stt("dve", h, xn, st[:, 3:4], gb[:, :], ALU.mult, ALU.mult, [xn_name, sn[3], "gb"], [hn])
        return (h, hn)

    def norm_stage2(hh, tile):
        h, hn = hh
        bk, bn = pb()
        bk16 = bk.bitcast(BF16)
        for c in range(8):
            P.op("pe", lambda e, c=c: e.transpose(bk16[:, c * 128:(c + 1) * 128], h[:, c * 128:(c + 1) * 128], ident[:, :]),
                 [hn, "ident"], [bn])
        src = bk16[:, :].rearrange("p (c t) -> p c t", c=8)
        cp("act", hT[:, :, tile * 128:(tile + 1) * 128], src, [bn], [blkname(tile // 4)])

    NLOOK = 2

    P.barrier()
    arena.reset()
    rots = {"x": Rot("x", 4, [D], F32), "junk": Rot("junk", 2, [D], BF16), "h": Rot("h", 4, [D], BF16)}
    load_gb(0)
    pend = {}
    for idx in range(NT + NLOOK):
        if idx < NT:
            xt, xnm = rots["x"].next()
            P.dma("sp", xt, x_d[idx * 128:(idx + 1) * 128, :], writes=[xnm])
            pend[idx] = norm_stage1(xt, xnm, idx, rots, False)
        if idx - NLOOK >= 0:
            norm_stage2(pend.pop(idx - NLOOK), idx - NLOOK)
    if dbg:
        P.dma("sp", hdbg_d, hT[:, :, :], reads=[blkname(b) for b in range(NBLK)])

    if stop == 'A0':
        P.emit()
        return P
    for l in range(n_layers):
        lastl = (l == n_layers - 1)
        P.barrier()
        P.dma("sp", vecs[:, :], vecs_d[l], writes=["vecs"])

        def gate_tile(Wg, Wgn, cc, blk, gr):
            bk, bn = pb()
            for kc in range(8):
                mm(bk[:, :], Wg[:, kc, cc * 128:(cc + 1) * 128], hT[:, kc, blk * 512:(blk + 1) * 512],
                   kc == 0, kc == 7, [Wgn, blkname(blk)], [bn])
            g, gn = gr.next()
            act(g, bk[:, :], AF.Silu, [bn], [gn])
            return g, gn

        P.barrier()
        arena.reset()
        qH = [arena.alloc([S], BF16), arena.alloc([S], BF16)]
        kT = arena.alloc([S + 2 * KPAD], BF16)
        vb = arena.alloc([48, 256], BF16)
        acc = arena.alloc([2, S], F32)
        Wqk_r = Rot("Wqk", 1, [8, 4, 128], BF16)
        Wv_r = Rot("Wv", 2, [8, 128], BF16)
        WgC = arena.alloc([8, 256], BF16)
        rope_r = Rot("rope", 2, [2, 512], F32)
        rt_r = Rot("rt", 4, [512], F32)
        qs_r = Rot("qs", 2, [512], F32)
        pT_r = Rot("pT", 5, [512], BF16)
        rd_r = Rot("rd", 2, [512], F32)
        g_r = Rot("g", 2, [512], F32)
        st_r = Rot("st", 2, [512], BF16)
        load_w(WgC, wcols(l, 3328 + 2 * 256, 256), "WgC")
        memset("pool", kT[:, 0:KPAD], 0.0, ["kTpadL"])
        memset("pool", kT[:, KPAD + S:KPAD + S + KPAD], 0.0, ["kTpadR"])
        memset("pool", qH[0][64:128, :], 0.0, ["qz0"])
        memset("pool", qH[1][0:64, :], 0.0, ["qz1"])
        memset("pool", vb[:, :, :], 0.0, ["vb"])
        memset("pool", vb[:, :, 64:192], 1.0, ["vb"])
        for hp in range(2):
            for g in range(3):
                dil = DIL[g]
                Lg = S // dil
                ntl = Lg // 128
                nch = ntl + 1
                Wqk, Wqkn = Wqk_r.next()
                Wv, Wvn = Wv_r.next()
                qc0 = 768 + g * 256 + hp * 128
                kc0 = 1536 + g * 256 + hp * 128
                vc0 = 2304 + g * 256 + hp * 128
                load_w(Wqk[:, :, 0, :], wcols(l, qc0, 128), Wqkn + "_0")
                load_w(Wqk[:, :, 2, :], wcols(l, kc0, 128), Wqkn + "_2")
                load_w(Wv, wcols(l, vc0, 128), Wvn)
                for blk in range(NBLK):
                    rp, rpn = rope_r.next()
                    P.dma("sp", rp[:, 0, :], ropec_d[:, blk * 512:(blk + 1) * 512], writes=[rpn + "c"])
                    P.dma("sp", rp[:, 1, :], ropes_d[:, blk * 512:(blk + 1) * 512], writes=[rpn + "s"])
                    for qk in range(2):
                        bk, bn = pb()
                        for kc in range(8):
                            mm(bk[:, :], Wqk[:, kc, 2 * qk, :], hT[:, kc, blk * 512:(blk + 1) * 512],
                               kc == 0, kc == 7, [Wqkn + "_%d" % (2 * qk), blkname(blk)], [bn])
                        qs, qsn = qs_r.next()
                        for (src, dst) in ((32, 0), (0, 32), (96, 64), (64, 96)):
                            cp("act", qs[dst:dst + 32, :], bk[src:src + 32, :], [bn], [qsn])
                        t1, t1n = rt_r.next()
                        t2, t2n = rt_r.next()
                        tt("dve", t1, bk[:, :], rp[:, 0, :], ALU.mult, [bn, rpn + "c"], [t1n])
                        tt("dve", t2, qs, rp[:, 1, :], ALU.mult, [qsn, rpn + "s"], [t2n])
                        if qk == 0:
                            tt("pool", qH[0][0:64, blk * 512:(blk + 1) * 512], t1[0:64, :], t2[0:64, :], ALU.add, [t1n, t2n], ["qT_b%d" % blk])
                            tt("pool", qH[1][64:128, blk * 512:(blk + 1) * 512], t1[64:128, :], t2[64:128, :], ALU.add, [t1n, t2n], ["qT_b%d" % blk])
                        else:
                            tt("dve", kT[:, KPAD + blk * 512:KPAD + (blk + 1) * 512], t1, t2, ALU.add, [t1n, t2n], ["kT_b%d" % blk])
                if stop == 'C1':
                    P.emit()
                    return P
                allq = ["qT_b%d" % b for b in range(NBLK)]
                allk = ["kT_b%d" % b for b in range(NBLK)] + ["kTpadL", "kTpadR"]
                allh = [blkname(b) for b in range(NBLK)]
                for r in range(dil):
                    for ci in range(nch):
                        ch = r * nch + ci
                        p0 = ci * 128 - 64
                        lo = max(p0, 0)
                        hi = min(p0 + 128, Lg)
                        npos = hi - lo
                        prow = lo - p0
                        t0 = r + dil * lo
                        bk, bn = pb()
                        for kc in range(8):
                            lhsT = hT[:, kc, t0:t0 + dil * (npos - 1) + 1:dil]
                            if prow == 0:
                                mm(bk[0:npos, 0:128], lhsT, Wv[:, kc, :], kc == 0, kc == 7, [Wvn] + allh, [bn])
                            else:
                                mm(bk[64:128, 0:128], lhsT, Wv[:, kc, :], kc == 0, kc == 7, [Wvn] + allh, [bn], tp=(0, 64))
                        dst = bass.AP(vb.tensor, vb.offset + prow * vb.ap[0][0] + ch * 256, [[vb.ap[0][0], npos], [192, 2], [1, 64]])
                        srcv = bk[prow:prow + npos, 0:128].rearrange("p (h d) -> p h d", h=2)
                        cp("act", dst, srcv, [bn], ["vb"])
                if stop == 'C2':
                    P.emit()
                    return P
                tiles = [(r, ti) for r in range(dil) for ti in range(ntl)]
                LOOK = 3
                staged = {}

                def stage_a(idx):
                    r, ti = tiles[idx]
                    variant = 1 if ti == 0 else (2 if ti == ntl - 1 else 0)
                    sbk, sbn = pb()
                    qa = r + dil * 128 * ti
                    for hh in range(2):
                        qap = qH[hh][:, qa:qa + dil * 127 + 1:dil]
                        for kc in range(2):
                            ka = KPAD + r + dil * (128 * ti - 64 + 128 * kc)
                            kap = kT[:, ka:ka + dil * 127 + 1:dil]
                            bi = 2 * hh + kc
                            mm(sbk[:, bi * 128:(bi + 1) * 128], kap, qap, True, True, allq + allk + ["qz0", "qz1"], [sbn])
                    pT, pTn = pT_r.next()
                    act(pT, sbk[:, :], AF.Exp, [sbn], [pTn], scale=0.125)
                    tt("dve", pT, pT, masks[:, variant, :], ALU.mult, [pTn, "masks"], [pTn])
                    staged[idx] = (pT, pTn)

                def stage_b(idx):
                    r, ti = tiles[idx]
                    pT, pTn = staged.pop(idx)
                    obk, obn = pb()
                    for hh in range(2):
                        for kc in range(2):
                            ch = r * nch + ti + kc
                            bi = 2 * hh + kc
                            mm(obk[:, hh * 128:(hh + 1) * 128], vb[:, ch, hh * 128:(hh + 1) * 128],
                               pT[:, bi * 128:(bi + 1) * 128], kc == 0, kc == 1, ["vb", pTn], [obn])
                    qa = r + dil * 128 * ti
                    dst = acc[:, :, qa:qa + dil * 127 + 1:dil]
                    srco = obk[:, 0:256].rearrange("p (h t) -> p h t", h=2)
                    bset = sorted(set([(qa) // 512, (qa + dil * 127) // 512]))
                    accn = ["acc_b%d" % b for b in range(bset[0], bset[-1] + 1)]
                    if g == 0:
                        cp("dve", dst, srco, [obn], accn)
                    else:
                        tt("dve", dst, srco, dst, ALU.add, [obn] + accn, accn)

                for idx in range(len(tiles) + LOOK):
                    if idx < len(tiles):
                        stage_a(idx)
                    if idx - LOOK >= 0:
                        stage_b(idx - LOOK)
            for blk in range(NBLK):
                sl = slice(blk * 512, (blk + 1) * 512)
                an = "acc_b%d" % blk
                rd, rdn = rd_r.next()
                act(rd[0:64, :], acc[64:128, 0, sl], AF.Ln, [an], [rdn])
                act(rd[64:128, :], acc[0:64, 1, sl], AF.Ln, [an], [rdn])
                act(rd, rd, AF.Exp, [rdn], [rdn], scale=-1.0)
                tt("pool", acc[0:64, 0, sl], acc[0:64, 0, sl], rd[0:64, :], ALU.mult, [an, rdn], [an])
                tt("pool", acc[64:128, 0, sl], acc[64:128, 1, sl], rd[64:128, :], ALU.mult, [an, rdn], [an])
            for blk in range(NBLK):
                sl = slice(blk * 512, (blk + 1) * 512)
                gt, gtn = gate_tile(WgC, "WgC", hp, blk, g_r)
                stg, stn = st_r.next()
                tt("dve", stg, acc[:, 0, sl], gt, ALU.mult, ["acc_b%d" % blk, gtn], [stn])
                P.dma("pool", yg_d[512 + hp * 128:512 + hp * 128 + 128, sl], stg, reads=[stn], writes=["yg_%d_%d" % (4 + hp, blk)])

        if stop == 'C':
            P.emit()
            return P
        P.barrier()
        arena.reset()
        Wa = arena.alloc([8, 256], BF16)
        WgA = arena.alloc([8, 256], BF16)
        cs = arena.alloc([2, 512], BF16)
        wf = arena.alloc([2, 256], BF16)
        uaT = arena.alloc([2, S], BF16)
        PQ = arena.alloc([32, 512], BF16)
        fT = arena.alloc([2, S], BF16)
        dft_r = Rot("dft", 3, [2, 2, 512], BF16)
        sq_r = Rot("sq", 2, [512], F32)
        dmid = arena.alloc([32], BF16)
        P.dma("sp", dmid, dmid_d, writes=["dmid"])
        g_r = Rot("g", 2, [512], F32)
        st_r = Rot("st", 2, [512], BF16)
        load_w(Wa, wcols(l, 0, 256), "Wa")
        load_w(WgA, wcols(l, 3328, 256), "WgA")
        load_w(wf, w_fourier_d[l].rearrange("(c p) n -> p c n", p=128), "wf")
        P.dma("sp", cs, cs_d, writes=["cs"])
        for blk in range(NBLK):
            for c in range(2):
                bk, bn = pb()
                for kc in range(8):
                    mm(bk[:, :], Wa[:, kc, c * 128:(c + 1) * 128], hT[:, kc, blk * 512:(blk + 1) * 512],
                       kc == 0, kc == 7, ["Wa", blkname(blk)], [bn])
                cp("act" if c == 0 else "dve", uaT[:, c, blk * 512:(blk + 1) * 512], bk[:, :], [bn], ["uaT_b%d" % blk])
        for a in range(NT):
            bk, bn = pb()
            for c in range(2):
                mm(bk[:, :], uaT[:, c, a * 128:(a + 1) * 128], cs[:, c, :], c == 0, c == 1, ["uaT_b%d" % (a // 4), "cs"], [bn])
            cp("act" if a % 2 == 0 else "dve", PQ[:, a, :], bk[:, :], [bn], ["PQ_%d" % a])
        for ps_ in range(4):
            bC = [pb() for c in range(2)]
            bS = [pb() for c in range(2)]
            for a2 in range(NT // 2):
                dt_, dtn = dft_r.next()
                P.dma("sp", dt_, dft_d[ps_, a2], writes=[dtn])
                for ai in range(2):
                    a = 2 * a2 + ai
                    for c in range(2):
                        mm(bC[c][0][:, :], PQ[:, a, c * 128:(c + 1) * 128], dt_[:, ai, 0, :],
                           a == 0, a == NT - 1, ["PQ_%d" % a, dtn], [bC[c][1]])
                        mm(bS[c][0][:, :], PQ[:, a, 256 + c * 128:256 + (c + 1) * 128], dt_[:, ai, 1, :],
                           a == 0, a == NT - 1, ["PQ_%d" % a, dtn], [bS[c][1]])
            for c in range(2):
                sq, sqn = sq_r.next()
                cp("act", sq, bS[c][0][:, :], [bS[c][1]], [sqn])
                k0 = ps_ * 512
                tt("dve", fT[:, c, k0:k0 + 512], bC[c][0][:, :], sq, ALU.subtract, [bC[c][1], sqn], ["fT_b%d" % ps_])
                j0 = 1 if ps_ == 0 else 0
                fv = fT[:, c, :]
                rev = bass.AP(fv.tensor, fv.offset + (S - k0 - j0), [[fv.ap[0][0], 128], [-1, 512 - j0]])
                hin = ["fT_b%d" % (7 - ps_)] + (["fT_b%d" % (8 - ps_)] if ps_ >= 1 else [])
                tt("dve", rev, bC[c][0][:, j0:512], sq[:, j0:512], ALU.add, [bC[c][1], sqn], hin)
        bm_, bmn_ = pb()
        for c in range(2):
            for a in range(NT):
                mm(bm_[:, c:c + 1], PQ[:, a, c * 128:(c + 1) * 128], dmid[:, a:a + 1], a == 0, a == NT - 1, ["PQ_%d" % a, "dmid"], [bmn_])
        for c in range(2):
            cp("act", fT[:, c, S // 2:S // 2 + 1], bm_[:, c:c + 1], [bmn_], ["fT_b4"])
        for blk in range(NBLK):
            sl = slice(blk * 512, (blk + 1) * 512)
            for co in range(2):
                bk, bn = pb()
                for c in range(2):
                    mm(bk[:, :], wf[:, c, co * 128:(co + 1) * 128], fT[:, c, sl], c == 0, c == 1, ["wf", "fT_b%d" % blk], [bn])
                gt, gtn = gate_tile(WgA, "WgA", co, blk, g_r)
                stg, stn = st_r.next()
                tt("dve", stg, bk[:, :], gt, ALU.mult, [bn, gtn], [stn])
                P.dma("pool", yg_d[co * 128:(co + 1) * 128, sl], stg, reads=[stn], writes=["yg_%d_%d" % (co, blk)])

        if stop == 'A':
            P.emit()
            return P
        P.barrier()
        arena.reset()
        Wb = arena.alloc([8, 512], BF16)
        WgB = arena.alloc([8, 256], BF16)
        wpw = arena.alloc([2, 256], BF16)
        dg = arena.alloc([2, 31, 128], BF16)
        uT = arena.alloc([2, S + 32], BF16)
        sg_r = Rot("sg", 2, [512], F32)
        y_r = Rot("y", 3, [2, 512], F32)
        ysq_r = Rot("ysq", 3, [2, 512], F32)
        tmp_r = Rot("ctmp", 8, [512], F32)
        s_r = Rot("s", 3, [2, 512], BF16)
        g_r = Rot("g", 2, [512], F32)
        st_r = Rot("st", 2, [512], BF16)
        load_w(Wb, wcols(l, 256, 512), "Wb")
        load_w(WgB, wcols(l, 3328 + 256, 256), "WgB")
        load_w(wpw, w_pw_d[l].rearrange("(c p) n -> p c n", p=128), "wpw")
        for c in range(2):
            for k in range(31):
                ts("dve", dg[:, c, k, :], ident[:, :], vecs[:, V_CW + c * 31 + k:V_CW + c * 31 + k + 1], None, ALU.mult, None,
                   ["ident", "vecs"], ["dg_%d_%d" % (c, k)])
        memset("pool", uT[:, :, 0:16], 0.0, ["uTpadL"])
        memset("pool", uT[:, :, 16 + S:32 + S], 0.0, ["uTpadR"])
        for blk in range(NBLK):
            for c in range(2):
                bka, bna = pb()
                bkg, bng = pb()
                for kc in range(8):
                    mm(bka[:, :], Wb[:, kc, c * 128:(c + 1) * 128], hT[:, kc, blk * 512:(blk + 1) * 512],
                       kc == 0, kc == 7, ["Wb", blkname(blk)], [bna])
                for kc in range(8):
                    mm(bkg[:, :], Wb[:, kc, 256 + c * 128:256 + (c + 1) * 128], hT[:, kc, blk * 512:(blk + 1) * 512],
                       kc == 0, kc == 7, ["Wb", blkname(blk)], [bng])
                sg, sgn = sg_r.next()
                act(sg, bkg[:, :], AF.Sigmoid, [bng], [sgn])
                tt("dve", uT[:, c, 16 + blk * 512:16 + (blk + 1) * 512], bka[:, :], sg, ALU.mult, [bna, sgn], ["uT_b%d" % blk])
        cst = {}

        def b_s1(blk):
            urd = ["uT_b%d" % b for b in range(max(blk - 1, 0), min(blk + 1, NBLK - 1) + 1)] + ["uTpadL", "uTpadR"]
            y, yn = y_r.next()
            ysq, ysqn = ysq_r.next()
            for c in range(2):
                bk, bn = pb()
                for k in range(31):
                    o = 16 + blk * 512 + k - 15
                    mm(bk[:, :], dg[:, c, k, :], uT[:, c, o:o + 512], k == 0, k == 30, ["dg_%d_%d" % (c, k)] + urd, [bn])
                act(y[:, c, :], bk[:, :], AF.Identity, [bn, "vecs"], [yn], bias=vecs[:, V_CB + c:V_CB + c + 1])
                act(ysq[:, c, :], bk[:, :], AF.Square, [bn, "vecs"], [ysqn], bias=vecs[:, V_CB + c:V_CB + c + 1])
            cst[blk] = [y, yn, ysq, ysqn]

        def b_s2(blk):
            y, yn, ysq, ysqn = cst[blk]
            bm, bmn = pb()
            bs, bsn = pb()
            for c in range(2):
                mm(bm[:, :], onesm[:, :], y[:, c, :], c == 0, c == 1, ["onesm", yn], [bmn])
            for c in range(2):
                mm(bs[:, :], onesm[:, :], ysq[:, c, :], c == 0, c == 1, ["onesm", ysqn], [bsn])
            msq, msqn = tmp_r.next()
            act(msq, bm[:, :], AF.Square, [bmn], [msqn])
            var, varn = tmp_r.next()
            tt("dve", var, bs[:, :], msq, ALU.subtract, [bsn, msqn], [varn])
            ts("dve", var, var, LN_EPS, None, ALU.add, None, [varn], [varn])
            act(var, var, AF.Sqrt, [varn], [varn])
            P.op("dve", lambda e, var=var: e.reciprocal(out=var, in_=var), [varn], [varn])
            s_, sn = s_r.next()
            for c in range(2):
                d_, dn = tmp_r.next()
                tt("dve", d_, y[:, c, :], bm[:, :], ALU.subtract, [yn, bmn], [dn])
                tt("pool", d_, d_, var, ALU.mult, [dn, varn], [dn])
                act(s_[:, c, :], d_, AF.Silu, [dn, "vecs"], [sn], bias=vecs[:, V_LB + c:V_LB + c + 1], scale=vecs[:, V_LG + c:V_LG + c + 1])
            cst[blk] = [s_, sn]

        def b_s3(blk):
            sl = slice(blk * 512, (blk + 1) * 512)
            s_, sn = cst.pop(blk)
            for co in range(2):
                bk, bn = pb()
                for c in range(2):
                    mm(bk[:, :], wpw[:, c, co * 128:(co + 1) * 128], s_[:, c, :], c == 0, c == 1, ["wpw", sn], [bn])
                gt, gtn = gate_tile(WgB, "WgB", co, blk, g_r)
                stg, stn = st_r.next()
                tt("dve", stg, bk[:, :], gt, ALU.mult, [bn, gtn], [stn])
                P.dma("pool", yg_d[256 + co * 128:256 + (co + 1) * 128, sl], stg, reads=[stn], writes=["yg_%d_%d" % (2 + co, blk)])

        for i in range(NBLK + 2):
            if i < NBLK:
                b_s1(i)
            if 0 <= i - 1 < NBLK:
                b_s2(i - 1)
            if i - 2 >= 0:
                b_s3(i - 2)

        if stop == 'B':
            P.emit()
            return P
        P.barrier()
        arena.reset()
        Wd = arena.alloc([8, 256], BF16)
        WgD = arena.alloc([8, 256], BF16)
        wpb = arena.alloc([2, 128], BF16)
        ud = arena.alloc([2, S + 16], F32)
        bA = arena.alloc([S + 16], F32)
        bB = arena.alloc([S + 16], F32)
        pooled = arena.alloc([2, S], BF16)
        edge = arena.alloc([2, 16], F32)
        etmp = arena.alloc([16], F32)
        po_r = Rot("po", 2, [512], F32)
        g_r = Rot("g", 2, [512], F32)
        st_r = Rot("st", 2, [512], BF16)
        load_w(Wd, wcols(l, 3072, 256), "Wd")
        load_w(WgD, wcols(l, 3328 + 768, 256), "WgD")
        memset("pool", wpb[:, :, :], 0.0, ["wpb"])
        for gi in range(4):
            r0 = (gi % 2) * 64
            P.dma("pool", wpb[r0:r0 + 64, gi // 2, r0:r0 + 64], w_pool_d[l, gi], reads=["wpb"], writes=["wpb_%d" % gi])
        P.dma("sp", edge, edge_d, writes=["edge"])
        memset("pool", ud[:, :, 0:8], 0.0, ["udpadL"])
        memset("pool", ud[:, :, 8 + S:16 + S], 0.0, ["udpadR"])
        for c in range(2):
            for blk in range(NBLK):
                bk, bn = pb()
                for kc in range(8):
                    mm(bk[:, :], Wd[:, kc, c * 128:(c + 1) * 128], hT[:, kc, blk * 512:(blk + 1) * 512],
                       kc == 0, kc == 7, ["Wd", blkname(blk)], [bn])
                cp("act", ud[:, c, 8 + blk * 512:8 + (blk + 1) * 512], bk[:, :], [bn], ["ud_%d" % c])
        NP_ = S + 16
        for c in range(2):
            u = ud[:, c, :]
            tt("dve", bA[:, 1:NP_], u[:, 0:NP_ - 1], u[:, 1:NP_], ALU.add, ["ud_%d" % c, "udpadL", "udpadR"], ["bA"])
            tt("dve", bB[:, 2:NP_ - 1], bA[:, 1:NP_ - 2], bA[:, 3:NP_], ALU.add, ["bA"], ["bB"])
            if c == 1:
                tt("dve", bA[:, 4:NP_ - 4], bB[:, 2:NP_ - 6], bB[:, 6:NP_ - 2], ALU.add, ["bB"], ["bA"])
                tt("dve", bB[:, 8:NP_ - 8], bA[:, 4:NP_ - 12], bA[:, 12:NP_ - 4], ALU.add, ["bA"], ["bB"])
            for half in range(2):
                rows = slice(half * 64, half * 64 + 64)
                W_ = bA if half == 0 else bB
                stt("dve", pooled[rows, c, :], W_[rows, 8:8 + S], vecs[rows, V_IS + c:V_IS + c + 1], u[rows, 8:8 + S],
                    ALU.mult, ALU.subtract, ["bA", "bB", "ud_%d" % c, "vecs"], ["pooled_%d" % c])
                for e0, t0 in ((0, 0), (8, S - 8)):
                    tt("dve", etmp[rows, e0:e0 + 8], W_[rows, 8 + t0:16 + t0], edge[rows, c, e0:e0 + 8], ALU.mult,
                       ["bA", "bB", "edge"], ["etmp"])
                    tt("dve", pooled[rows, c, t0:t0 + 8], etmp[rows, e0:e0 + 8], u[rows, 8 + t0:16 + t0], ALU.subtract,
                       ["etmp", "ud_%d" % c], ["pooled_%d" % c])
        for co in range(2):
            for blk in range(NBLK):
                sl = slice(blk * 512, (blk + 1) * 512)
                bk, bn = pb()
                mm(bk[:, :], wpb[:, co, :], pooled[:, co, sl], True, True, ["wpb", "wpb_0", "wpb_1", "wpb_2", "wpb_3", "pooled_%d" % co], [bn])
                gt, gtn = gate_tile(WgD, "WgD", co, blk, g_r)
                stg, stn = st_r.next()
                po, pon = po_r.next()
                act(po, bk[:, :], AF.Copy, [bn, "vecs"], [pon], scale=vecs[:, V_PS + co:V_PS + co + 1])
                tt("pool", stg, po, gt, ALU.mult, [pon, gtn], [stn])
                P.dma("pool", yg_d[768 + co * 128:768 + (co + 1) * 128, sl], stg, reads=[stn], writes=["yg_%d_%d" % (6 + co, blk)])

        if stop == 'D':
            P.emit()
            return P
        P.barrier()
        arena.reset()
        ygT = arena.alloc([8, S], BF16)
        Wgt_r = Rot("Wgt", 3, [4, 8, 128], BF16)
        Wbr_r = Rot("Wbr", 3, [4, 2, 128], BF16)
        mst_r = Rot("mst", 2, [S], BF16)
        mg_r = Rot("mg", 2, [512], F32)
        mt_r = Rot("mtmp", 3, [512], F32)
        mac_r = Rot("macc", 2, [512], F32)
        for c8 in range(8):
            P.dma("sp", ygT[:, c8, :], yg_d[c8 * 128:(c8 + 1) * 128, :],
                  reads=["yg_%d_%d" % (c8, b) for b in range(NBLK)], writes=["ygT_%d" % c8])
        wslots = {}

        def m_load(j):
            Wgt, Wgtn = Wgt_r.next()
            Wbr, Wbrn = Wbr_r.next()
            for n in range(4):
                load_w(Wgt[:, n, :, :], w_gate_d[l, n].rearrange("(kc p) m -> p kc m", p=128)[:, :, j * 128:(j + 1) * 128], Wgtn + "_%d" % n)
                load_w(Wbr[:, n, :, :], w_branch_d[l, n].rearrange("(c p) m -> p c m", p=128)[:, :, j * 128:(j + 1) * 128], Wbrn + "_%d" % n)
            wslots[j] = (Wgt, Wgtn, Wbr, Wbrn)

        m_load(0)
        for j in range(8):
            if j + 1 < 8:
                m_load(j + 1)
            Wgt, Wgtn, Wbr, Wbrn = wslots.pop(j)
            mst, mstn = mst_r.next()
            for blk in range(NBLK):
                sl = slice(blk * 512, (blk + 1) * 512)
                macc, maccn = mac_r.next()
                for n in range(4):
                    bg, bgn = pb()
                    by, byn = pb()
                    for kc in range(8):
                        mm(bg[:, :], Wgt[:, n, kc, :], hT[:, kc, sl], kc == 0, kc == 7, [Wgtn + "_%d" % n, blkname(blk)], [bgn])
                    for c in range(2):
                        mm(by[:, :], Wbr[:, n, c, :], ygT[:, 2 * n + c, sl], c == 0, c == 1, [Wbrn + "_%d" % n, "ygT_%d" % (2 * n + c)], [byn])
                    mg, mgn = mg_r.next()
                    act(mg, bg[:, :], AF.Sigmoid, [bgn, "vecs"], [mgn], bias=vecs[:, V_BG + n * 8 + j:V_BG + n * 8 + j + 1])
                    if n == 0:
                        tt("dve", macc, by[:, :], mg, ALU.mult, [byn, mgn], [maccn])
                    else:
                        tmp, tmpn = mt_r.next()
                        tt("dve", tmp, by[:, :], mg, ALU.mult, [byn, mgn], [tmpn])
                        if n < 3:
                            tt("pool", macc, macc, tmp, ALU.add, [maccn, tmpn], [maccn])
                        else:
                            tt("pool", mst[:, sl], macc, tmp, ALU.add, [maccn, tmpn], [mstn])
            P.dma("pool", mT_d[j * 128:(j + 1) * 128, :], mst, reads=[mstn], writes=["mT_%d" % j])

        if stop == 'M':
            P.emit()
            return P
        P.barrier()
        arena.reset()
        wo = arena.alloc([8, D], BF16)
        mTb_r = Rot("mTb", 2, [8, 512], BF16)
        rots = {"x": Rot("x", 3, [D], F32), "xn": Rot("xn", 4, [D], F32), "junk": Rot("junk", 2, [D], BF16),
                "h": Rot("h", 4, [D], BF16), "ofin": Rot("ofin", 2, [D], F32)}
        load_w(wo, w_out_d[l].rearrange("(jc p) m -> p jc m", p=128), "wo")
        load_gb(l + 1)
        xsrc = x_d if l == 0 else xmid_d
        pend = {}
        cur = {}

        def e_stage1(tile):
            blk, t4 = tile // 4, tile % 4
            if t4 == 0:
                mTb, mTbn = mTb_r.next()
                P.dma("sp", mTb, mT_d.rearrange("(jc p) t -> p jc t", p=128)[:, :, blk * 512:(blk + 1) * 512],
                      reads=["mT_%d" % j for j in range(8)], writes=[mTbn])
                cur["m"] = (mTb, mTbn)
            mTb, mTbn = cur["m"]
            xt, xnm = rots["x"].next()
            rd_ = ["xmid_%d" % tile] if l > 0 else []
            P.dma("sp", xt, xsrc[tile * 128:(tile + 1) * 128, :], reads=rd_, writes=[xnm])
            hb = []
            for half in range(2):
                bk, bn = pb()
                for jc in range(8):
                    mm(bk[:, :], mTb[:, jc, t4 * 128:(t4 + 1) * 128], wo[:, jc, half * 512:(half + 1) * 512],
                       jc == 0, jc == 7, [mTbn, "wo"], [bn])
                hb.append((bk, bn))
            xn, xnn = rots["xn"].next()
            for half in range(2):
                tt("dve", xn[:, half * 512:(half + 1) * 512], xt[:, half * 512:(half + 1) * 512], hb[half][0][:, :], ALU.add,
                   [xnm, hb[half][1]], [xnn])
            if (not lastl) or dbg:
                P.dma("pool", xmid_d[tile * 128:(tile + 1) * 128, :], xn, reads=[xnn], writes=["xmid_%d" % tile])
            return norm_stage1(xn, xnn, tile, rots, lastl)

        for idx in range(NT + NLOOK):
            if idx < NT:
                pend[idx] = e_stage1(idx)
            if idx - NLOOK >= 0:
                hh = pend.pop(idx - NLOOK)
                if hh is not None:
                    norm_stage2(hh, idx - NLOOK)
    P.emit()
    return P

import ml_dtypes as _mld

_BF = _mld.bfloat16
_CONST_CACHE = {}


def _constants():
    if _CONST_CACHE:
        return _CONST_CACHE
    c = {}
    c["ident"] = np.eye(128, dtype=np.float32).astype(_BF)
    p = np.arange(128)[:, None]
    n = np.arange(128)[None, :]
    M0 = (n <= p).astype(np.float32)
    M1 = (n >= p).astype(np.float32)
    M0f = M0 * (p >= 64)
    M1l = M1 * (p < 64)
    mk = np.zeros((128, 3, 512), np.float32)
    for v, (a, b) in enumerate(((M0, M1), (M0f, M1), (M0, M1l))):
        mk[:, v, :] = np.concatenate([a, b, a, b], axis=1)
    c["masks"] = mk.astype(_BF)
    inv = (1.0 / (np.float32(10000.0) ** (np.arange(0, 64, 2, dtype=np.float32) / np.float32(64)))).astype(np.float32)
    ang = (np.arange(S, dtype=np.float32)[None, :] * inv[:, None]).astype(np.float32)
    cosv = np.cos(ang).astype(np.float32)
    sinv = np.sin(ang).astype(np.float32)
    rc = np.zeros((128, S), np.float32)
    rs = np.zeros((128, S), np.float32)
    for q in range(4):
        rc[q * 32:(q + 1) * 32] = cosv
        rs[q * 32:(q + 1) * 32] = sinv if (q % 2 == 1) else -sinv
    c["ropec"] = rc
    c["ropes"] = rs
    s_idx = np.arange(S, dtype=np.int64)[:, None]
    k_idx = np.arange(S // 2, dtype=np.int64)[None, :]
    ph = ((s_idx * k_idx) % S).astype(np.float64) * (2.0 * np.pi / S)
    Cm = (np.cos(ph) / 512.0).astype(np.float32).reshape(16, 2, 128, 4, 512)
    Sm = (np.sin(ph) / 512.0).astype(np.float32).reshape(16, 2, 128, 4, 512)
    del ph
    dft = np.empty((4, 16, 128, 2, 2, 512), dtype=_BF)
    dft[:, :, :, :, 0, :] = Cm.transpose(3, 0, 2, 1, 4).astype(_BF)
    dft[:, :, :, :, 1, :] = Sm.transpose(3, 0, 2, 1, 4).astype(_BF)
    c["dft"] = dft
    sgn = np.where((np.arange(S) % 2) == 0, 1.0, -1.0).astype(np.float32) / 512.0
    c["dmid"] = np.ascontiguousarray(sgn.reshape(32, 128).T).astype(_BF)
    cm = np.arange(64)
    phc = ((cm[:, None] * cm[None, :]) % 64).astype(np.float64) * (2.0 * np.pi / 64)
    cs = np.zeros((256, 512), np.float32)
    for gi in range(4):
        cs[gi * 64:(gi + 1) * 64, gi * 64:(gi + 1) * 64] = np.cos(phc)
        cs[gi * 64:(gi + 1) * 64, 256 + gi * 64:256 + (gi + 1) * 64] = np.sin(phc)
    c["cs"] = np.ascontiguousarray(cs.reshape(2, 128, 512).transpose(1, 0, 2)).astype(_BF)
    sizes = (2, 4, 8, 16)
    inv_s = np.zeros((128, 2), np.float32)
    edge = np.zeros((128, 2, 16), np.float32)
    for gi, sz in enumerate(sizes):
        rows = slice((gi % 2) * 64, (gi % 2) * 64 + 64)
        cch = gi // 2
        inv_s[rows, cch] = 1.0 / sz
        for e in range(16):
            t = e if e < 8 else S - 16 + e
            lo = max(t - sz // 2, 0)
            hi = min(t + sz - 1 - sz // 2, S - 1)
            edge[rows, cch, e] = 1.0 / float(hi - lo + 1)
    c["inv_s"] = inv_s
    c["edge"] = edge
    perm = np.zeros(1536, np.int64)
    for j in range(1536):
        base = (j // 64) * 64
        d = j % 64
        perm[j] = base + (d + 32 if d < 32 else d - 32)
    c["perm"] = perm
    _CONST_CACHE.update(c)
    return c


def _prep_inputs(inp):
    c = _constants()
    f = lambda a: np.ascontiguousarray(np.asarray(a, dtype=np.float32))
    w_in = f(inp["w_in"])
    L = w_in.shape[0]
    vecs = np.zeros((L, 128, NV), np.float32)

    def pc(v):
        return f(v).reshape(L, 2, 128).transpose(0, 2, 1)

    vecs[:, :, V_CB:V_CB + 2] = pc(inp["conv_b"])
    vecs[:, :, V_LG:V_LG + 2] = pc(inp["conv_ln_g"])
    vecs[:, :, V_LB:V_LB + 2] = pc(inp["conv_ln_b"])
    vecs[:, :, V_PS:V_PS + 2] = pc(inp["pool_scale"])
    bg = f(inp["b_gate"]).reshape(L, 4, 8, 128).transpose(0, 3, 1, 2).reshape(L, 128, 32)
    vecs[:, :, V_BG:V_BG + 32] = bg
    vecs[:, :, V_IS:V_IS + 2] = c["inv_s"][None]
    cw = f(inp["conv_w"]).reshape(L, 31, 2, 128).transpose(0, 3, 2, 1).reshape(L, 128, 62)
    vecs[:, :, V_CW:V_CW + 62] = cw
    gvec = np.concatenate([f(inp["norm_g"]), f(inp["final_g"])[None, :]], axis=0)
    shared = {
        "w_in": w_in, "w_fourier": f(inp["w_fourier"]), "w_pw": f(inp["w_pw"]),
        "w_pool": f(inp["w_pool"]), "w_branch": f(inp["w_branch"]), "w_gate": f(inp["w_gate"]),
        "w_out": f(inp["w_out"]), "gvec": np.ascontiguousarray(gvec), "vecs": vecs,
        "ident": c["ident"], "masks": c["masks"], "ropec": c["ropec"], "ropes": c["ropes"],
        "dft": c["dft"], "dmid": c["dmid"], "cs": c["cs"], "edge": c["edge"],
    }
    return shared


_NC_CACHE = {}


def _get_nc(n_layers=2, dbg=False, stop=None):
    key = (n_layers, dbg, stop)
    if key not in _NC_CACHE:
        nc = bass.Bass("TRN2", target_bir_lowering=False)
        build_program(nc, n_layers=n_layers, dbg=dbg, stop=stop)
        _NC_CACHE[key] = nc
    return _NC_CACHE[key]


def kernel(**inputs):
    from concourse.bass_utils import run_bass_kernel_spmd
    x = np.ascontiguousarray(np.asarray(inputs["x"], dtype=np.float32))
    B = x.shape[0]
    shared = _prep_inputs(inputs)
    nc = _get_nc(2, False)
    in_maps = []
    for b in range(B):
        m = dict(shared)
        m["x"] = x[b]
        in_maps.append(m)
    res = run_bass_kernel_spmd(nc, in_maps, core_ids=list(range(B)))
    out = np.stack([np.asarray(r["out"], dtype=np.float32) for r in res.results], axis=0)
    return out
```

```python
import bisect
import numpy as np
import concourse.bass as bass
import concourse.mybir as mybir

F32 = mybir.dt.float32
BF16 = mybir.dt.bfloat16
ALU = mybir.AluOpType
AF = mybir.ActivationFunctionType
AX = mybir.AxisListType

ENGS = ("pe", "act", "dve", "pool", "sp")


class Buf:
    __slots__ = ("name", "last_w", "readers")

    def __init__(self, name):
        self.name = name
        self.last_w = []
        self.readers = {}


class Op:
    __slots__ = ("eng", "fn", "deps", "gidx", "eidx", "marked", "dma", "cum", "tag")

    def __init__(self, eng, fn, gidx):
        self.eng = eng
        self.fn = fn
        self.deps = []
        self.gidx = gidx
        self.eidx = -1
        self.marked = False
        self.dma = None
        self.tag = None
        self.cum = 0


class Prog:
    def __init__(self, nc, n_dma_sems_sp=24, n_dma_sems_pool=16):
        self.nc = nc
        self.ops = []
        self.eops = {e: [] for e in ENGS}
        self.ndma = {"sp": n_dma_sems_sp, "pool": n_dma_sems_pool, "act": 16}
        self.dma_rr = {"sp": 0, "pool": 0, "act": 0}
        self.dma_cnt = {}
        self.dma_last = {}
        self.bufs = {}
        self.epoch = []
        import os as _os
        self.tagging = bool(_os.environ.get('KTAG'))

    def buf(self, name):
        b = self.bufs.get(name)
        if b is None:
            b = Buf(name)
            b.last_w = list(self.epoch)
            self.bufs[name] = b
        return b

    def barrier(self):
        toks = []
        for e in ENGS:
            if self.eops[e]:
                o = self.eops[e][-1]
                if o.dma is None:
                    toks.append(("op", o))
                else:
                    for q in reversed(self.eops[e]):
                        if q.dma is None:
                            toks.append(("op", q))
                            break
        for key, t in self.dma_last.items():
            toks.append(t)
        self.epoch = toks
        for b in self.bufs.values():
            b.last_w = list(toks)
            b.readers = {}

    def _mkop(self, eng, fn, reads, writes):
        op = Op(eng, fn, len(self.ops))
        if self.tagging:
            import sys as _sys
            fr = _sys._getframe(2)
            while fr is not None and fr.f_code.co_name not in ("build_program", "norm_tile", "gate_tile"):
                fr = fr.f_back
            op.tag = "L%d" % fr.f_lineno if fr is not None else "?"
        op.eidx = len(self.eops[eng])
        self.ops.append(op)
        self.eops[eng].append(op)
        deps = op.deps
        for b in reads:
            for t in b.last_w:
                deps.append((t, "RAW"))
            if b.name.startswith("bank"):
                for k, t in b.readers.items():
                    if k != eng:
                        deps.append((t, "RAW"))
        for b in writes:
            for t in b.last_w:
                deps.append((t, "WAW"))
            for t in b.readers.values():
                deps.append((t, "WAR"))
        return op

    def _commit(self, op, tok, reads, writes):
        for b in writes:
            b.last_w = [tok]
            b.readers = {}
        for b in reads:
            if tok[0] == "op":
                b.readers[op.eng] = tok
            else:
                b.readers[("dma", tok[1], tok[2])] = tok

    def op(self, eng, fn, reads=(), writes=()):
        reads = [self.buf(x) if isinstance(x, str) else x for x in reads]
        writes = [self.buf(x) if isinstance(x, str) else x for x in writes]
        op = self._mkop(eng, fn, reads, writes)
        tok = ("op", op)
        self._commit(op, tok, reads, writes)
        return op

    def dma(self, eng, out, in_, reads=(), writes=(), **kw):
        reads = [self.buf(x) if isinstance(x, str) else x for x in reads]
        writes = [self.buf(x) if isinstance(x, str) else x for x in writes]

        def fn(e, out=out, in_=in_, kw=kw):
            return e.dma_start(out=out, in_=in_, **kw)

        op = self._mkop(eng, fn, reads, writes)
        k = self.dma_rr[eng]
        self.dma_rr[eng] = (k + 1) % self.ndma[eng]
        key = (eng, k)
        prev = self.dma_last.get(key)
        if prev is not None:
            op.deps.append((prev, "SEM"))
        cnt = self.dma_cnt.get(key, 0) + 1
        self.dma_cnt[key] = cnt
        tok = ("dma", key, 16 * cnt)
        self.dma_last[key] = tok
        op.dma = (key, 16 * cnt)
        self._commit(op, tok, reads, writes)
        return op

    def emit(self):
        nc = self.nc
        for op in self.ops:
            for (t, kind) in op.deps:
                if t[0] != "op":
                    continue
                p = t[1]
                if p.eng == op.eng and p.eng == "pe":
                    continue
                if kind != "WAR":
                    p.marked = True
        marked_idx = {e: [o.eidx for o in self.eops[e] if o.marked] for e in ENGS}
        resolved = {}
        for op in self.ops:
            for (t, kind) in op.deps:
                if t[0] != "op" or kind != "WAR":
                    continue
                p = t[1]
                if p.eng == op.eng and p.eng == "pe":
                    continue
                if p.marked:
                    continue
                lst = marked_idx[p.eng]
                i = bisect.bisect_left(lst, p.eidx)
                ok = False
                if i < len(lst):
                    q = self.eops[p.eng][lst[i]]
                    if q.gidx < op.gidx:
                        resolved[(id(op), id(p))] = q
                        ok = True
                if not ok:
                    p.marked = True
                    bisect.insort(lst, p.eidx)
        for e in ENGS:
            c = 0
            for o in self.eops[e]:
                if o.marked:
                    c += 1
                o.cum = c
        sems = {}
        ctx = []
        for e in ENGS:
            g = nc.semaphore("s_" + e)
            sems[e] = g.__enter__()
            ctx.append(g)
        dsem = {}
        for key in self.dma_cnt:
            g = nc.semaphore("d_%s_%d" % key)
            dsem[key] = g.__enter__()
            ctx.append(g)
        self.n_waits = 0

        def run_engine(ename, eng):
            waited = {}
            for o in self.eops[ename]:
                need = {}
                for (t, kind) in o.deps:
                    if t[0] == "op":
                        p = t[1]
                        if p.eng == ename and ename == "pe":
                            continue
                        if not p.marked:
                            p = resolved[(id(o), id(p))]
                        key = ("e", p.eng)
                        val = p.cum
                    else:
                        key = ("d", t[1])
                        val = t[2]
                    if val > need.get(key, 0):
                        need[key] = val
                for key, val in need.items():
                    if val > waited.get(key, 0):
                        waited[key] = val
                        s = sems[key[1]] if key[0] == "e" else dsem[key[1]]
                        eng.wait_ge(s, val)
                        self.n_waits += 1
                ins = o.fn(eng)
                if o.tag is not None:
                    ins.annotate(o.tag)
                if o.dma is not None:
                    ins.then_inc(dsem[o.dma[0]], 16)
                elif o.marked:
                    ins.then_inc(sems[ename], 1)
            if ename == "sp":
                for key, cnt in self.dma_cnt.items():
                    if 16 * cnt > waited.get(("d", key), 0):
                        eng.wait_ge(dsem[key], 16 * cnt)

        with nc.Block() as block:
            @block.tensor
            def _(eng):
                run_engine("pe", eng)

            @block.scalar
            def _(eng):
                run_engine("act", eng)

            @block.vector
            def _(eng):
                run_engine("dve", eng)

            @block.gpsimd
            def _(eng):
                run_engine("pool", eng)

            @block.sync
            def _(eng):
                run_engine("sp", eng)
        for g in reversed(ctx):
            g.__exit__(None, None, None)
S = 4096
D = 1024
NBLK = 8
NT = 32
KPAD = 1024
DIL = (1, 4, 16)
NORM_EPS = 1e-6
LN_EPS = 1e-5
NV = 128
V_CB, V_LG, V_LB, V_PS, V_BG, V_IS, V_CW = 0, 2, 4, 6, 8, 40, 42


class Arena:
    def __init__(self, nc, nbytes):
        self.nbytes = nbytes
        self.h16 = nc.alloc_sbuf_tensor("arena", [128, nbytes // 2], BF16)
        self.h32 = self.h16.bitcast(F32)
        self.off = 0

    def reset(self):
        self.off = 0

    def alloc(self, free_shape, dt):
        es = 2 if dt == BF16 else 4
        n = 1
        for s in free_shape:
            n *= s
        nb = (n * es + 63) // 64 * 64
        assert self.off + nb <= self.nbytes, ("arena overflow", self.off, nb, self.nbytes)
        h = self.h16 if dt == BF16 else self.h32
        ps = self.nbytes // es
        dims = [[ps, 128]]
        st = n
        for s in free_shape:
            st //= s
            dims.append([st, s])
        ap = bass.AP(h, self.off // es, dims)
        self.off += nb
        return ap


def build_program(nc, n_layers=2, dbg=False, stop=None):
    P = Prog(nc)
    skind = "ExternalOutput" if dbg else "Internal"

    def din(name, shape, dt=F32):
        return nc.dram_tensor(name, list(shape), dt, kind="ExternalInput").ap()

    L = 2
    x_d = din("x", [S, D])
    w_in_d = din("w_in", [L, D, 4352])
    w_fourier_d = din("w_fourier", [L, 256, 256])
    w_pw_d = din("w_pw", [L, 256, 256])
    w_pool_d = din("w_pool", [L, 4, 64, 64])
    w_branch_d = din("w_branch", [L, 4, 256, 1024])
    w_gate_d = din("w_gate", [L, 4, D, D])
    w_out_d = din("w_out", [L, D, D])
    gvec_h = nc.dram_tensor("gvec", [L + 1, D], F32, kind="ExternalInput")
    vecs_d = din("vecs", [L, 128, NV])
    ident_d = din("ident", [128, 128], BF16)
    masks_d = din("masks", [128, 3, 512], BF16)
    ropec_d = din("ropec", [128, S])
    ropes_d = din("ropes", [128, S])
    dft_d = din("dft", [4, 16, 128, 2, 2, 512], BF16)
    dmid_d = din("dmid", [128, 32], BF16)
    cs_d = din("cs", [128, 2, 512], BF16)
    edge_d = din("edge", [128, 2, 16])
    out_d = nc.dram_tensor("out", [S, D], F32, kind="ExternalOutput").ap()
    yg_d = nc.dram_tensor("yg_scr", [1024, S], BF16, kind=skind).ap()
    mT_d = nc.dram_tensor("mT_scr", [1024, S], BF16, kind=skind).ap()
    xmid_d = nc.dram_tensor("xmid_scr", [S, D], F32, kind=skind).ap()
    hdbg_d = nc.dram_tensor("hT_dbg", [128, 8, S], BF16, kind="ExternalOutput").ap() if dbg else None

    hT = nc.alloc_sbuf_tensor("hT", [128, 8, S], BF16)
    ident = nc.alloc_sbuf_tensor("identsb", [128, 128], BF16)
    masks = nc.alloc_sbuf_tensor("maskssb", [128, 3, 512], BF16)
    gb = nc.alloc_sbuf_tensor("gb", [128, D], F32)
    vecs = nc.alloc_sbuf_tensor("vecssb", [128, NV], F32)
    onesm = nc.alloc_sbuf_tensor("onesm", [128, 128], F32)
    arena = Arena(nc, 138752)
    psall = nc.alloc_psum_tensor("psall", [128, 4096], F32)
    psall16 = psall.bitcast(BF16)
    banks = [psall[:, i * 512:(i + 1) * 512] for i in range(8)]
    banks16 = [psall16[:, i * 1024:(i + 1) * 1024] for i in range(8)]
    bank_rr = [0]

    def pb():
        i = bank_rr[0]
        bank_rr[0] = (i + 1) % 8
        return banks[i], "bank%d" % i

    def pb_pair():
        if bank_rr[0] % 2 == 1:
            bank_rr[0] = (bank_rr[0] + 1) % 8
        i = bank_rr[0]
        bank_rr[0] = (i + 2) % 8
        pair = psall[:, i * 512:(i + 2) * 512].rearrange("p (b c) -> p b c", b=2)
        return (banks[i], "bank%d" % i), (banks[i + 1], "bank%d" % (i + 1)), pair

    uid = [0]

    def nm(s):
        uid[0] += 1
        return "%s#%d" % (s, uid[0])

    class Rot:
        def __init__(self, name, n, shape, dt):
            self.tiles = [(arena.alloc(shape, dt), nm(name)) for _ in range(n)]
            self.i = 0

        def next(self):
            t = self.tiles[self.i]
            self.i = (self.i + 1) % len(self.tiles)
            return t

    def mm(out, lhsT, rhs, start, stop, reads, writes, tp=None):
        if tp is None:
            P.op("pe", lambda e: e.matmul(out, lhsT=lhsT, rhs=rhs, start=start, stop=stop), reads, writes)
        else:
            P.op("pe", lambda e: e.matmul(out, lhsT=lhsT, rhs=rhs, start=start, stop=stop, tile_position=tp), reads, writes)

    def act(out, in_, func, reads, writes, bias=None, scale=None, accum=None):
        kw = {}
        if bias is not None:
            kw["bias"] = bias
        if scale is not None:
            kw["scale"] = scale
        if accum is not None:
            kw["accum_out"] = accum
        P.op("act", lambda e: e.activation(out=out, in_=in_, func=func, **kw), reads, writes)

    def tt(eng, out, in0, in1, op, reads, writes):
        P.op(eng, lambda e: e.tensor_tensor(out=out, in0=in0, in1=in1, op=op), reads, writes)

    def ts(eng, out, in0, s1, s2, op0, op1, reads, writes):
        if s2 is None:
            P.op(eng, lambda e: e.tensor_scalar(out=out, in0=in0, scalar1=s1, scalar2=None, op0=op0), reads, writes)
        else:
            P.op(eng, lambda e: e.tensor_scalar(out=out, in0=in0, scalar1=s1, scalar2=s2, op0=op0, op1=op1), reads, writes)

    def stt(eng, out, in0, scalar, in1, op0, op1, reads, writes):
        P.op(eng, lambda e: e.scalar_tensor_tensor(out=out, in0=in0, scalar=scalar, in1=in1, op0=op0, op1=op1), reads, writes)

    def cp(eng, out, in_, reads, writes):
        if eng == "act":
            act(out, in_, AF.Copy, reads, writes)
        else:
            P.op(eng, lambda e: e.tensor_copy(out=out, in_=in_), reads, writes)

    def memset(eng, ap, val, writes):
        P.op(eng, lambda e: e.memset(ap, val), (), writes)

    def blkname(b):
        return "hT_b%d" % b

    def wcols(l, c0, n):
        return w_in_d[l].rearrange("(kc p) n -> p kc n", p=128)[:, :, c0:c0 + n]

    def load_w(ap_sb, dram_ap, name):
        P.dma("pool", ap_sb, dram_ap, writes=[name])

    P.dma("sp", ident[:, :], ident_d, writes=["ident"])
    P.dma("sp", masks[:, :, :], masks_d, writes=["masks"])
    memset("pool", onesm[:, :], 1.0 / 256.0, ["onesm"])

    def load_gb(idx):
        src = bass.AP(gvec_h, idx * D, [[0, 128], [1, D]])
        P.dma("sp", gb[:, :], src, writes=["gb"])

    stat = nc.alloc_sbuf_tensor("stat4", [128, 16], F32)

    def norm_stage1(xn, xn_name, tile, rots, last):
        sl_ = tile % 4
        st = stat[:, sl_ * 4:sl_ * 4 + 4]
        sn = ["stat%d_%d" % (sl_, k) for k in range(4)]
        junk, jn = rots["junk"].next()
        memset("pool", st[:, 0:1], 0.0, [sn[0]])
        act(junk, xn, AF.Square, [xn_name], [jn, sn[0]], accum=st[:, 0:1])
        ts("dve", st[:, 1:2], st[:, 0:1], 1.0 / D, NORM_EPS, ALU.mult, ALU.add, [sn[0]], [sn[1]])
        act(st[:, 2:3], st[:, 1:2], AF.Sqrt, [sn[1]], [sn[2]])
        P.op("dve", lambda e: e.reciprocal(out=st[:, 3:4], in_=st[:, 2:3]), [sn[2]], [sn[3]])
        if last:
            o, on = rots["ofin"].next()
            stt("dve", o, xn, st[:, 3:4], gb[:, :], ALU.mult, ALU.mult, [xn_name, sn[3], "gb"], [on])
            P.dma("pool", out_d[tile * 128:(tile + 1) * 128, :], o, reads=[on])
            return None
        h, hn = rots["h"].next()
        stt("dve", h, xn, st[:, 3:4], gb[:, :], ALU.mult, ALU.mult, [xn_name, sn[3], "gb"], [hn])
        return (h, hn)

    def norm_stage2(hh, tile):
        h, hn = hh
        bk, bn = pb()
        bk16 = banks16[int(bn[4:])]
        for c in range(8):
            P.op("pe", lambda e, c=c: e.transpose(bk16[:, c * 128:(c + 1) * 128], h[:, c * 128:(c + 1) * 128], ident[:, :]),
                 [hn, "ident"], [bn])
        src = bk16[:, :].rearrange("p (c t) -> p c t", c=8)
        cp("act", hT[:, :, tile * 128:(tile + 1) * 128], src, [bn], [blkname(tile // 4)])

    NLOOK = 2

    P.barrier()
    arena.reset()
    rots = {"x": Rot("x", 4, [D], F32), "junk": Rot("junk", 2, [D], BF16), "h": Rot("h", 4, [D], BF16)}
    load_gb(0)
    pend = {}
    for idx in range(NT + NLOOK):
        if idx < NT:
            xt, xnm = rots["x"].next()
            P.dma("sp", xt, x_d[idx * 128:(idx + 1) * 128, :], writes=[xnm])
            pend[idx] = norm_stage1(xt, xnm, idx, rots, False)
        if idx - NLOOK >= 0:
            norm_stage2(pend.pop(idx - NLOOK), idx - NLOOK)
    if dbg:
        P.dma("sp", hdbg_d, hT[:, :, :], reads=[blkname(b) for b in range(NBLK)])

    if stop == 'A0':
        P.emit()
        return P
    for l in range(n_layers):
        lastl = (l == n_layers - 1)
        P.barrier()
        P.dma("sp", vecs[:, :], vecs_d[l], writes=["vecs"])

        def gate_tile(Wg, Wgn, cc, blk, gr):
            bk, bn = pb()
            for kc in range(8):
                mm(bk[:, :], Wg[:, kc, cc * 128:(cc + 1) * 128], hT[:, kc, blk * 512:(blk + 1) * 512],
                   kc == 0, kc == 7, [Wgn, blkname(blk)], [bn])
            g, gn = gr.next()
            act(g, bk[:, :], AF.Silu, [bn], [gn])
            return g, gn

        P.barrier()
        arena.reset()
        qT = arena.alloc([S], BF16)
        kT = arena.alloc([S + 2 * KPAD], BF16)
        vb = arena.alloc([48, 256], BF16)
        acc = arena.alloc([2, S], F32)
        Wqk_r = Rot("Wqk", 2, [8, 2, 128], BF16)
        Wv_r = Rot("Wv", 2, [8, 128], BF16)
        WgC = arena.alloc([8, 256], BF16)
        rope_r = Rot("rope", 2, [2, 512], F32)
        rt_r = Rot("rt", 4, [512], F32)
        qs_r = Rot("qs", 2, [2, 512], F32)
        pT_r = Rot("pT", 5, [512], BF16)
        rd_r = Rot("rd", 2, [512], F32)
        g_r = Rot("g", 2, [512], F32)
        st_r = Rot("st", 2, [512], BF16)
        load_w(WgC, wcols(l, 3328 + 2 * 256, 256), "WgC")
        memset("pool", kT[:, 0:KPAD], 0.0, ["kTpadL"])
        memset("pool", kT[:, KPAD + S:KPAD + S + KPAD], 0.0, ["kTpadR"])
        memset("pool", vb[:, :, :], 0.0, ["vb"])
        memset("pool", vb[:, :, 64:192], 1.0, ["vb"])
        def finalize(hp):
            for blk in range(NBLK):
                sl = slice(blk * 512, (blk + 1) * 512)
                an = "acc_b%d" % blk
                rd, rdn = rd_r.next()
                act(rd[0:64, :], acc[64:128, 0, sl], AF.Ln, [an], [rdn])
                act(rd[64:128, :], acc[0:64, 1, sl], AF.Ln, [an], [rdn])
                act(rd, rd, AF.Exp, [rdn], [rdn], scale=-1.0)
                tt("pool", acc[0:64, 0, sl], acc[0:64, 0, sl], rd[0:64, :], ALU.mult, [an, rdn], [an])
                tt("pool", acc[64:128, 0, sl], acc[64:128, 1, sl], rd[64:128, :], ALU.mult, [an, rdn], [an])
            for blk in range(NBLK):
                sl = slice(blk * 512, (blk + 1) * 512)
                gt, gtn = gate_tile(WgC, "WgC", hp, blk, g_r)
                stg, stn = st_r.next()
                tt("dve", stg, acc[:, 0, sl], gt, ALU.mult, ["acc_b%d" % blk, gtn], [stn])
                P.dma("pool", yg_d[512 + hp * 128:512 + hp * 128 + 128, sl], stg, reads=[stn], writes=["yg_%d_%d" % (4 + hp, blk)])
        fin_pending = []
        for hp in range(2):
            for g in range(3):
                dil = DIL[g]
                Lg = S // dil
                ntl = Lg // 128
                nch = ntl + 1
                Wqk, Wqkn = Wqk_r.next()
                Wv, Wvn = Wv_r.next()
                qc0 = 768 + g * 256 + hp * 128
                kc0 = 1536 + g * 256 + hp * 128
                vc0 = 2304 + g * 256 + hp * 128
                load_w(Wqk[:, :, 0, :], wcols(l, qc0, 128), Wqkn + "_0")
                load_w(Wqk[:, :, 1, :], wcols(l, kc0, 128), Wqkn + "_2")
                load_w(Wv, wcols(l, vc0, 128), Wvn)
                for blk in range(NBLK):
                    rp, rpn = rope_r.next()
                    P.dma("sp", rp[:, 0, :], ropec_d[:, blk * 512:(blk + 1) * 512], writes=[rpn + "c"])
                    P.dma("sp", rp[:, 1, :], ropes_d[:, blk * 512:(blk + 1) * 512], writes=[rpn + "s"])
                    (bq, bqn), (bkk, bkn), pair = pb_pair()
                    for qk, (bk, bn) in enumerate(((bq, bqn), (bkk, bkn))):
                        for kc in range(8):
                            mm(bk[:, :], Wqk[:, kc, qk, :], hT[:, kc, blk * 512:(blk + 1) * 512],
                               kc == 0, kc == 7, [Wqkn + "_%d" % (2 * qk), blkname(blk)], [bn])
                    qs, qsn = qs_r.next()
                    for (src, dst) in ((32, 0), (0, 32), (96, 64), (64, 96)):
                        cp("act", qs[dst:dst + 32, :, :], pair[src:src + 32, :, :], [bqn, bkn], [qsn])
                    for qk, (bk, bn) in enumerate(((bq, bqn), (bkk, bkn))):
                        t1, t1n = rt_r.next()
                        t2, t2n = rt_r.next()
                        tt("dve", t1, bk[:, :], rp[:, 0, :], ALU.mult, [bn, rpn + "c"], [t1n])
                        tt("dve", t2, qs[:, qk, :], rp[:, 1, :], ALU.mult, [qsn, rpn + "s"], [t2n])
                        if qk == 0:
                            tt("pool", qT[:, blk * 512:(blk + 1) * 512], t1, t2, ALU.add, [t1n, t2n], ["qT_b%d" % blk])
                        else:
                            tt("dve", kT[:, KPAD + blk * 512:KPAD + (blk + 1) * 512], t1, t2, ALU.add, [t1n, t2n], ["kT_b%d" % blk])
                if stop == 'C1':
                    P.emit()
                    return P
                allq = ["qT_b%d" % b for b in range(NBLK)]
                allk = ["kT_b%d" % b for b in range(NBLK)] + ["kTpadL", "kTpadR"]
                allh = [blkname(b) for b in range(NBLK)]
                for r in range(dil):
                    for ci in range(nch):
                        ch = r * nch + ci
                        p0 = ci * 128 - 64
                        lo = max(p0, 0)
                        hi = min(p0 + 128, Lg)
                        npos = hi - lo
                        prow = lo - p0
                        t0 = r + dil * lo
                        bk, bn = pb()
                        for kc in range(8):
                            lhsT = hT[:, kc, t0:t0 + dil * (npos - 1) + 1:dil]
                            if prow == 0:
                                mm(bk[0:npos, 0:128], lhsT, Wv[:, kc, :], kc == 0, kc == 7, [Wvn] + allh, [bn])
                            else:
                                mm(bk[64:128, 0:128], lhsT, Wv[:, kc, :], kc == 0, kc == 7, [Wvn] + allh, [bn], tp=(0, 64))
                        dst = bass.AP(vb.tensor, vb.offset + prow * vb.ap[0][0] + ch * 256, [[vb.ap[0][0], npos], [192, 2], [1, 64]])
                        srcv = bk[prow:prow + npos, 0:128].rearrange("p (h d) -> p h d", h=2)
                        cp("act", dst, srcv, [bn], ["vb"])
                if stop == 'C2':
                    P.emit()
                    return P
                if g == 0 and fin_pending:
                    finalize(fin_pending.pop())
                tiles = [(r, ti) for r in range(dil) for ti in range(ntl)]
                LOOK = 3
                staged = {}

                def stage_a(idx):
                    r, ti = tiles[idx]
                    variant = 1 if ti == 0 else (2 if ti == ntl - 1 else 0)
                    (s0, s0n), (s1, s1n), spair = pb_pair()
                    qa = r + dil * 128 * ti
                    for kc in range(2):
                        ka = KPAD + r + dil * (128 * ti - 64 + 128 * kc)
                        for hh, (sb_, sbn_) in enumerate(((s0, s0n), (s1, s1n))):
                            rows = slice(hh * 64, hh * 64 + 64)
                            mm(sb_[:, kc * 128:(kc + 1) * 128], kT[rows, ka:ka + dil * 127 + 1:dil],
                               qT[rows, qa:qa + dil * 127 + 1:dil], True, True, allq + allk, [sbn_])
                    pT, pTn = pT_r.next()
                    act(pT.rearrange("p (h c) -> p h c", h=2), spair[:, :, 0:256], AF.Exp, [s0n, s1n], [pTn], scale=0.125)
                    tt("dve", pT, pT, masks[:, variant, :], ALU.mult, [pTn, "masks"], [pTn])
                    staged[idx] = (pT, pTn)

                def stage_b(idx):
                    r, ti = tiles[idx]
                    pT, pTn = staged.pop(idx)
                    obk, obn = pb()
                    for hh in range(2):
                        for kc in range(2):
                            ch = r * nch + ti + kc
                            bi = 2 * hh + kc
                            mm(obk[:, hh * 128:(hh + 1) * 128], vb[:, ch, hh * 128:(hh + 1) * 128],
                               pT[:, bi * 128:(bi + 1) * 128], kc == 0, kc == 1, ["vb", pTn], [obn])
                    qa = r + dil * 128 * ti
                    dst = acc[:, :, qa:qa + dil * 127 + 1:dil]
                    srco = obk[:, 0:256].rearrange("p (h t) -> p h t", h=2)
                    bset = sorted(set([(qa) // 512, (qa + dil * 127) // 512]))
                    accn = ["acc_b%d" % b for b in range(bset[0], bset[-1] + 1)]
                    if g == 0:
                        cp("dve", dst, srco, [obn], accn)
                    else:
                        tt("dve", dst, srco, dst, ALU.add, [obn] + accn, accn)

                for idx in range(len(tiles) + LOOK):
                    if idx < len(tiles):
                        stage_a(idx)
                    if idx - LOOK >= 0:
                        stage_b(idx - LOOK)
                if g == 2:
                    fin_pending.append(hp)
        while fin_pending:
            finalize(fin_pending.pop())
        if stop == 'C':
            P.emit()
            return P
        P.barrier()
        arena.reset()
        Wa = arena.alloc([8, 256], BF16)
        WgA = arena.alloc([8, 256], BF16)
        cs = arena.alloc([2, 512], BF16)
        wf = arena.alloc([2, 256], BF16)
        uaT = arena.alloc([2, S], BF16)
        PQ = arena.alloc([32, 512], BF16)
        fT = arena.alloc([2, S], BF16)
        dft_r = Rot("dft", 3, [2, 2, 512], BF16)
        sq_r = Rot("sq", 2, [512], F32)
        dmid = arena.alloc([32], BF16)
        P.dma("sp", dmid, dmid_d, writes=["dmid"])
        g_r = Rot("g", 2, [512], F32)
        st_r = Rot("st", 2, [512], BF16)
        load_w(Wa, wcols(l, 0, 256), "Wa")
        load_w(WgA, wcols(l, 3328, 256), "WgA")
        load_w(wf, w_fourier_d[l].rearrange("(c p) n -> p c n", p=128), "wf")
        P.dma("sp", cs, cs_d, writes=["cs"])
        for blk in range(NBLK):
            for c in range(2):
                bk, bn = pb()
                for kc in range(8):
                    mm(bk[:, :], Wa[:, kc, c * 128:(c + 1) * 128], hT[:, kc, blk * 512:(blk + 1) * 512],
                       kc == 0, kc == 7, ["Wa", blkname(blk)], [bn])
                cp("act" if c == 0 else "dve", uaT[:, c, blk * 512:(blk + 1) * 512], bk[:, :], [bn], ["uaT_b%d" % blk])
        for a in range(NT):
            bk, bn = pb()
            for c in range(2):
                mm(bk[:, :], uaT[:, c, a * 128:(a + 1) * 128], cs[:, c, :], c == 0, c == 1, ["uaT_b%d" % (a // 4), "cs"], [bn])
            cp("act" if a % 2 == 0 else "dve", PQ[:, a, :], bk[:, :], [bn], ["PQ_%d" % a])
        for ps_ in range(4):
            bC = [pb() for c in range(2)]
            bS = [pb() for c in range(2)]
            for a2 in range(NT // 2):
                dt_, dtn = dft_r.next()
                P.dma("sp", dt_, dft_d[ps_, a2], writes=[dtn])
                for ai in range(2):
                    a = 2 * a2 + ai
                    for c in range(2):
                        mm(bC[c][0][:, :], PQ[:, a, c * 128:(c + 1) * 128], dt_[:, ai, 0, :],
                           a == 0, a == NT - 1, ["PQ_%d" % a, dtn], [bC[c][1]])
                        mm(bS[c][0][:, :], PQ[:, a, 256 + c * 128:256 + (c + 1) * 128], dt_[:, ai, 1, :],
                           a == 0, a == NT - 1, ["PQ_%d" % a, dtn], [bS[c][1]])
            for c in range(2):
                sq, sqn = sq_r.next()
                cp("act", sq, bS[c][0][:, :], [bS[c][1]], [sqn])
                k0 = ps_ * 512
                tt("dve", fT[:, c, k0:k0 + 512], bC[c][0][:, :], sq, ALU.subtract, [bC[c][1], sqn], ["fT_b%d" % ps_])
                j0 = 1 if ps_ == 0 else 0
                fv = fT[:, c, :]
                rev = bass.AP(fv.tensor, fv.offset + (S - k0 - j0), [[fv.ap[0][0], 128], [-1, 512 - j0]])
                hin = ["fT_b%d" % (7 - ps_)] + (["fT_b%d" % (8 - ps_)] if ps_ >= 1 else [])
                tt("dve", rev, bC[c][0][:, j0:512], sq[:, j0:512], ALU.add, [bC[c][1], sqn], hin)
        bm_, bmn_ = pb()
        for c in range(2):
            for a in range(NT):
                mm(bm_[:, c:c + 1], PQ[:, a, c * 128:(c + 1) * 128], dmid[:, a:a + 1], a == 0, a == NT - 1, ["PQ_%d" % a, "dmid"], [bmn_])
        for c in range(2):
            cp("act", fT[:, c, S // 2:S // 2 + 1], bm_[:, c:c + 1], [bmn_], ["fT_b4"])
        for blk in range(NBLK):
            sl = slice(blk * 512, (blk + 1) * 512)
            for co in range(2):
                bk, bn = pb()
                for c in range(2):
                    mm(bk[:, :], wf[:, c, co * 128:(co + 1) * 128], fT[:, c, sl], c == 0, c == 1, ["wf", "fT_b%d" % blk], [bn])
                gt, gtn = gate_tile(WgA, "WgA", co, blk, g_r)
                stg, stn = st_r.next()
                tt("dve", stg, bk[:, :], gt, ALU.mult, [bn, gtn], [stn])
                P.dma("pool", yg_d[co * 128:(co + 1) * 128, sl], stg, reads=[stn], writes=["yg_%d_%d" % (co, blk)])

        if stop == 'A':
            P.emit()
            return P
        P.barrier()
        arena.reset()
        Wb = arena.alloc([8, 512], BF16)
        WgB = arena.alloc([8, 256], BF16)
        wpw = arena.alloc([2, 256], BF16)
        dg = arena.alloc([2, 31, 128], BF16)
        uT = arena.alloc([2, S + 32], BF16)
        sg_r = Rot("sg", 2, [512], F32)
        y_r = Rot("y", 3, [2, 512], F32)
        ysq_r = Rot("ysq", 3, [2, 512], F32)
        tmp_r = Rot("ctmp", 8, [512], F32)
        s_r = Rot("s", 3, [2, 512], BF16)
        g_r = Rot("g", 2, [512], F32)
        st_r = Rot("st", 2, [512], BF16)
        load_w(Wb, wcols(l, 256, 512), "Wb")
        load_w(WgB, wcols(l, 3328 + 256, 256), "WgB")
        load_w(wpw, w_pw_d[l].rearrange("(c p) n -> p c n", p=128), "wpw")
        for c in range(2):
            for k in range(31):
                ts("dve", dg[:, c, k, :], ident[:, :], vecs[:, V_CW + c * 31 + k:V_CW + c * 31 + k + 1], None, ALU.mult, None,
                   ["ident", "vecs"], ["dg_%d_%d" % (c, k)])
        memset("pool", uT[:, :, 0:16], 0.0, ["uTpadL"])
        memset("pool", uT[:, :, 16 + S:32 + S], 0.0, ["uTpadR"])
        for blk in range(NBLK):
            for c in range(2):
                bka, bna = pb()
                bkg, bng = pb()
                for kc in range(8):
                    mm(bka[:, :], Wb[:, kc, c * 128:(c + 1) * 128], hT[:, kc, blk * 512:(blk + 1) * 512],
                       kc == 0, kc == 7, ["Wb", blkname(blk)], [bna])
                for kc in range(8):
                    mm(bkg[:, :], Wb[:, kc, 256 + c * 128:256 + (c + 1) * 128], hT[:, kc, blk * 512:(blk + 1) * 512],
                       kc == 0, kc == 7, ["Wb", blkname(blk)], [bng])
                sg, sgn = sg_r.next()
                act(sg, bkg[:, :], AF.Sigmoid, [bng], [sgn])
                tt("dve", uT[:, c, 16 + blk * 512:16 + (blk + 1) * 512], bka[:, :], sg, ALU.mult, [bna, sgn], ["uT_b%d" % blk])
        cst = {}

        def b_s1(blk):
            urd = ["uT_b%d" % b for b in range(max(blk - 1, 0), min(blk + 1, NBLK - 1) + 1)] + ["uTpadL", "uTpadR"]
            y, yn = y_r.next()
            ysq, ysqn = ysq_r.next()
            for c in range(2):
                bk, bn = pb()
                for k in range(31):
                    o = 16 + blk * 512 + k - 15
                    mm(bk[:, :], dg[:, c, k, :], uT[:, c, o:o + 512], k == 0, k == 30, ["dg_%d_%d" % (c, k)] + urd, [bn])
                act(y[:, c, :], bk[:, :], AF.Identity, [bn, "vecs"], [yn], bias=vecs[:, V_CB + c:V_CB + c + 1])
                act(ysq[:, c, :], bk[:, :], AF.Square, [bn, "vecs"], [ysqn], bias=vecs[:, V_CB + c:V_CB + c + 1])
            cst[blk] = [y, yn, ysq, ysqn]

        def b_s2(blk):
            y, yn, ysq, ysqn = cst[blk]
            bm, bmn = pb()
            bs, bsn = pb()
            for c in range(2):
                mm(bm[:, :], onesm[:, :], y[:, c, :], c == 0, c == 1, ["onesm", yn], [bmn])
            for c in range(2):
                mm(bs[:, :], onesm[:, :], ysq[:, c, :], c == 0, c == 1, ["onesm", ysqn], [bsn])
            msq, msqn = tmp_r.next()
            act(msq, bm[:, :], AF.Square, [bmn], [msqn])
            var, varn = tmp_r.next()
            tt("dve", var, bs[:, :], msq, ALU.subtract, [bsn, msqn], [varn])
            ts("dve", var, var, LN_EPS, None, ALU.add, None, [varn], [varn])
            act(var, var, AF.Sqrt, [varn], [varn])
            P.op("dve", lambda e, var=var: e.reciprocal(out=var, in_=var), [varn], [varn])
            s_, sn = s_r.next()
            for c in range(2):
                d_, dn = tmp_r.next()
                tt("dve", d_, y[:, c, :], bm[:, :], ALU.subtract, [yn, bmn], [dn])
                tt("pool", d_, d_, var, ALU.mult, [dn, varn], [dn])
                act(s_[:, c, :], d_, AF.Silu, [dn, "vecs"], [sn], bias=vecs[:, V_LB + c:V_LB + c + 1], scale=vecs[:, V_LG + c:V_LG + c + 1])
            cst[blk] = [s_, sn]

        def b_s3(blk):
            sl = slice(blk * 512, (blk + 1) * 512)
            s_, sn = cst.pop(blk)
            for co in range(2):
                bk, bn = pb()
                for c in range(2):
                    mm(bk[:, :], wpw[:, c, co * 128:(co + 1) * 128], s_[:, c, :], c == 0, c == 1, ["wpw", sn], [bn])
                gt, gtn = gate_tile(WgB, "WgB", co, blk, g_r)
                stg, stn = st_r.next()
                tt("dve", stg, bk[:, :], gt, ALU.mult, [bn, gtn], [stn])
                P.dma("pool", yg_d[256 + co * 128:256 + (co + 1) * 128, sl], stg, reads=[stn], writes=["yg_%d_%d" % (2 + co, blk)])

        for i in range(NBLK + 2):
            if i < NBLK:
                b_s1(i)
            if 0 <= i - 1 < NBLK:
                b_s2(i - 1)
            if i - 2 >= 0:
                b_s3(i - 2)

        if stop == 'B':
            P.emit()
            return P
        P.barrier()
        arena.reset()
        Wd = arena.alloc([8, 256], BF16)
        WgD = arena.alloc([8, 256], BF16)
        wpb = arena.alloc([2, 128], BF16)
        ud = arena.alloc([2, S + 16], F32)
        bA = arena.alloc([S + 16], F32)
        bB = arena.alloc([S + 16], F32)
        pooled = arena.alloc([2, S], BF16)
        edge = arena.alloc([2, 16], F32)
        etmp = arena.alloc([16], F32)
        po_r = Rot("po", 2, [512], F32)
        g_r = Rot("g", 2, [512], F32)
        st_r = Rot("st", 2, [512], BF16)
        load_w(Wd, wcols(l, 3072, 256), "Wd")
        load_w(WgD, wcols(l, 3328 + 768, 256), "WgD")
        memset("pool", wpb[:, :, :], 0.0, ["wpb"])
        for gi in range(4):
            r0 = (gi % 2) * 64
            P.dma("pool", wpb[r0:r0 + 64, gi // 2, r0:r0 + 64], w_pool_d[l, gi], reads=["wpb"], writes=["wpb_%d" % gi])
        P.dma("sp", edge, edge_d, writes=["edge"])
        memset("pool", ud[:, :, 0:8], 0.0, ["udpadL"])
        memset("pool", ud[:, :, 8 + S:16 + S], 0.0, ["udpadR"])
        for c in range(2):
            for blk in range(NBLK):
                bk, bn = pb()
                for kc in range(8):
                    mm(bk[:, :], Wd[:, kc, c * 128:(c + 1) * 128], hT[:, kc, blk * 512:(blk + 1) * 512],
                       kc == 0, kc == 7, ["Wd", blkname(blk)], [bn])
                cp("act", ud[:, c, 8 + blk * 512:8 + (blk + 1) * 512], bk[:, :], [bn], ["ud_%d" % c])
        NP_ = S + 16
        for c in range(2):
            u = ud[:, c, :]
            tt("dve", bA[:, 1:NP_], u[:, 0:NP_ - 1], u[:, 1:NP_], ALU.add, ["ud_%d" % c, "udpadL", "udpadR"], ["bA"])
            tt("dve", bB[:, 2:NP_ - 1], bA[:, 1:NP_ - 2], bA[:, 3:NP_], ALU.add, ["bA"], ["bB"])
            if c == 1:
                tt("dve", bA[:, 4:NP_ - 4], bB[:, 2:NP_ - 6], bB[:, 6:NP_ - 2], ALU.add, ["bB"], ["bA"])
                tt("dve", bB[:, 8:NP_ - 8], bA[:, 4:NP_ - 12], bA[:, 12:NP_ - 4], ALU.add, ["bA"], ["bB"])
            for half in range(2):
                rows = slice(half * 64, half * 64 + 64)
                W_ = bA if half == 0 else bB
                stt("dve", pooled[rows, c, :], W_[rows, 8:8 + S], vecs[rows, V_IS + c:V_IS + c + 1], u[rows, 8:8 + S],
                    ALU.mult, ALU.subtract, ["bA", "bB", "ud_%d" % c, "vecs"], ["pooled_%d" % c])
                for e0, t0 in ((0, 0), (8, S - 8)):
                    tt("dve", etmp[rows, e0:e0 + 8], W_[rows, 8 + t0:16 + t0], edge[rows, c, e0:e0 + 8], ALU.mult,
                       ["bA", "bB", "edge"], ["etmp"])
                    tt("dve", pooled[rows, c, t0:t0 + 8], etmp[rows, e0:e0 + 8], u[rows, 8 + t0:16 + t0], ALU.subtract,
                       ["etmp", "ud_%d" % c], ["pooled_%d" % c])
        for co in range(2):
            for blk in range(NBLK):
                sl = slice(blk * 512, (blk + 1) * 512)
                bk, bn = pb()
                mm(bk[:, :], wpb[:, co, :], pooled[:, co, sl], True, True, ["wpb", "wpb_0", "wpb_1", "wpb_2", "wpb_3", "pooled_%d" % co], [bn])
                gt, gtn = gate_tile(WgD, "WgD", co, blk, g_r)
                stg, stn = st_r.next()
                po, pon = po_r.next()
                act(po, bk[:, :], AF.Copy, [bn, "vecs"], [pon], scale=vecs[:, V_PS + co:V_PS + co + 1])
                tt("pool", stg, po, gt, ALU.mult, [pon, gtn], [stn])
                P.dma("pool", yg_d[768 + co * 128:768 + (co + 1) * 128, sl], stg, reads=[stn], writes=["yg_%d_%d" % (6 + co, blk)])

        if stop == 'D':
            P.emit()
            return P
        P.barrier()
        arena.reset()
        ygT = arena.alloc([8, S], BF16)
        Wgt_r = Rot("Wgt", 3, [4, 8, 128], BF16)
        Wbr_r = Rot("Wbr", 3, [4, 2, 128], BF16)
        mst_r = Rot("mst", 2, [S], BF16)
        mg_r = Rot("mg", 2, [512], F32)
        mt_r = Rot("mtmp", 3, [512], F32)
        mac_r = Rot("macc", 2, [512], F32)
        for c8 in range(8):
            P.dma("sp", ygT[:, c8, :], yg_d[c8 * 128:(c8 + 1) * 128, :],
                  reads=["yg_%d_%d" % (c8, b) for b in range(NBLK)], writes=["ygT_%d" % c8])
        wslots = {}

        def m_load(j):
            Wgt, Wgtn = Wgt_r.next()
            Wbr, Wbrn = Wbr_r.next()
            for n in range(4):
                load_w(Wgt[:, n, :, :], w_gate_d[l, n].rearrange("(kc p) m -> p kc m", p=128)[:, :, j * 128:(j + 1) * 128], Wgtn + "_%d" % n)
                load_w(Wbr[:, n, :, :], w_branch_d[l, n].rearrange("(c p) m -> p c m", p=128)[:, :, j * 128:(j + 1) * 128], Wbrn + "_%d" % n)
            wslots[j] = (Wgt, Wgtn, Wbr, Wbrn)

        m_load(0)
        for j in range(8):
            if j + 1 < 8:
                m_load(j + 1)
            Wgt, Wgtn, Wbr, Wbrn = wslots.pop(j)
            mst, mstn = mst_r.next()
            for blk in range(NBLK):
                sl = slice(blk * 512, (blk + 1) * 512)
                macc, maccn = mac_r.next()
                for n in range(4):
                    bg, bgn = pb()
                    by, byn = pb()
                    for kc in range(8):
                        mm(bg[:, :], Wgt[:, n, kc, :], hT[:, kc, sl], kc == 0, kc == 7, [Wgtn + "_%d" % n, blkname(blk)], [bgn])
                    for c in range(2):
                        mm(by[:, :], Wbr[:, n, c, :], ygT[:, 2 * n + c, sl], c == 0, c == 1, [Wbrn + "_%d" % n, "ygT_%d" % (2 * n + c)], [byn])
                    mg, mgn = mg_r.next()
                    act(mg, bg[:, :], AF.Sigmoid, [bgn, "vecs"], [mgn], bias=vecs[:, V_BG + n * 8 + j:V_BG + n * 8 + j + 1])
                    if n == 0:
                        tt("dve", macc, by[:, :], mg, ALU.mult, [byn, mgn], [maccn])
                    else:
                        tmp, tmpn = mt_r.next()
                        tt("dve", tmp, by[:, :], mg, ALU.mult, [byn, mgn], [tmpn])
                        if n < 3:
                            tt("pool", macc, macc, tmp, ALU.add, [maccn, tmpn], [maccn])
                        else:
                            tt("pool", mst[:, sl], macc, tmp, ALU.add, [maccn, tmpn], [mstn])
            P.dma("pool", mT_d[j * 128:(j + 1) * 128, :], mst, reads=[mstn], writes=["mT_%d" % j])

        if stop == 'M':
            P.emit()
            return P
        P.barrier()
        arena.reset()
        wo = arena.alloc([8, D], BF16)
        mTb_r = Rot("mTb", 2, [8, 512], BF16)
        rots = {"x": Rot("x", 3, [D], F32), "xn": Rot("xn", 4, [D], F32), "junk": Rot("junk", 2, [D], BF16),
                "h": Rot("h", 4, [D], BF16), "ofin": Rot("ofin", 2, [D], F32)}
        load_w(wo, w_out_d[l].rearrange("(jc p) m -> p jc m", p=128), "wo")
        load_gb(l + 1)
        xsrc = x_d if l == 0 else xmid_d
        pend = {}
        cur = {}

        def e_stage1(tile):
            blk, t4 = tile // 4, tile % 4
            if t4 == 0:
                mTb, mTbn = mTb_r.next()
                P.dma("sp", mTb, mT_d.rearrange("(jc p) t -> p jc t", p=128)[:, :, blk * 512:(blk + 1) * 512],
                      reads=["mT_%d" % j for j in range(8)], writes=[mTbn])
                cur["m"] = (mTb, mTbn)
            mTb, mTbn = cur["m"]
            xt, xnm = rots["x"].next()
            rd_ = ["xmid_%d" % tile] if l > 0 else []
            P.dma("sp", xt, xsrc[tile * 128:(tile + 1) * 128, :], reads=rd_, writes=[xnm])
            hb = []
            for half in range(2):
                bk, bn = pb()
                for jc in range(8):
                    mm(bk[:, :], mTb[:, jc, t4 * 128:(t4 + 1) * 128], wo[:, jc, half * 512:(half + 1) * 512],
                       jc == 0, jc == 7, [mTbn, "wo"], [bn])
                hb.append((bk, bn))
            xn, xnn = rots["xn"].next()
            for half in range(2):
                tt("dve", xn[:, half * 512:(half + 1) * 512], xt[:, half * 512:(half + 1) * 512], hb[half][0][:, :], ALU.add,
                   [xnm, hb[half][1]], [xnn])
            if (not lastl) or dbg:
                P.dma("pool", xmid_d[tile * 128:(tile + 1) * 128, :], xn, reads=[xnn], writes=["xmid_%d" % tile])
            return norm_stage1(xn, xnn, tile, rots, lastl)

        for idx in range(NT + NLOOK):
            if idx < NT:
                pend[idx] = e_stage1(idx)
            if idx - NLOOK >= 0:
                hh = pend.pop(idx - NLOOK)
                if hh is not None:
                    norm_stage2(hh, idx - NLOOK)
    P.emit()
    return P

import ml_dtypes as _mld

_BF = _mld.bfloat16
_CONST_CACHE = {}


def _constants():
    if _CONST_CACHE:
        return _CONST_CACHE
    c = {}
    c["ident"] = np.eye(128, dtype=np.float32).astype(_BF)
    p = np.arange(128)[:, None]
    n = np.arange(128)[None, :]
    M0 = (n <= p).astype(np.float32)
    M1 = (n >= p).astype(np.float32)
    M0f = M0 * (p >= 64)
    M1l = M1 * (p < 64)
    mk = np.zeros((128, 3, 512), np.float32)
    for v, (a, b) in enumerate(((M0, M1), (M0f, M1), (M0, M1l))):
        mk[:, v, :] = np.concatenate([a, b, a, b], axis=1)
    c["masks"] = mk.astype(_BF)
    inv = (1.0 / (np.float32(10000.0) ** (np.arange(0, 64, 2, dtype=np.float32) / np.float32(64)))).astype(np.float32)
    ang = (np.arange(S, dtype=np.float32)[None, :] * inv[:, None]).astype(np.float32)
    cosv = np.cos(ang).astype(np.float32)
    sinv = np.sin(ang).astype(np.float32)
    rc = np.zeros((128, S), np.float32)
    rs = np.zeros((128, S), np.float32)
    for q in range(4):
        rc[q * 32:(q + 1) * 32] = cosv
        rs[q * 32:(q + 1) * 32] = sinv if (q % 2 == 1) else -sinv
    c["ropec"] = rc
    c["ropes"] = rs
    s_idx = np.arange(S, dtype=np.int64)[:, None]
    k_idx = np.arange(S // 2, dtype=np.int64)[None, :]
    ph = ((s_idx * k_idx) % S).astype(np.float64) * (2.0 * np.pi / S)
    Cm = (np.cos(ph) / 512.0).astype(np.float32).reshape(16, 2, 128, 4, 512)
    Sm = (np.sin(ph) / 512.0).astype(np.float32).reshape(16, 2, 128, 4, 512)
    del ph
    dft = np.empty((4, 16, 128, 2, 2, 512), dtype=_BF)
    dft[:, :, :, :, 0, :] = Cm.transpose(3, 0, 2, 1, 4).astype(_BF)
    dft[:, :, :, :, 1, :] = Sm.transpose(3, 0, 2, 1, 4).astype(_BF)
    c["dft"] = dft
    sgn = np.where((np.arange(S) % 2) == 0, 1.0, -1.0).astype(np.float32) / 512.0
    c["dmid"] = np.ascontiguousarray(sgn.reshape(32, 128).T).astype(_BF)
    cm = np.arange(64)
    phc = ((cm[:, None] * cm[None, :]) % 64).astype(np.float64) * (2.0 * np.pi / 64)
    cs = np.zeros((256, 512), np.float32)
    for gi in range(4):
        cs[gi * 64:(gi + 1) * 64, gi * 64:(gi + 1) * 64] = np.cos(phc)
        cs[gi * 64:(gi + 1) * 64, 256 + gi * 64:256 + (gi + 1) * 64] = np.sin(phc)
    c["cs"] = np.ascontiguousarray(cs.reshape(2, 128, 512).transpose(1, 0, 2)).astype(_BF)
    sizes = (2, 4, 8, 16)
    inv_s = np.zeros((128, 2), np.float32)
    edge = np.zeros((128, 2, 16), np.float32)
    for gi, sz in enumerate(sizes):
        rows = slice((gi % 2) * 64, (gi % 2) * 64 + 64)
        cch = gi // 2
        inv_s[rows, cch] = 1.0 / sz
        for e in range(16):
            t = e if e < 8 else S - 16 + e
            lo = max(t - sz // 2, 0)
            hi = min(t + sz - 1 - sz // 2, S - 1)
            edge[rows, cch, e] = 1.0 / float(hi - lo + 1)
    c["inv_s"] = inv_s
    c["edge"] = edge
    perm = np.zeros(1536, np.int64)
    for j in range(1536):
        base = (j // 64) * 64
        d = j % 64
        perm[j] = base + (d + 32 if d < 32 else d - 32)
    c["perm"] = perm
    _CONST_CACHE.update(c)
    return c


def _prep_inputs(inp):
    c = _constants()
    f = lambda a: np.ascontiguousarray(np.asarray(a, dtype=np.float32))
    w_in = f(inp["w_in"])
    L = w_in.shape[0]
    vecs = np.zeros((L, 128, NV), np.float32)

    def pc(v):
        return f(v).reshape(L, 2, 128).transpose(0, 2, 1)

    vecs[:, :, V_CB:V_CB + 2] = pc(inp["conv_b"])
    vecs[:, :, V_LG:V_LG + 2] = pc(inp["conv_ln_g"])
    vecs[:, :, V_LB:V_LB + 2] = pc(inp["conv_ln_b"])
    vecs[:, :, V_PS:V_PS + 2] = pc(inp["pool_scale"])
    bg = f(inp["b_gate"]).reshape(L, 4, 8, 128).transpose(0, 3, 1, 2).reshape(L, 128, 32)
    vecs[:, :, V_BG:V_BG + 32] = bg
    vecs[:, :, V_IS:V_IS + 2] = c["inv_s"][None]
    cw = f(inp["conv_w"]).reshape(L, 31, 2, 128).transpose(0, 3, 2, 1).reshape(L, 128, 62)
    vecs[:, :, V_CW:V_CW + 62] = cw
    gvec = np.concatenate([f(inp["norm_g"]), f(inp["final_g"])[None, :]], axis=0)
    shared = {
        "w_in": w_in, "w_fourier": f(inp["w_fourier"]), "w_pw": f(inp["w_pw"]),
        "w_pool": f(inp["w_pool"]), "w_branch": f(inp["w_branch"]), "w_gate": f(inp["w_gate"]),
        "w_out": f(inp["w_out"]), "gvec": np.ascontiguousarray(gvec), "vecs": vecs,
        "ident": c["ident"], "masks": c["masks"], "ropec": c["ropec"], "ropes": c["ropes"],
        "dft": c["dft"], "dmid": c["dmid"], "cs": c["cs"], "edge": c["edge"],
    }
    return shared


_NC_CACHE = {}


def _get_nc(n_layers=2, dbg=False, stop=None):
    key = (n_layers, dbg, stop)
    if key not in _NC_CACHE:
        nc = bass.Bass("TRN2", target_bir_lowering=False)
        build_program(nc, n_layers=n_layers, dbg=dbg, stop=stop)
        _NC_CACHE[key] = nc
    return _NC_CACHE[key]


def kernel(**inputs):
    from concourse.bass_utils import run_bass_kernel_spmd
    x = np.ascontiguousarray(np.asarray(inputs["x"], dtype=np.float32))
    B = x.shape[0]
    shared = _prep_inputs(inputs)
    nc = _get_nc(2, False)
    in_maps = []
    for b in range(B):
        m = dict(shared)
        m["x"] = x[b]
        in_maps.append(m)
    res = run_bass_kernel_spmd(nc, in_maps, core_ids=list(range(B)))
    out = np.stack([np.asarray(r["out"], dtype=np.float32) for r in res.results], axis=0)
    return out
```

```python
import bisect
import numpy as np
import concourse.bass as bass
import concourse.mybir as mybir

F32 = mybir.dt.float32
BF16 = mybir.dt.bfloat16
ALU = mybir.AluOpType
AF = mybir.ActivationFunctionType
AX = mybir.AxisListType

ENGS = ("pe", "act", "dve", "pool", "sp")


class Buf:
    __slots__ = ("name", "last_w", "readers")

    def __init__(self, name):
        self.name = name
        self.last_w = []
        self.readers = {}


class Op:
    __slots__ = ("eng", "fn", "deps", "gidx", "eidx", "marked", "dma", "cum", "tag")

    def __init__(self, eng, fn, gidx):
        self.eng = eng
        self.fn = fn
        self.deps = []
        self.gidx = gidx
        self.eidx = -1
        self.marked = False
        self.dma = None
        self.tag = None
        self.cum = 0


class Prog:
    def __init__(self, nc, n_dma_sems_sp=24, n_dma_sems_pool=16):
        self.nc = nc
        self.ops = []
        self.eops = {e: [] for e in ENGS}
        self.ndma = {"sp": n_dma_sems_sp, "pool": n_dma_sems_pool, "act": 16}
        self.dma_rr = {"sp": 0, "pool": 0, "act": 0}
        self.dma_cnt = {}
        self.dma_last = {}
        self.bufs = {}
        self.epoch = []
        import os as _os
        self.tagging = bool(_os.environ.get('KTAG'))

    def buf(self, name):
        b = self.bufs.get(name)
        if b is None:
            b = Buf(name)
            b.last_w = list(self.epoch)
            self.bufs[name] = b
        return b

    def barrier(self):
        toks = []
        for e in ENGS:
            if self.eops[e]:
                o = self.eops[e][-1]
                if o.dma is None:
                    toks.append(("op", o))
                else:
                    for q in reversed(self.eops[e]):
                        if q.dma is None:
                            toks.append(("op", q))
                            break
        for key, t in self.dma_last.items():
            toks.append(t)
        self.epoch = toks
        for b in self.bufs.values():
            b.last_w = list(toks)
            b.readers = {}

    def _mkop(self, eng, fn, reads, writes):
        op = Op(eng, fn, len(self.ops))
        if self.tagging:
            import sys as _sys
            fr = _sys._getframe(2)
            while fr is not None and fr.f_code.co_name not in ("build_program", "norm_tile", "gate_tile"):
                fr = fr.f_back
            op.tag = "L%d" % fr.f_lineno if fr is not None else "?"
        op.eidx = len(self.eops[eng])
        self.ops.append(op)
        self.eops[eng].append(op)
        deps = op.deps
        for b in reads:
            for t in b.last_w:
                deps.append((t, "RAW"))
            if b.name.startswith("bank"):
                for k, t in b.readers.items():
                    if k != eng:
                        deps.append((t, "RAW"))
        for b in writes:
            for t in b.last_w:
                deps.append((t, "WAW"))
            for t in b.readers.values():
                deps.append((t, "WAR"))
        return op

    def _commit(self, op, tok, reads, writes):
        for b in writes:
            b.last_w = [tok]
            b.readers = {}
        for b in reads:
            if tok[0] == "op":
                b.readers[op.eng] = tok
            else:
                b.readers[("dma", tok[1], tok[2])] = tok

    def op(self, eng, fn, reads=(), writes=()):
        reads = [self.buf(x) if isinstance(x, str) else x for x in reads]
        writes = [self.buf(x) if isinstance(x, str) else x for x in writes]
        op = self._mkop(eng, fn, reads, writes)
        tok = ("op", op)
        self._commit(op, tok, reads, writes)
        return op

    def dma(self, eng, out, in_, reads=(), writes=(), **kw):
        reads = [self.buf(x) if isinstance(x, str) else x for x in reads]
        writes = [self.buf(x) if isinstance(x, str) else x for x in writes]

        def fn(e, out=out, in_=in_, kw=kw):
            return e.dma_start(out=out, in_=in_, **kw)

        op = self._mkop(eng, fn, reads, writes)
        k = self.dma_rr[eng]
        self.dma_rr[eng] = (k + 1) % self.ndma[eng]
        key = (eng, k)
        prev = self.dma_last.get(key)
        if prev is not None:
            op.deps.append((prev, "SEM"))
        cnt = self.dma_cnt.get(key, 0) + 1
        self.dma_cnt[key] = cnt
        tok = ("dma", key, 16 * cnt)
        self.dma_last[key] = tok
        op.dma = (key, 16 * cnt)
        self._commit(op, tok, reads, writes)
        return op

    def emit(self):
        nc = self.nc
        for op in self.ops:
            for (t, kind) in op.deps:
                if t[0] != "op":
                    continue
                p = t[1]
                if p.eng == op.eng and p.eng == "pe":
                    continue
                if kind != "WAR":
                    p.marked = True
        marked_idx = {e: [o.eidx for o in self.eops[e] if o.marked] for e in ENGS}
        resolved = {}
        for op in self.ops:
            for (t, kind) in op.deps:
                if t[0] != "op" or kind != "WAR":
                    continue
                p = t[1]
                if p.eng == op.eng and p.eng == "pe":
                    continue
                if p.marked:
                    continue
                lst = marked_idx[p.eng]
                i = bisect.bisect_left(lst, p.eidx)
                ok = False
                if i < len(lst):
                    q = self.eops[p.eng][lst[i]]
                    if q.gidx < op.gidx:
                        resolved[(id(op), id(p))] = q
                        ok = True
                if not ok:
                    p.marked = True
                    bisect.insort(lst, p.eidx)
        for e in ENGS:
            c = 0
            for o in self.eops[e]:
                if o.marked:
                    c += 1
                o.cum = c
        sems = {}
        ctx = []
        for e in ENGS:
            g = nc.semaphore("s_" + e)
            sems[e] = g.__enter__()
            ctx.append(g)
        dsem = {}
        for key in self.dma_cnt:
            g = nc.semaphore("d_%s_%d" % key)
            dsem[key] = g.__enter__()
            ctx.append(g)
        self.n_waits = 0

        def run_engine(ename, eng):
            waited = {}
            for o in self.eops[ename]:
                need = {}
                for (t, kind) in o.deps:
                    if t[0] == "op":
                        p = t[1]
                        if p.eng == ename and ename == "pe":
                            continue
                        if not p.marked:
                            p = resolved[(id(o), id(p))]
                        key = ("e", p.eng)
                        val = p.cum
                    else:
                        key = ("d", t[1])
                        val = t[2]
                    if val > need.get(key, 0):
                        need[key] = val
                for key, val in need.items():
                    if val > waited.get(key, 0):
                        waited[key] = val
                        s = sems[key[1]] if key[0] == "e" else dsem[key[1]]
                        eng.wait_ge(s, val)
                        self.n_waits += 1
                ins = o.fn(eng)
                if o.tag is not None:
                    ins.annotate(o.tag)
                if o.dma is not None:
                    ins.then_inc(dsem[o.dma[0]], 16)
                elif o.marked:
                    ins.then_inc(sems[ename], 1)
            if ename == "sp":
                for key, cnt in self.dma_cnt.items():
                    if 16 * cnt > waited.get(("d", key), 0):
                        eng.wait_ge(dsem[key], 16 * cnt)

        with nc.Block() as block:
            @block.tensor
            def _(eng):
                run_engine("pe", eng)

            @block.scalar
            def _(eng):
                run_engine("act", eng)

            @block.vector
            def _(eng):
                run_engine("dve", eng)

            @block.gpsimd
            def _(eng):
                run_engine("pool", eng)

            @block.sync
            def _(eng):
                run_engine("sp", eng)
        for g in reversed(ctx):
            g.__exit__(None, None, None)
S = 4096
D = 1024
NBLK = 8
NT = 32
KPAD = 1024
DIL = (1, 4, 16)
NORM_EPS = 1e-6
LN_EPS = 1e-5
NV = 128
V_CB, V_LG, V_LB, V_PS, V_BG, V_IS, V_CW = 0, 2, 4, 6, 8, 40, 42


class Arena:
    def __init__(self, nc, nbytes):
        self.nbytes = nbytes
        self.h16 = nc.alloc_sbuf_tensor("arena", [128, nbytes // 2], BF16)
        self.h32 = self.h16.bitcast(F32)
        self.off = 0

    def reset(self):
        self.off = 0

    def alloc(self, free_shape, dt):
        es = 2 if dt == BF16 else 4
        n = 1
        for s in free_shape:
            n *= s
        nb = (n * es + 63) // 64 * 64
        assert self.off + nb <= self.nbytes, ("arena overflow", self.off, nb, self.nbytes)
        h = self.h16 if dt == BF16 else self.h32
        ps = self.nbytes // es
        dims = [[ps, 128]]
        st = n
        for s in free_shape:
            st //= s
            dims.append([st, s])
        ap = bass.AP(h, self.off // es, dims)
        self.off += nb
        return ap


def build_program(nc, n_layers=2, dbg=False, stop=None):
    P = Prog(nc)
    skind = "ExternalOutput" if dbg else "Internal"

    def din(name, shape, dt=F32):
        return nc.dram_tensor(name, list(shape), dt, kind="ExternalInput").ap()

    L = 2
    x_d = din("x", [S, D])
    w_in_d = din("w_in", [L, D, 4352])
    w_fourier_d = din("w_fourier", [L, 256, 256])
    w_pw_d = din("w_pw", [L, 256, 256])
    w_pool_d = din("w_pool", [L, 4, 64, 64])
    w_branch_d = din("w_branch", [L, 4, 256, 1024])
    w_gate_d = din("w_gate", [L, 4, D, D])
    w_out_d = din("w_out", [L, D, D])
    gvec_h = nc.dram_tensor("gvec", [L + 1, D], F32, kind="ExternalInput")
    vecs_d = din("vecs", [L, 128, NV])
    ident_d = din("ident", [128, 128], BF16)
    masks_d = din("masks", [128, 3, 512], BF16)
    ropec_d = din("ropec", [128, S])
    ropes_d = din("ropes", [128, S])
    dft_d = din("dft", [4, 16, 128, 2, 2, 512], BF16)
    dmid_d = din("dmid", [128, 32], BF16)
    cs_d = din("cs", [128, 2, 512], BF16)
    edge_d = din("edge", [128, 2, 16])
    out_d = nc.dram_tensor("out", [S, D], F32, kind="ExternalOutput").ap()
    yg_d = nc.dram_tensor("yg_scr", [1024, S], BF16, kind=skind).ap()
    mT_d = nc.dram_tensor("mT_scr", [1024, S], BF16, kind=skind).ap()
    xmid_d = nc.dram_tensor("xmid_scr", [S, D], F32, kind=skind).ap()
    hdbg_d = nc.dram_tensor("hT_dbg", [128, 8, S], BF16, kind="ExternalOutput").ap() if dbg else None

    hT = nc.alloc_sbuf_tensor("hT", [128, 8, S], BF16)
    ident = nc.alloc_sbuf_tensor("identsb", [128, 128], BF16)
    masks = nc.alloc_sbuf_tensor("maskssb", [128, 3, 512], BF16)
    gb = nc.alloc_sbuf_tensor("gb", [128, D], F32)
    vecs = nc.alloc_sbuf_tensor("vecssb", [128, NV], F32)
    onesm = nc.alloc_sbuf_tensor("onesm", [128, 128], F32)
    epsc = nc.alloc_sbuf_tensor("epsc", [128, 1], F32)
    arena = Arena(nc, 138752)
    psall = nc.alloc_psum_tensor("psall", [128, 4096], F32)
    psall16 = psall.bitcast(BF16)
    banks = [psall[:, i * 512:(i + 1) * 512] for i in range(8)]
    banks16 = [psall16[:, i * 1024:(i + 1) * 1024] for i in range(8)]
    bank_rr = [0]

    def pb():
        i = bank_rr[0]
        bank_rr[0] = (i + 1) % 8
        return banks[i], "bank%d" % i

    def pb_pair():
        if bank_rr[0] % 2 == 1:
            bank_rr[0] = (bank_rr[0] + 1) % 8
        i = bank_rr[0]
        bank_rr[0] = (i + 2) % 8
        pair = psall[:, i * 512:(i + 2) * 512].rearrange("p (b c) -> p b c", b=2)
        return (banks[i], "bank%d" % i), (banks[i + 1], "bank%d" % (i + 1)), pair

    uid = [0]

    def nm(s):
        uid[0] += 1
        return "%s#%d" % (s, uid[0])

    class Rot:
        def __init__(self, name, n, shape, dt):
            self.tiles = [(arena.alloc(shape, dt), nm(name)) for _ in range(n)]
            self.i = 0

        def next(self):
            t = self.tiles[self.i]
            self.i = (self.i + 1) % len(self.tiles)
            return t

    def mm(out, lhsT, rhs, start, stop, reads, writes, tp=None):
        if tp is None:
            P.op("pe", lambda e: e.matmul(out, lhsT=lhsT, rhs=rhs, start=start, stop=stop), reads, writes)
        else:
            P.op("pe", lambda e: e.matmul(out, lhsT=lhsT, rhs=rhs, start=start, stop=stop, tile_position=tp), reads, writes)

    def act(out, in_, func, reads, writes, bias=None, scale=None, accum=None):
        kw = {}
        if bias is not None:
            kw["bias"] = bias
        if scale is not None:
            kw["scale"] = scale
        if accum is not None:
            kw["accum_out"] = accum
        P.op("act", lambda e: e.activation(out=out, in_=in_, func=func, **kw), reads, writes)

    def tt(eng, out, in0, in1, op, reads, writes):
        P.op(eng, lambda e: e.tensor_tensor(out=out, in0=in0, in1=in1, op=op), reads, writes)

    def ts(eng, out, in0, s1, s2, op0, op1, reads, writes):
        if s2 is None:
            P.op(eng, lambda e: e.tensor_scalar(out=out, in0=in0, scalar1=s1, scalar2=None, op0=op0), reads, writes)
        else:
            P.op(eng, lambda e: e.tensor_scalar(out=out, in0=in0, scalar1=s1, scalar2=s2, op0=op0, op1=op1), reads, writes)

    def stt(eng, out, in0, scalar, in1, op0, op1, reads, writes):
        P.op(eng, lambda e: e.scalar_tensor_tensor(out=out, in0=in0, scalar=scalar, in1=in1, op0=op0, op1=op1), reads, writes)

    def cp(eng, out, in_, reads, writes):
        if eng == "act":
            act(out, in_, AF.Copy, reads, writes)
        else:
            P.op(eng, lambda e: e.tensor_copy(out=out, in_=in_), reads, writes)

    def memset(eng, ap, val, writes):
        P.op(eng, lambda e: e.memset(ap, val), (), writes)

    def blkname(b):
        return "hT_b%d" % b

    def wcols(l, c0, n):
        return w_in_d[l].rearrange("(kc p) n -> p kc n", p=128)[:, :, c0:c0 + n]

    def load_w(ap_sb, dram_ap, name):
        P.dma("pool", ap_sb, dram_ap, writes=[name])

    P.dma("sp", ident[:, :], ident_d, writes=["ident"])
    P.dma("sp", masks[:, :, :], masks_d, writes=["masks"])
    memset("pool", onesm[:, :], 1.0 / 256.0, ["onesm"])
    memset("pool", epsc[:, :], NORM_EPS, ["epsc"])

    def load_gb(idx):
        src = bass.AP(gvec_h, idx * D, [[0, 128], [1, D]])
        P.dma("sp", gb[:, :], src, writes=["gb"])

    stat = nc.alloc_sbuf_tensor("stat4", [128, 16], F32)

    def norm_s1a(xn, xn_name, tile, rots):
        sl_ = tile % 4
        st = stat[:, sl_ * 4:sl_ * 4 + 4]
        sn = ["stat%d_%d" % (sl_, k) for k in range(4)]
        junk, jn = rots["junk"].next()
        memset("pool", st[:, 0:1], 0.0, [sn[0]])
        act(junk, xn, AF.Square, [xn_name], [jn, sn[0]], accum=st[:, 0:1])
        act(st[:, 2:3], st[:, 0:1], AF.Sqrt, [sn[0], "epsc"], [sn[2]], bias=epsc[:, 0:1], scale=1.0 / D)

    def norm_s1b(xn, xn_name, tile, rots, last):
        sl_ = tile % 4
        st = stat[:, sl_ * 4:sl_ * 4 + 4]
        sn = ["stat%d_%d" % (sl_, k) for k in range(4)]
        P.op("dve", lambda e: e.reciprocal(out=st[:, 3:4], in_=st[:, 2:3]), [sn[2]], [sn[3]])
        if last:
            o, on = rots["ofin"].next()
            stt("dve", o, xn, st[:, 3:4], gb[:, :], ALU.mult, ALU.mult, [xn_name, sn[3], "gb"], [on])
            P.dma("pool", out_d[tile * 128:(tile + 1) * 128, :], o, reads=[on])
            return None
        h, hn = rots["h"].next()
        stt("dve", h, xn, st[:, 3:4], gb[:, :], ALU.mult, ALU.mult, [xn_name, sn[3], "gb"], [hn])
        return (h, hn)

    def norm_stage2(hh, tile):
        h, hn = hh
        bk, bn = pb()
        bk16 = banks16[int(bn[4:])]
        for c in range(8):
            P.op("pe", lambda e, c=c: e.transpose(bk16[:, c * 128:(c + 1) * 128], h[:, c * 128:(c + 1) * 128], ident[:, :]),
                 [hn, "ident"], [bn])
        src = bk16[:, :].rearrange("p (c t) -> p c t", c=8)
        cp("act", hT[:, :, tile * 128:(tile + 1) * 128], src, [bn], [blkname(tile // 4)])

    NLOOK = 2

    P.barrier()
    arena.reset()
    rots = {"x": Rot("x", 4, [D], F32), "junk": Rot("junk", 2, [D], BF16), "h": Rot("h", 4, [D], BF16)}
    load_gb(0)
    pend = {}
    xts = {}
    for idx in range(NT + 3):
        if idx < NT:
            xt, xnm = rots["x"].next()
            P.dma("sp", xt, x_d[idx * 128:(idx + 1) * 128, :], writes=[xnm])
            xts[idx] = (xt, xnm)
            norm_s1a(xt, xnm, idx, rots)
        if 0 <= idx - 1 < NT:
            xt, xnm = xts.pop(idx - 1)
            pend[idx - 1] = norm_s1b(xt, xnm, idx - 1, rots, False)
        if idx - 3 >= 0:
            norm_stage2(pend.pop(idx - 3), idx - 3)
    if dbg:
        P.dma("sp", hdbg_d, hT[:, :, :], reads=[blkname(b) for b in range(NBLK)])

    if stop == 'A0':
        P.emit()
        return P
    for l in range(n_layers):
        lastl = (l == n_layers - 1)
        P.barrier()
        P.dma("sp", vecs[:, :], vecs_d[l], writes=["vecs"])

        def gate_tile(Wg, Wgn, cc, blk, gr):
            bk, bn = pb()
            for kc in range(8):
                mm(bk[:, :], Wg[:, kc, cc * 128:(cc + 1) * 128], hT[:, kc, blk * 512:(blk + 1) * 512],
                   kc == 0, kc == 7, [Wgn, blkname(blk)], [bn])
            g, gn = gr.next()
            act(g, bk[:, :], AF.Silu, [bn], [gn])
            return g, gn

        P.barrier()
        arena.reset()
        qT = arena.alloc([S], BF16)
        kT = arena.alloc([S + 2 * KPAD], BF16)
        vb = arena.alloc([48, 256], BF16)
        acc = arena.alloc([2, S], F32)
        Wqk_r = Rot("Wqk", 2, [8, 2, 128], BF16)
        Wv_r = Rot("Wv", 2, [8, 128], BF16)
        WgC = arena.alloc([8, 256], BF16)
        rope_r = Rot("rope", 2, [2, 512], F32)
        rt_r = Rot("rt", 4, [512], F32)
        qs_r = Rot("qs", 2, [2, 512], F32)
        pT_r = Rot("pT", 5, [512], BF16)
        rd_r = Rot("rd", 2, [512], F32)
        g_r = Rot("g", 2, [512], F32)
        st_r = Rot("st", 2, [512], BF16)
        load_w(WgC, wcols(l, 3328 + 2 * 256, 256), "WgC")
        memset("pool", kT[:, 0:KPAD], 0.0, ["kTpadL"])
        memset("pool", kT[:, KPAD + S:KPAD + S + KPAD], 0.0, ["kTpadR"])
        memset("pool", vb[:, :, :], 0.0, ["vb"])
        memset("pool", vb[:, :, 64:192], 1.0, ["vb"])
        def finalize(hp):
            for blk in range(NBLK):
                sl = slice(blk * 512, (blk + 1) * 512)
                an = "acc_b%d" % blk
                rd, rdn = rd_r.next()
                act(rd[0:64, :], acc[64:128, 0, sl], AF.Ln, [an], [rdn])
                act(rd[64:128, :], acc[0:64, 1, sl], AF.Ln, [an], [rdn])
                act(rd, rd, AF.Exp, [rdn], [rdn], scale=-1.0)
                tt("pool", acc[0:64, 0, sl], acc[0:64, 0, sl], rd[0:64, :], ALU.mult, [an, rdn], [an])
                tt("pool", acc[64:128, 0, sl], acc[64:128, 1, sl], rd[64:128, :], ALU.mult, [an, rdn], [an])
            for blk in range(NBLK):
                sl = slice(blk * 512, (blk + 1) * 512)
                gt, gtn = gate_tile(WgC, "WgC", hp, blk, g_r)
                stg, stn = st_r.next()
                tt("dve", stg, acc[:, 0, sl], gt, ALU.mult, ["acc_b%d" % blk, gtn], [stn])
                P.dma("pool", yg_d[512 + hp * 128:512 + hp * 128 + 128, sl], stg, reads=[stn], writes=["yg_%d_%d" % (4 + hp, blk)])
        fin_pending = []
        for hp in range(2):
            for g in range(3):
                dil = DIL[g]
                Lg = S // dil
                ntl = Lg // 128
                nch = ntl + 1
                Wqk, Wqkn = Wqk_r.next()
                Wv, Wvn = Wv_r.next()
                qc0 = 768 + g * 256 + hp * 128
                kc0 = 1536 + g * 256 + hp * 128
                vc0 = 2304 + g * 256 + hp * 128
                load_w(Wqk[:, :, 0, :], wcols(l, qc0, 128), Wqkn + "_0")
                load_w(Wqk[:, :, 1, :], wcols(l, kc0, 128), Wqkn + "_2")
                load_w(Wv, wcols(l, vc0, 128), Wvn)
                for blk in range(NBLK):
                    rp, rpn = rope_r.next()
                    P.dma("sp", rp[:, 0, :], ropec_d[:, blk * 512:(blk + 1) * 512], writes=[rpn + "c"])
                    P.dma("sp", rp[:, 1, :], ropes_d[:, blk * 512:(blk + 1) * 512], writes=[rpn + "s"])
                    (bq, bqn), (bkk, bkn), pair = pb_pair()
                    for qk, (bk, bn) in enumerate(((bq, bqn), (bkk, bkn))):
                        for kc in range(8):
                            mm(bk[:, :], Wqk[:, kc, qk, :], hT[:, kc, blk * 512:(blk + 1) * 512],
                               kc == 0, kc == 7, [Wqkn + "_%d" % (2 * qk), blkname(blk)], [bn])
                    qs, qsn = qs_r.next()
                    for (src, dst) in ((32, 0), (0, 32), (96, 64), (64, 96)):
                        cp("act", qs[dst:dst + 32, :, :], pair[src:src + 32, :, :], [bqn, bkn], [qsn])
                    for qk, (bk, bn) in enumerate(((bq, bqn), (bkk, bkn))):
                        t1, t1n = rt_r.next()
                        t2, t2n = rt_r.next()
                        tt("dve", t1, bk[:, :], rp[:, 0, :], ALU.mult, [bn, rpn + "c"], [t1n])
                        tt("dve", t2, qs[:, qk, :], rp[:, 1, :], ALU.mult, [qsn, rpn + "s"], [t2n])
                        if qk == 0:
                            tt("pool", qT[:, blk * 512:(blk + 1) * 512], t1, t2, ALU.add, [t1n, t2n], ["qT_b%d" % blk])
                        else:
                            tt("dve", kT[:, KPAD + blk * 512:KPAD + (blk + 1) * 512], t1, t2, ALU.add, [t1n, t2n], ["kT_b%d" % blk])
                if stop == 'C1':
                    P.emit()
                    return P
                allq = ["qT_b%d" % b for b in range(NBLK)]
                allk = ["kT_b%d" % b for b in range(NBLK)] + ["kTpadL", "kTpadR"]
                allh = [blkname(b) for b in range(NBLK)]
                for r in range(dil):
                    for ci in range(nch):
                        ch = r * nch + ci
                        p0 = ci * 128 - 64
                        lo = max(p0, 0)
                        hi = min(p0 + 128, Lg)
                        npos = hi - lo
                        prow = lo - p0
                        t0 = r + dil * lo
                        bk, bn = pb()
                        for kc in range(8):
                            lhsT = hT[:, kc, t0:t0 + dil * (npos - 1) + 1:dil]
                            if prow == 0:
                                mm(bk[0:npos, 0:128], lhsT, Wv[:, kc, :], kc == 0, kc == 7, [Wvn] + allh, [bn])
                            else:
                                mm(bk[64:128, 0:128], lhsT, Wv[:, kc, :], kc == 0, kc == 7, [Wvn] + allh, [bn], tp=(0, 64))
                        dst = bass.AP(vb.tensor, vb.offset + prow * vb.ap[0][0] + ch * 256, [[vb.ap[0][0], npos], [192, 2], [1, 64]])
                        srcv = bk[prow:prow + npos, 0:128].rearrange("p (h d) -> p h d", h=2)
                        cp("act", dst, srcv, [bn], ["vb"])
                if stop == 'C2':
                    P.emit()
                    return P
                if g == 0 and fin_pending:
                    finalize(fin_pending.pop())
                tiles = [(r, ti) for r in range(dil) for ti in range(ntl)]
                LOOK = 3
                staged = {}

                def stage_a(idx):
                    r, ti = tiles[idx]
                    variant = 1 if ti == 0 else (2 if ti == ntl - 1 else 0)
                    (s0, s0n), (s1, s1n), spair = pb_pair()
                    qa = r + dil * 128 * ti
                    for kc in range(2):
                        ka = KPAD + r + dil * (128 * ti - 64 + 128 * kc)
                        for hh, (sb_, sbn_) in enumerate(((s0, s0n), (s1, s1n))):
                            rows = slice(hh * 64, hh * 64 + 64)
                            mm(sb_[:, kc * 128:(kc + 1) * 128], kT[rows, ka:ka + dil * 127 + 1:dil],
                               qT[rows, qa:qa + dil * 127 + 1:dil], True, True, allq + allk, [sbn_])
                    pT, pTn = pT_r.next()
                    act(pT.rearrange("p (h c) -> p h c", h=2), spair[:, :, 0:256], AF.Exp, [s0n, s1n], [pTn], scale=0.125)
                    tt("dve", pT, pT, masks[:, variant, :], ALU.mult, [pTn, "masks"], [pTn])
                    staged[idx] = (pT, pTn)

                def stage_b(idx):
                    r, ti = tiles[idx]
                    pT, pTn = staged.pop(idx)
                    obk, obn = pb()
                    for hh in range(2):
                        for kc in range(2):
                            ch = r * nch + ti + kc
                            bi = 2 * hh + kc
                            mm(obk[:, hh * 128:(hh + 1) * 128], vb[:, ch, hh * 128:(hh + 1) * 128],
                               pT[:, bi * 128:(bi + 1) * 128], kc == 0, kc == 1, ["vb", pTn], [obn])
                    qa = r + dil * 128 * ti
                    dst = acc[:, :, qa:qa + dil * 127 + 1:dil]
                    srco = obk[:, 0:256].rearrange("p (h t) -> p h t", h=2)
                    bset = sorted(set([(qa) // 512, (qa + dil * 127) // 512]))
                    accn = ["acc_b%d" % b for b in range(bset[0], bset[-1] + 1)]
                    if g == 0:
                        cp("dve", dst, srco, [obn], accn)
                    else:
                        tt("dve", dst, srco, dst, ALU.add, [obn] + accn, accn)

                for idx in range(len(tiles) + LOOK):
                    if idx < len(tiles):
                        stage_a(idx)
                    if idx - LOOK >= 0:
                        stage_b(idx - LOOK)
                if g == 2:
                    fin_pending.append(hp)
        while fin_pending:
            finalize(fin_pending.pop())
        if stop == 'C':
            P.emit()
            return P
        P.barrier()
        arena.reset()
        Wa = arena.alloc([8, 256], BF16)
        WgA = arena.alloc([8, 256], BF16)
        cs = arena.alloc([2, 512], BF16)
        wf = arena.alloc([2, 256], BF16)
        uaT = arena.alloc([2, S], BF16)
        PQ = arena.alloc([32, 512], BF16)
        fT = arena.alloc([2, S], BF16)
        dft_r = Rot("dft", 3, [2, 2, 512], BF16)
        sq_r = Rot("sq", 2, [512], F32)
        dmid = arena.alloc([32], BF16)
        P.dma("sp", dmid, dmid_d, writes=["dmid"])
        g_r = Rot("g", 2, [512], F32)
        st_r = Rot("st", 2, [512], BF16)
        load_w(Wa, wcols(l, 0, 256), "Wa")
        load_w(WgA, wcols(l, 3328, 256), "WgA")
        load_w(wf, w_fourier_d[l].rearrange("(c p) n -> p c n", p=128), "wf")
        P.dma("sp", cs, cs_d, writes=["cs"])
        for blk in range(NBLK):
            for c in range(2):
                bk, bn = pb()
                for kc in range(8):
                    mm(bk[:, :], Wa[:, kc, c * 128:(c + 1) * 128], hT[:, kc, blk * 512:(blk + 1) * 512],
                       kc == 0, kc == 7, ["Wa", blkname(blk)], [bn])
                cp("act" if c == 0 else "dve", uaT[:, c, blk * 512:(blk + 1) * 512], bk[:, :], [bn], ["uaT_b%d" % blk])
        for a in range(NT):
            bk, bn = pb()
            for c in range(2):
                mm(bk[:, :], uaT[:, c, a * 128:(a + 1) * 128], cs[:, c, :], c == 0, c == 1, ["uaT_b%d" % (a // 4), "cs"], [bn])
            cp("act" if a % 2 == 0 else "dve", PQ[:, a, :], bk[:, :], [bn], ["PQ_%d" % a])
        for ps_ in range(4):
            bC = [pb() for c in range(2)]
            bS = [pb() for c in range(2)]
            for a2 in range(NT // 2):
                dt_, dtn = dft_r.next()
                P.dma("sp", dt_, dft_d[ps_, a2], writes=[dtn])
                for ai in range(2):
                    a = 2 * a2 + ai
                    for c in range(2):
                        mm(bC[c][0][:, :], PQ[:, a, c * 128:(c + 1) * 128], dt_[:, ai, 0, :],
                           a == 0, a == NT - 1, ["PQ_%d" % a, dtn], [bC[c][1]])
                        mm(bS[c][0][:, :], PQ[:, a, 256 + c * 128:256 + (c + 1) * 128], dt_[:, ai, 1, :],
                           a == 0, a == NT - 1, ["PQ_%d" % a, dtn], [bS[c][1]])
            for c in range(2):
                sq, sqn = sq_r.next()
                cp("act", sq, bS[c][0][:, :], [bS[c][1]], [sqn])
                k0 = ps_ * 512
                tt("dve", fT[:, c, k0:k0 + 512], bC[c][0][:, :], sq, ALU.subtract, [bC[c][1], sqn], ["fT_b%d" % ps_])
                j0 = 1 if ps_ == 0 else 0
                fv = fT[:, c, :]
                rev = bass.AP(fv.tensor, fv.offset + (S - k0 - j0), [[fv.ap[0][0], 128], [-1, 512 - j0]])
                hin = ["fT_b%d" % (7 - ps_)] + (["fT_b%d" % (8 - ps_)] if ps_ >= 1 else [])
                tt("dve", rev, bC[c][0][:, j0:512], sq[:, j0:512], ALU.add, [bC[c][1], sqn], hin)
        bm_, bmn_ = pb()
        for c in range(2):
            for a in range(NT):
                mm(bm_[:, c:c + 1], PQ[:, a, c * 128:(c + 1) * 128], dmid[:, a:a + 1], a == 0, a == NT - 1, ["PQ_%d" % a, "dmid"], [bmn_])
        for c in range(2):
            cp("act", fT[:, c, S // 2:S // 2 + 1], bm_[:, c:c + 1], [bmn_], ["fT_b4"])
        for blk in range(NBLK):
            sl = slice(blk * 512, (blk + 1) * 512)
            for co in range(2):
                bk, bn = pb()
                for c in range(2):
                    mm(bk[:, :], wf[:, c, co * 128:(co + 1) * 128], fT[:, c, sl], c == 0, c == 1, ["wf", "fT_b%d" % blk], [bn])
                gt, gtn = gate_tile(WgA, "WgA", co, blk, g_r)
                stg, stn = st_r.next()
                tt("dve", stg, bk[:, :], gt, ALU.mult, [bn, gtn], [stn])
                P.dma("pool", yg_d[co * 128:(co + 1) * 128, sl], stg, reads=[stn], writes=["yg_%d_%d" % (co, blk)])

        if stop == 'A':
            P.emit()
            return P
        P.barrier()
        arena.reset()
        Wb = arena.alloc([8, 512], BF16)
        WgB = arena.alloc([8, 256], BF16)
        wpw = arena.alloc([2, 256], BF16)
        dg = arena.alloc([2, 31, 128], BF16)
        uT = arena.alloc([2, S + 32], BF16)
        sg_r = Rot("sg", 2, [512], F32)
        y_r = Rot("y", 3, [2, 512], F32)
        ysq_r = Rot("ysq", 3, [2, 512], F32)
        tmp_r = Rot("ctmp", 8, [512], F32)
        s_r = Rot("s", 3, [2, 512], BF16)
        g_r = Rot("g", 2, [512], F32)
        st_r = Rot("st", 2, [512], BF16)
        load_w(Wb, wcols(l, 256, 512), "Wb")
        load_w(WgB, wcols(l, 3328 + 256, 256), "WgB")
        load_w(wpw, w_pw_d[l].rearrange("(c p) n -> p c n", p=128), "wpw")
        for c in range(2):
            for k in range(31):
                ts("dve", dg[:, c, k, :], ident[:, :], vecs[:, V_CW + c * 31 + k:V_CW + c * 31 + k + 1], None, ALU.mult, None,
                   ["ident", "vecs"], ["dg_%d_%d" % (c, k)])
        memset("pool", uT[:, :, 0:16], 0.0, ["uTpadL"])
        memset("pool", uT[:, :, 16 + S:32 + S], 0.0, ["uTpadR"])
        for blk in range(NBLK):
            for c in range(2):
                bka, bna = pb()
                bkg, bng = pb()
                for kc in range(8):
                    mm(bka[:, :], Wb[:, kc, c * 128:(c + 1) * 128], hT[:, kc, blk * 512:(blk + 1) * 512],
                       kc == 0, kc == 7, ["Wb", blkname(blk)], [bna])
                for kc in range(8):
                    mm(bkg[:, :], Wb[:, kc, 256 + c * 128:256 + (c + 1) * 128], hT[:, kc, blk * 512:(blk + 1) * 512],
                       kc == 0, kc == 7, ["Wb", blkname(blk)], [bng])
                sg, sgn = sg_r.next()
                act(sg, bkg[:, :], AF.Sigmoid, [bng], [sgn])
                tt("dve", uT[:, c, 16 + blk * 512:16 + (blk + 1) * 512], bka[:, :], sg, ALU.mult, [bna, sgn], ["uT_b%d" % blk])
        cst = {}

        def b_s1(blk):
            urd = ["uT_b%d" % b for b in range(max(blk - 1, 0), min(blk + 1, NBLK - 1) + 1)] + ["uTpadL", "uTpadR"]
            y, yn = y_r.next()
            ysq, ysqn = ysq_r.next()
            for c in range(2):
                bk, bn = pb()
                for k in range(31):
                    o = 16 + blk * 512 + k - 15
                    mm(bk[:, :], dg[:, c, k, :], uT[:, c, o:o + 512], k == 0, k == 30, ["dg_%d_%d" % (c, k)] + urd, [bn])
                act(y[:, c, :], bk[:, :], AF.Identity, [bn, "vecs"], [yn], bias=vecs[:, V_CB + c:V_CB + c + 1])
                act(ysq[:, c, :], bk[:, :], AF.Square, [bn, "vecs"], [ysqn], bias=vecs[:, V_CB + c:V_CB + c + 1])
            cst[blk] = [y, yn, ysq, ysqn]

        def b_s2(blk):
            y, yn, ysq, ysqn = cst[blk]
            bm, bmn = pb()
            bs, bsn = pb()
            for c in range(2):
                mm(bm[:, :], onesm[:, :], y[:, c, :], c == 0, c == 1, ["onesm", yn], [bmn])
            for c in range(2):
                mm(bs[:, :], onesm[:, :], ysq[:, c, :], c == 0, c == 1, ["onesm", ysqn], [bsn])
            msq, msqn = tmp_r.next()
            act(msq, bm[:, :], AF.Square, [bmn], [msqn])
            var, varn = tmp_r.next()
            tt("dve", var, bs[:, :], msq, ALU.subtract, [bsn, msqn], [varn])
            ts("dve", var, var, LN_EPS, None, ALU.add, None, [varn], [varn])
            act(var, var, AF.Sqrt, [varn], [varn])
            P.op("dve", lambda e, var=var: e.reciprocal(out=var, in_=var), [varn], [varn])
            s_, sn = s_r.next()
            for c in range(2):
                d_, dn = tmp_r.next()
                tt("dve", d_, y[:, c, :], bm[:, :], ALU.subtract, [yn, bmn], [dn])
                tt("pool", d_, d_, var, ALU.mult, [dn, varn], [dn])
                act(s_[:, c, :], d_, AF.Silu, [dn, "vecs"], [sn], bias=vecs[:, V_LB + c:V_LB + c + 1], scale=vecs[:, V_LG + c:V_LG + c + 1])
            cst[blk] = [s_, sn]

        def b_s3(blk):
            sl = slice(blk * 512, (blk + 1) * 512)
            s_, sn = cst.pop(blk)
            for co in range(2):
                bk, bn = pb()
                for c in range(2):
                    mm(bk[:, :], wpw[:, c, co * 128:(co + 1) * 128], s_[:, c, :], c == 0, c == 1, ["wpw", sn], [bn])
                gt, gtn = gate_tile(WgB, "WgB", co, blk, g_r)
                stg, stn = st_r.next()
                tt("dve", stg, bk[:, :], gt, ALU.mult, [bn, gtn], [stn])
                P.dma("pool", yg_d[256 + co * 128:256 + (co + 1) * 128, sl], stg, reads=[stn], writes=["yg_%d_%d" % (2 + co, blk)])

        for i in range(NBLK + 2):
            if i < NBLK:
                b_s1(i)
            if 0 <= i - 1 < NBLK:
                b_s2(i - 1)
            if i - 2 >= 0:
                b_s3(i - 2)

        if stop == 'B':
            P.emit()
            return P
        P.barrier()
        arena.reset()
        Wd = arena.alloc([8, 256], BF16)
        WgD = arena.alloc([8, 256], BF16)
        wpb = arena.alloc([2, 128], BF16)
        ud = arena.alloc([2, S + 16], F32)
        bA = arena.alloc([S + 16], F32)
        bB = arena.alloc([S + 16], F32)
        pooled = arena.alloc([2, S], BF16)
        edge = arena.alloc([2, 16], F32)
        etmp = arena.alloc([16], F32)
        po_r = Rot("po", 2, [512], F32)
        g_r = Rot("g", 2, [512], F32)
        st_r = Rot("st", 2, [512], BF16)
        load_w(Wd, wcols(l, 3072, 256), "Wd")
        load_w(WgD, wcols(l, 3328 + 768, 256), "WgD")
        memset("pool", wpb[:, :, :], 0.0, ["wpb"])
        for gi in range(4):
            r0 = (gi % 2) * 64
            P.dma("pool", wpb[r0:r0 + 64, gi // 2, r0:r0 + 64], w_pool_d[l, gi], reads=["wpb"], writes=["wpb_%d" % gi])
        P.dma("sp", edge, edge_d, writes=["edge"])
        memset("pool", ud[:, :, 0:8], 0.0, ["udpadL"])
        memset("pool", ud[:, :, 8 + S:16 + S], 0.0, ["udpadR"])
        for c in range(2):
            for blk in range(NBLK):
                bk, bn = pb()
                for kc in range(8):
                    mm(bk[:, :], Wd[:, kc, c * 128:(c + 1) * 128], hT[:, kc, blk * 512:(blk + 1) * 512],
                       kc == 0, kc == 7, ["Wd", blkname(blk)], [bn])
                cp("act", ud[:, c, 8 + blk * 512:8 + (blk + 1) * 512], bk[:, :], [bn], ["ud_%d" % c])
        NP_ = S + 16
        for c in range(2):
            u = ud[:, c, :]
            tt("dve", bA[:, 1:NP_], u[:, 0:NP_ - 1], u[:, 1:NP_], ALU.add, ["ud_%d" % c, "udpadL", "udpadR"], ["bA"])
            tt("dve", bB[:, 2:NP_ - 1], bA[:, 1:NP_ - 2], bA[:, 3:NP_], ALU.add, ["bA"], ["bB"])
            if c == 1:
                tt("dve", bA[:, 4:NP_ - 4], bB[:, 2:NP_ - 6], bB[:, 6:NP_ - 2], ALU.add, ["bB"], ["bA"])
                tt("dve", bB[:, 8:NP_ - 8], bA[:, 4:NP_ - 12], bA[:, 12:NP_ - 4], ALU.add, ["bA"], ["bB"])
            for half in range(2):
                rows = slice(half * 64, half * 64 + 64)
                W_ = bA if half == 0 else bB
                stt("dve", pooled[rows, c, :], W_[rows, 8:8 + S], vecs[rows, V_IS + c:V_IS + c + 1], u[rows, 8:8 + S],
                    ALU.mult, ALU.subtract, ["bA", "bB", "ud_%d" % c, "vecs"], ["pooled_%d" % c])
                for e0, t0 in ((0, 0), (8, S - 8)):
                    tt("dve", etmp[rows, e0:e0 + 8], W_[rows, 8 + t0:16 + t0], edge[rows, c, e0:e0 + 8], ALU.mult,
                       ["bA", "bB", "edge"], ["etmp"])
                    tt("dve", pooled[rows, c, t0:t0 + 8], etmp[rows, e0:e0 + 8], u[rows, 8 + t0:16 + t0], ALU.subtract,
                       ["etmp", "ud_%d" % c], ["pooled_%d" % c])
        for co in range(2):
            for blk in range(NBLK):
                sl = slice(blk * 512, (blk + 1) * 512)
                bk, bn = pb()
                mm(bk[:, :], wpb[:, co, :], pooled[:, co, sl], True, True, ["wpb", "wpb_0", "wpb_1", "wpb_2", "wpb_3", "pooled_%d" % co], [bn])
                gt, gtn = gate_tile(WgD, "WgD", co, blk, g_r)
                stg, stn = st_r.next()
                po, pon = po_r.next()
                act(po, bk[:, :], AF.Copy, [bn, "vecs"], [pon], scale=vecs[:, V_PS + co:V_PS + co + 1])
                tt("pool", stg, po, gt, ALU.mult, [pon, gtn], [stn])
                P.dma("pool", yg_d[768 + co * 128:768 + (co + 1) * 128, sl], stg, reads=[stn], writes=["yg_%d_%d" % (6 + co, blk)])

        if stop == 'D':
            P.emit()
            return P
        P.barrier()
        arena.reset()
        ygT = arena.alloc([8, S], BF16)
        Wgt_r = Rot("Wgt", 3, [4, 8, 128], BF16)
        Wbr_r = Rot("Wbr", 3, [4, 2, 128], BF16)
        mst_r = Rot("mst", 2, [S], BF16)
        mg_r = Rot("mg", 2, [512], F32)
        mt_r = Rot("mtmp", 3, [512], F32)
        mac_r = Rot("macc", 2, [512], F32)
        for c8 in range(8):
            P.dma("sp", ygT[:, c8, :], yg_d[c8 * 128:(c8 + 1) * 128, :],
                  reads=["yg_%d_%d" % (c8, b) for b in range(NBLK)], writes=["ygT_%d" % c8])
        wslots = {}

        def m_load(j):
            Wgt, Wgtn = Wgt_r.next()
            Wbr, Wbrn = Wbr_r.next()
            for n in range(4):
                load_w(Wgt[:, n, :, :], w_gate_d[l, n].rearrange("(kc p) m -> p kc m", p=128)[:, :, j * 128:(j + 1) * 128], Wgtn + "_%d" % n)
                load_w(Wbr[:, n, :, :], w_branch_d[l, n].rearrange("(c p) m -> p c m", p=128)[:, :, j * 128:(j + 1) * 128], Wbrn + "_%d" % n)
            wslots[j] = (Wgt, Wgtn, Wbr, Wbrn)

        m_load(0)
        for j in range(8):
            if j + 1 < 8:
                m_load(j + 1)
            Wgt, Wgtn, Wbr, Wbrn = wslots.pop(j)
            mst, mstn = mst_r.next()
            for blk in range(NBLK):
                sl = slice(blk * 512, (blk + 1) * 512)
                macc, maccn = mac_r.next()
                for n in range(4):
                    bg, bgn = pb()
                    by, byn = pb()
                    for kc in range(8):
                        mm(bg[:, :], Wgt[:, n, kc, :], hT[:, kc, sl], kc == 0, kc == 7, [Wgtn + "_%d" % n, blkname(blk)], [bgn])
                    for c in range(2):
                        mm(by[:, :], Wbr[:, n, c, :], ygT[:, 2 * n + c, sl], c == 0, c == 1, [Wbrn + "_%d" % n, "ygT_%d" % (2 * n + c)], [byn])
                    mg, mgn = mg_r.next()
                    act(mg, bg[:, :], AF.Sigmoid, [bgn, "vecs"], [mgn], bias=vecs[:, V_BG + n * 8 + j:V_BG + n * 8 + j + 1])
                    if n == 0:
                        tt("dve", macc, by[:, :], mg, ALU.mult, [byn, mgn], [maccn])
                    else:
                        tmp, tmpn = mt_r.next()
                        tt("dve", tmp, by[:, :], mg, ALU.mult, [byn, mgn], [tmpn])
                        if n < 3:
                            tt("pool", macc, macc, tmp, ALU.add, [maccn, tmpn], [maccn])
                        else:
                            tt("pool", mst[:, sl], macc, tmp, ALU.add, [maccn, tmpn], [mstn])
            P.dma("pool", mT_d[j * 128:(j + 1) * 128, :], mst, reads=[mstn], writes=["mT_%d" % j])

        if stop == 'M':
            P.emit()
            return P
        P.barrier()
        arena.reset()
        wo = arena.alloc([8, D], BF16)
        mTb_r = Rot("mTb", 2, [8, 512], BF16)
        rots = {"x": Rot("x", 3, [D], F32), "xn": Rot("xn", 4, [D], F32), "junk": Rot("junk", 2, [D], BF16),
                "h": Rot("h", 4, [D], BF16), "ofin": Rot("ofin", 2, [D], F32)}
        load_w(wo, w_out_d[l].rearrange("(jc p) m -> p jc m", p=128), "wo")
        load_gb(l + 1)
        xsrc = x_d if l == 0 else xmid_d
        pend = {}
        cur = {}

        def e_stage1(tile):
            blk, t4 = tile // 4, tile % 4
            if t4 == 0:
                mTb, mTbn = mTb_r.next()
                P.dma("sp", mTb, mT_d.rearrange("(jc p) t -> p jc t", p=128)[:, :, blk * 512:(blk + 1) * 512],
                      reads=["mT_%d" % j for j in range(8)], writes=[mTbn])
                cur["m"] = (mTb, mTbn)
            mTb, mTbn = cur["m"]
            xt, xnm = rots["x"].next()
            rd_ = ["xmid_%d" % tile] if l > 0 else []
            P.dma("sp", xt, xsrc[tile * 128:(tile + 1) * 128, :], reads=rd_, writes=[xnm])
            hb = []
            for half in range(2):
                bk, bn = pb()
                for jc in range(8):
                    mm(bk[:, :], mTb[:, jc, t4 * 128:(t4 + 1) * 128], wo[:, jc, half * 512:(half + 1) * 512],
                       jc == 0, jc == 7, [mTbn, "wo"], [bn])
                hb.append((bk, bn))
            xn, xnn = rots["xn"].next()
            for half in range(2):
                tt("dve", xn[:, half * 512:(half + 1) * 512], xt[:, half * 512:(half + 1) * 512], hb[half][0][:, :], ALU.add,
                   [xnm, hb[half][1]], [xnn])
            if (not lastl) or dbg:
                P.dma("pool", xmid_d[tile * 128:(tile + 1) * 128, :], xn, reads=[xnn], writes=["xmid_%d" % tile])
            norm_s1a(xn, xnn, tile, rots)
            return (xn, xnn)

        xns = {}
        for idx in range(NT + 3):
            if idx < NT:
                xns[idx] = e_stage1(idx)
            if 0 <= idx - 1 < NT:
                xn, xnn = xns.pop(idx - 1)
                pend[idx - 1] = norm_s1b(xn, xnn, idx - 1, rots, lastl)
            if idx - 3 >= 0:
                hh = pend.pop(idx - 3)
                if hh is not None:
                    norm_stage2(hh, idx - 3)
    P.emit()
    return P

import ml_dtypes as _mld

_BF = _mld.bfloat16
_CONST_CACHE = {}


def _constants():
    if _CONST_CACHE:
        return _CONST_CACHE
    c = {}
    c["ident"] = np.eye(128, dtype=np.float32).astype(_BF)
    p = np.arange(128)[:, None]
    n = np.arange(128)[None, :]
    M0 = (n <= p).astype(np.float32)
    M1 = (n >= p).astype(np.float32)
    M0f = M0 * (p >= 64)
    M1l = M1 * (p < 64)
    mk = np.zeros((128, 3, 512), np.float32)
    for v, (a, b) in enumerate(((M0, M1), (M0f, M1), (M0, M1l))):
        mk[:, v, :] = np.concatenate([a, b, a, b], axis=1)
    c["masks"] = mk.astype(_BF)
    inv = (1.0 / (np.float32(10000.0) ** (np.arange(0, 64, 2, dtype=np.float32) / np.float32(64)))).astype(np.float32)
    ang = (np.arange(S, dtype=np.float32)[None, :] * inv[:, None]).astype(np.float32)
    cosv = np.cos(ang).astype(np.float32)
    sinv = np.sin(ang).astype(np.float32)
    rc = np.zeros((128, S), np.float32)
    rs = np.zeros((128, S), np.float32)
    for q in range(4):
        rc[q * 32:(q + 1) * 32] = cosv
        rs[q * 32:(q + 1) * 32] = sinv if (q % 2 == 1) else -sinv
    c["ropec"] = rc
    c["ropes"] = rs
    s_idx = np.arange(S, dtype=np.int64)[:, None]
    k_idx = np.arange(S // 2, dtype=np.int64)[None, :]
    ph = ((s_idx * k_idx) % S).astype(np.float64) * (2.0 * np.pi / S)
    Cm = (np.cos(ph) / 512.0).astype(np.float32).reshape(16, 2, 128, 4, 512)
    Sm = (np.sin(ph) / 512.0).astype(np.float32).reshape(16, 2, 128, 4, 512)
    del ph
    dft = np.empty((4, 16, 128, 2, 2, 512), dtype=_BF)
    dft[:, :, :, :, 0, :] = Cm.transpose(3, 0, 2, 1, 4).astype(_BF)
    dft[:, :, :, :, 1, :] = Sm.transpose(3, 0, 2, 1, 4).astype(_BF)
    c["dft"] = dft
    sgn = np.where((np.arange(S) % 2) == 0, 1.0, -1.0).astype(np.float32) / 512.0
    c["dmid"] = np.ascontiguousarray(sgn.reshape(32, 128).T).astype(_BF)
    cm = np.arange(64)
    phc = ((cm[:, None] * cm[None, :]) % 64).astype(np.float64) * (2.0 * np.pi / 64)
    cs = np.zeros((256, 512), np.float32)
    for gi in range(4):
        cs[gi * 64:(gi + 1) * 64, gi * 64:(gi + 1) * 64] = np.cos(phc)
        cs[gi * 64:(gi + 1) * 64, 256 + gi * 64:256 + (gi + 1) * 64] = np.sin(phc)
    c["cs"] = np.ascontiguousarray(cs.reshape(2, 128, 512).transpose(1, 0, 2)).astype(_BF)
    sizes = (2, 4, 8, 16)
    inv_s = np.zeros((128, 2), np.float32)
    edge = np.zeros((128, 2, 16), np.float32)
    for gi, sz in enumerate(sizes):
        rows = slice((gi % 2) * 64, (gi % 2) * 64 + 64)
        cch = gi // 2
        inv_s[rows, cch] = 1.0 / sz
        for e in range(16):
            t = e if e < 8 else S - 16 + e
            lo = max(t - sz // 2, 0)
            hi = min(t + sz - 1 - sz // 2, S - 1)
            edge[rows, cch, e] = 1.0 / float(hi - lo + 1)
    c["inv_s"] = inv_s
    c["edge"] = edge
    perm = np.zeros(1536, np.int64)
    for j in range(1536):
        base = (j // 64) * 64
        d = j % 64
        perm[j] = base + (d + 32 if d < 32 else d - 32)
    c["perm"] = perm
    _CONST_CACHE.update(c)
    return c


def _prep_inputs(inp):
    c = _constants()
    f = lambda a: np.ascontiguousarray(np.asarray(a, dtype=np.float32))
    w_in = f(inp["w_in"])
    L = w_in.shape[0]
    vecs = np.zeros((L, 128, NV), np.float32)

    def pc(v):
        return f(v).reshape(L, 2, 128).transpose(0, 2, 1)

    vecs[:, :, V_CB:V_CB + 2] = pc(inp["conv_b"])
    vecs[:, :, V_LG:V_LG + 2] = pc(inp["conv_ln_g"])
    vecs[:, :, V_LB:V_LB + 2] = pc(inp["conv_ln_b"])
    vecs[:, :, V_PS:V_PS + 2] = pc(inp["pool_scale"])
    bg = f(inp["b_gate"]).reshape(L, 4, 8, 128).transpose(0, 3, 1, 2).reshape(L, 128, 32)
    vecs[:, :, V_BG:V_BG + 32] = bg
    vecs[:, :, V_IS:V_IS + 2] = c["inv_s"][None]
    cw = f(inp["conv_w"]).reshape(L, 31, 2, 128).transpose(0, 3, 2, 1).reshape(L, 128, 62)
    vecs[:, :, V_CW:V_CW + 62] = cw
    gvec = np.concatenate([f(inp["norm_g"]), f(inp["final_g"])[None, :]], axis=0)
    shared = {
        "w_in": w_in, "w_fourier": f(inp["w_fourier"]), "w_pw": f(inp["w_pw"]),
        "w_pool": f(inp["w_pool"]), "w_branch": f(inp["w_branch"]), "w_gate": f(inp["w_gate"]),
        "w_out": f(inp["w_out"]), "gvec": np.ascontiguousarray(gvec), "vecs": vecs,
        "ident": c["ident"], "masks": c["masks"], "ropec": c["ropec"], "ropes": c["ropes"],
        "dft": c["dft"], "dmid": c["dmid"], "cs": c["cs"], "edge": c["edge"],
    }
    return shared


_NC_CACHE = {}


def _get_nc(n_layers=2, dbg=False, stop=None):
    key = (n_layers, dbg, stop)
    if key not in _NC_CACHE:
        nc = bass.Bass("TRN2", target_bir_lowering=False)
        build_program(nc, n_layers=n_layers, dbg=dbg, stop=stop)
        _NC_CACHE[key] = nc
    return _NC_CACHE[key]


def kernel(**inputs):
    from concourse.bass_utils import run_bass_kernel_spmd
    x = np.ascontiguousarray(np.asarray(inputs["x"], dtype=np.float32))
    B = x.shape[0]
    shared = _prep_inputs(inputs)
    nc = _get_nc(2, False)
    in_maps = []
    for b in range(B):
        m = dict(shared)
        m["x"] = x[b]
        in_maps.append(m)
    res = run_bass_kernel_spmd(nc, in_maps, core_ids=list(range(B)))
    out = np.stack([np.asarray(r["out"], dtype=np.float32) for r in res.results], axis=0)
    return out
```

```python
import bisect
import numpy as np
import concourse.bass as bass
import concourse.mybir as mybir

F32 = mybir.dt.float32
BF16 = mybir.dt.bfloat16
ALU = mybir.AluOpType
AF = mybir.ActivationFunctionType
AX = mybir.AxisListType

ENGS = ("pe", "act", "dve", "pool", "sp")


class Buf:
    __slots__ = ("name", "last_w", "readers")

    def __init__(self, name):
        self.name = name
        self.last_w = []
        self.readers = {}


class Op:
    __slots__ = ("eng", "fn", "deps", "gidx", "eidx", "marked", "dma", "cum", "tag")

    def __init__(self, eng, fn, gidx):
        self.eng = eng
        self.fn = fn
        self.deps = []
        self.gidx = gidx
        self.eidx = -1
        self.marked = False
        self.dma = None
        self.tag = None
        self.cum = 0


class Prog:
    def __init__(self, nc, n_dma_sems_sp=24, n_dma_sems_pool=16):
        self.nc = nc
        self.ops = []
        self.eops = {e: [] for e in ENGS}
        self.ndma = {"sp": n_dma_sems_sp, "pool": n_dma_sems_pool, "act": 16}
        self.dma_rr = {"sp": 0, "pool": 0, "act": 0}
        self.dma_cnt = {}
        self.dma_last = {}
        self.bufs = {}
        self.epoch = []
        import os as _os
        self.tagging = bool(_os.environ.get('KTAG'))

    def buf(self, name):
        b = self.bufs.get(name)
        if b is None:
            b = Buf(name)
            b.last_w = list(self.epoch)
            self.bufs[name] = b
        return b

    def barrier(self):
        toks = []
        for e in ENGS:
            if self.eops[e]:
                o = self.eops[e][-1]
                if o.dma is None:
                    toks.append(("op", o))
                else:
                    for q in reversed(self.eops[e]):
                        if q.dma is None:
                            toks.append(("op", q))
                            break
        for key, t in self.dma_last.items():
            toks.append(t)
        self.epoch = toks
        for b in self.bufs.values():
            b.last_w = list(toks)
            b.readers = {}

    def _mkop(self, eng, fn, reads, writes):
        op = Op(eng, fn, len(self.ops))
        if self.tagging:
            import sys as _sys
            fr = _sys._getframe(2)
            while fr is not None and fr.f_code.co_name not in ("build_program", "norm_tile", "gate_tile"):
                fr = fr.f_back
            op.tag = "L%d" % fr.f_lineno if fr is not None else "?"
        op.eidx = len(self.eops[eng])
        self.ops.append(op)
        self.eops[eng].append(op)
        deps = op.deps
        for b in reads:
            for t in b.last_w:
                deps.append((t, "RAW"))
            if b.name.startswith("bank"):
                for k, t in b.readers.items():
                    if k != eng:
                        deps.append((t, "RAW"))
        for b in writes:
            for t in b.last_w:
                deps.append((t, "WAW"))
            for t in b.readers.values():
                deps.append((t, "WAR"))
        return op

    def _commit(self, op, tok, reads, writes):
        for b in writes:
            b.last_w = [tok]
            b.readers = {}
        for b in reads:
            if tok[0] == "op":
                b.readers[op.eng] = tok
            else:
                b.readers[("dma", tok[1], tok[2])] = tok

    def op(self, eng, fn, reads=(), writes=()):
        reads = [self.buf(x) if isinstance(x, str) else x for x in reads]
        writes = [self.buf(x) if isinstance(x, str) else x for x in writes]
        op = self._mkop(eng, fn, reads, writes)
        tok = ("op", op)
        self._commit(op, tok, reads, writes)
        return op

    def dma(self, eng, out, in_, reads=(), writes=(), **kw):
        reads = [self.buf(x) if isinstance(x, str) else x for x in reads]
        writes = [self.buf(x) if isinstance(x, str) else x for x in writes]

        def fn(e, out=out, in_=in_, kw=kw):
            return e.dma_start(out=out, in_=in_, **kw)

        op = self._mkop(eng, fn, reads, writes)
        k = self.dma_rr[eng]
        self.dma_rr[eng] = (k + 1) % self.ndma[eng]
        key = (eng, k)
        prev = self.dma_last.get(key)
        if prev is not None:
            op.deps.append((prev, "SEM"))
        cnt = self.dma_cnt.get(key, 0) + 1
        self.dma_cnt[key] = cnt
        tok = ("dma", key, 16 * cnt)
        self.dma_last[key] = tok
        op.dma = (key, 16 * cnt)
        self._commit(op, tok, reads, writes)
        return op

    def emit(self):
        nc = self.nc
        for op in self.ops:
            for (t, kind) in op.deps:
                if t[0] != "op":
                    continue
                p = t[1]
                if p.eng == op.eng and p.eng == "pe":
                    continue
                if kind != "WAR":
                    p.marked = True
        marked_idx = {e: [o.eidx for o in self.eops[e] if o.marked] for e in ENGS}
        resolved = {}
        for op in self.ops:
            for (t, kind) in op.deps:
                if t[0] != "op" or kind != "WAR":
                    continue
                p = t[1]
                if p.eng == op.eng and p.eng == "pe":
                    continue
                if p.marked:
                    continue
                lst = marked_idx[p.eng]
                i = bisect.bisect_left(lst, p.eidx)
                ok = False
                if i < len(lst):
                    q = self.eops[p.eng][lst[i]]
                    if q.gidx < op.gidx:
                        resolved[(id(op), id(p))] = q
                        ok = True
                if not ok:
                    p.marked = True
                    bisect.insort(lst, p.eidx)
        for e in ENGS:
            c = 0
            for o in self.eops[e]:
                if o.marked:
                    c += 1
                o.cum = c
        sems = {}
        ctx = []
        for e in ENGS:
            g = nc.semaphore("s_" + e)
            sems[e] = g.__enter__()
            ctx.append(g)
        dsem = {}
        for key in self.dma_cnt:
            g = nc.semaphore("d_%s_%d" % key)
            dsem[key] = g.__enter__()
            ctx.append(g)
        self.n_waits = 0

        def run_engine(ename, eng):
            waited = {}
            for o in self.eops[ename]:
                need = {}
                for (t, kind) in o.deps:
                    if t[0] == "op":
                        p = t[1]
                        if p.eng == ename and ename == "pe":
                            continue
                        if not p.marked:
                            p = resolved[(id(o), id(p))]
                        key = ("e", p.eng)
                        val = p.cum
                    else:
                        key = ("d", t[1])
                        val = t[2]
                    if val > need.get(key, 0):
                        need[key] = val
                for key, val in need.items():
                    if val > waited.get(key, 0):
                        waited[key] = val
                        s = sems[key[1]] if key[0] == "e" else dsem[key[1]]
                        eng.wait_ge(s, val)
                        self.n_waits += 1
                ins = o.fn(eng)
                if o.tag is not None:
                    ins.annotate(o.tag)
                if o.dma is not None:
                    ins.then_inc(dsem[o.dma[0]], 16)
                elif o.marked:
                    ins.then_inc(sems[ename], 1)
            if ename == "sp":
                for key, cnt in self.dma_cnt.items():
                    if 16 * cnt > waited.get(("d", key), 0):
                        eng.wait_ge(dsem[key], 16 * cnt)

        with nc.Block() as block:
            @block.tensor
            def _(eng):
                run_engine("pe", eng)

            @block.scalar
            def _(eng):
                run_engine("act", eng)

            @block.vector
            def _(eng):
                run_engine("dve", eng)

            @block.gpsimd
            def _(eng):
                run_engine("pool", eng)

            @block.sync
            def _(eng):
                run_engine("sp", eng)
        for g in reversed(ctx):
            g.__exit__(None, None, None)
S = 4096
D = 1024
NBLK = 8
NT = 32
KPAD = 1024
DIL = (1, 4, 16)
NORM_EPS = 1e-6
LN_EPS = 1e-5
NV = 128
V_CB, V_LG, V_LB, V_PS, V_BG, V_IS, V_CW = 0, 2, 4, 6, 8, 40, 42


class Arena:
    def __init__(self, nc, nbytes):
        self.nbytes = nbytes
        self.h16 = nc.alloc_sbuf_tensor("arena", [128, nbytes // 2], BF16)
        self.h32 = self.h16.bitcast(F32)
        self.off = 0

    def reset(self):
        self.off = 0

    def alloc(self, free_shape, dt):
        es = 2 if dt == BF16 else 4
        n = 1
        for s in free_shape:
            n *= s
        nb = (n * es + 63) // 64 * 64
        assert self.off + nb <= self.nbytes, ("arena overflow", self.off, nb, self.nbytes)
        h = self.h16 if dt == BF16 else self.h32
        ps = self.nbytes // es
        dims = [[ps, 128]]
        st = n
        for s in free_shape:
            st //= s
            dims.append([st, s])
        ap = bass.AP(h, self.off // es, dims)
        self.off += nb
        return ap


def build_program(nc, n_layers=2, dbg=False, stop=None):
    P = Prog(nc)
    skind = "ExternalOutput" if dbg else "Internal"

    def din(name, shape, dt=F32):
        return nc.dram_tensor(name, list(shape), dt, kind="ExternalInput").ap()

    L = 2
    x_d = din("x", [S, D])
    w_in_d = din("w_in", [L, D, 4352])
    w_fourier_d = din("w_fourier", [L, 256, 256])
    w_pw_d = din("w_pw", [L, 256, 256])
    w_pool_d = din("w_pool", [L, 4, 64, 64])
    w_branch_d = din("w_branch", [L, 4, 256, 1024])
    w_gate_d = din("w_gate", [L, 4, D, D])
    w_out_d = din("w_out", [L, D, D])
    gvec_h = nc.dram_tensor("gvec", [L + 1, D], F32, kind="ExternalInput")
    vecs_d = din("vecs", [L, 128, NV])
    ident_d = din("ident", [128, 128], BF16)
    masks_d = din("masks", [128, 3, 512], BF16)
    ropec_d = din("ropec", [128, S])
    ropes_d = din("ropes", [128, S])
    dft_d = din("dft", [4, 16, 128, 2, 2, 512], BF16)
    dmid_d = din("dmid", [128, 32], BF16)
    cs_d = din("cs", [128, 2, 512], BF16)
    edge_d = din("edge", [128, 2, 16])
    out_d = nc.dram_tensor("out", [S, D], F32, kind="ExternalOutput").ap()
    yg_d = nc.dram_tensor("yg_scr", [1024, S], BF16, kind=skind).ap()
    mT_d = nc.dram_tensor("mT_scr", [1024, S], BF16, kind=skind).ap()
    xmid_d = nc.dram_tensor("xmid_scr", [S, D], F32, kind=skind).ap()
    hdbg_d = nc.dram_tensor("hT_dbg", [128, 8, S], BF16, kind="ExternalOutput").ap() if dbg else None

    hT = nc.alloc_sbuf_tensor("hT", [128, 8, S], BF16)
    ident = nc.alloc_sbuf_tensor("identsb", [128, 128], BF16)
    masks = nc.alloc_sbuf_tensor("maskssb", [128, 3, 512], BF16)
    gb = nc.alloc_sbuf_tensor("gb", [128, D], F32)
    vecs = nc.alloc_sbuf_tensor("vecssb", [128, NV], F32)
    onesm = nc.alloc_sbuf_tensor("onesm", [128, 128], F32)
    epsc = nc.alloc_sbuf_tensor("epsc", [128, 1], F32)
    arena = Arena(nc, 138752)
    psall = nc.alloc_psum_tensor("psall", [128, 4096], F32)
    psall16 = psall.bitcast(BF16)
    banks = [psall[:, i * 512:(i + 1) * 512] for i in range(8)]
    banks16 = [psall16[:, i * 1024:(i + 1) * 1024] for i in range(8)]
    bank_rr = [0]

    def pb():
        i = bank_rr[0]
        bank_rr[0] = (i + 1) % 8
        return banks[i], "bank%d" % i

    def pb_pair():
        if bank_rr[0] % 2 == 1:
            bank_rr[0] = (bank_rr[0] + 1) % 8
        i = bank_rr[0]
        bank_rr[0] = (i + 2) % 8
        pair = psall[:, i * 512:(i + 2) * 512].rearrange("p (b c) -> p b c", b=2)
        return (banks[i], "bank%d" % i), (banks[i + 1], "bank%d" % (i + 1)), pair

    uid = [0]

    def nm(s):
        uid[0] += 1
        return "%s#%d" % (s, uid[0])

    class Rot:
        def __init__(self, name, n, shape, dt):
            self.tiles = [(arena.alloc(shape, dt), nm(name)) for _ in range(n)]
            self.i = 0

        def next(self):
            t = self.tiles[self.i]
            self.i = (self.i + 1) % len(self.tiles)
            return t

    def mm(out, lhsT, rhs, start, stop, reads, writes, tp=None):
        if tp is None:
            P.op("pe", lambda e: e.matmul(out, lhsT=lhsT, rhs=rhs, start=start, stop=stop), reads, writes)
        else:
            P.op("pe", lambda e: e.matmul(out, lhsT=lhsT, rhs=rhs, start=start, stop=stop, tile_position=tp), reads, writes)

    def act(out, in_, func, reads, writes, bias=None, scale=None, accum=None):
        kw = {}
        if bias is not None:
            kw["bias"] = bias
        if scale is not None:
            kw["scale"] = scale
        if accum is not None:
            kw["accum_out"] = accum
        P.op("act", lambda e: e.activation(out=out, in_=in_, func=func, **kw), reads, writes)

    def tt(eng, out, in0, in1, op, reads, writes):
        P.op(eng, lambda e: e.tensor_tensor(out=out, in0=in0, in1=in1, op=op), reads, writes)

    def ts(eng, out, in0, s1, s2, op0, op1, reads, writes):
        if s2 is None:
            P.op(eng, lambda e: e.tensor_scalar(out=out, in0=in0, scalar1=s1, scalar2=None, op0=op0), reads, writes)
        else:
            P.op(eng, lambda e: e.tensor_scalar(out=out, in0=in0, scalar1=s1, scalar2=s2, op0=op0, op1=op1), reads, writes)

    def stt(eng, out, in0, scalar, in1, op0, op1, reads, writes):
        P.op(eng, lambda e: e.scalar_tensor_tensor(out=out, in0=in0, scalar=scalar, in1=in1, op0=op0, op1=op1), reads, writes)

    def cp(eng, out, in_, reads, writes):
        if eng == "act":
            act(out, in_, AF.Copy, reads, writes)
        else:
            P.op(eng, lambda e: e.tensor_copy(out=out, in_=in_), reads, writes)

    def memset(eng, ap, val, writes):
        P.op(eng, lambda e: e.memset(ap, val), (), writes)

    def blkname(b):
        return "hT_b%d" % b

    def wcols(l, c0, n):
        return w_in_d[l].rearrange("(kc p) n -> p kc n", p=128)[:, :, c0:c0 + n]

    def load_w(ap_sb, dram_ap, name):
        P.dma("pool", ap_sb, dram_ap, writes=[name])

    P.dma("sp", ident[:, :], ident_d, writes=["ident"])
    P.dma("sp", masks[:, :, :], masks_d, writes=["masks"])
    memset("pool", onesm[:, :], 1.0 / 256.0, ["onesm"])
    memset("pool", epsc[:, :], NORM_EPS, ["epsc"])

    def load_gb(idx):
        src = bass.AP(gvec_h, idx * D, [[0, 128], [1, D]])
        P.dma("sp", gb[:, :], src, writes=["gb"])

    stat = nc.alloc_sbuf_tensor("stat4", [128, 16], F32)

    def norm_s1a(xn, xn_name, tile, rots):
        sl_ = tile % 4
        st = stat[:, sl_ * 4:sl_ * 4 + 4]
        sn = ["stat%d_%d" % (sl_, k) for k in range(4)]
        junk, jn = rots["junk"].next()
        memset("pool", st[:, 0:1], 0.0, [sn[0]])
        act(junk, xn, AF.Square, [xn_name], [jn, sn[0]], accum=st[:, 0:1])
        act(st[:, 2:3], st[:, 0:1], AF.Sqrt, [sn[0], "epsc"], [sn[2]], bias=epsc[:, 0:1], scale=1.0 / D)

    def norm_s1b(xn, xn_name, tile, rots, last):
        sl_ = tile % 4
        st = stat[:, sl_ * 4:sl_ * 4 + 4]
        sn = ["stat%d_%d" % (sl_, k) for k in range(4)]
        P.op("dve", lambda e: e.reciprocal(out=st[:, 3:4], in_=st[:, 2:3]), [sn[2]], [sn[3]])
        if last:
            o, on = rots["ofin"].next()
            stt("dve", o, xn, st[:, 3:4], gb[:, :], ALU.mult, ALU.mult, [xn_name, sn[3], "gb"], [on])
            P.dma("pool", out_d[tile * 128:(tile + 1) * 128, :], o, reads=[on])
            return None
        h, hn = rots["h"].next()
        stt("dve", h, xn, st[:, 3:4], gb[:, :], ALU.mult, ALU.mult, [xn_name, sn[3], "gb"], [hn])
        return (h, hn)

    def norm_stage2(hh, tile):
        h, hn = hh
        bk, bn = pb()
        bk16 = banks16[int(bn[4:])]
        for c in range(8):
            P.op("pe", lambda e, c=c: e.transpose(bk16[:, c * 128:(c + 1) * 128], h[:, c * 128:(c + 1) * 128], ident[:, :]),
                 [hn, "ident"], [bn])
        src = bk16[:, :].rearrange("p (c t) -> p c t", c=8)
        cp("act", hT[:, :, tile * 128:(tile + 1) * 128], src, [bn], [blkname(tile // 4)])

    NLOOK = 2

    P.barrier()
    arena.reset()
    rots = {"x": Rot("x", 4, [D], F32), "junk": Rot("junk", 2, [D], BF16), "h": Rot("h", 4, [D], BF16)}
    load_gb(0)
    pend = {}
    xts = {}
    for idx in range(NT + 3):
        if idx < NT:
            xt, xnm = rots["x"].next()
            P.dma("sp", xt, x_d[idx * 128:(idx + 1) * 128, :], writes=[xnm])
            xts[idx] = (xt, xnm)
            norm_s1a(xt, xnm, idx, rots)
        if 0 <= idx - 1 < NT:
            xt, xnm = xts.pop(idx - 1)
            pend[idx - 1] = norm_s1b(xt, xnm, idx - 1, rots, False)
        if idx - 3 >= 0:
            norm_stage2(pend.pop(idx - 3), idx - 3)
    if dbg:
        P.dma("sp", hdbg_d, hT[:, :, :], reads=[blkname(b) for b in range(NBLK)])

    if stop == 'A0':
        P.emit()
        return P
    for l in range(n_layers):
        lastl = (l == n_layers - 1)
        P.barrier()
        P.dma("sp", vecs[:, :], vecs_d[l], writes=["vecs"])

        def gate_tile(Wg, Wgn, cc, blk, gr):
            bk, bn = pb()
            for kc in range(8):
                mm(bk[:, :], Wg[:, kc, cc * 128:(cc + 1) * 128], hT[:, kc, blk * 512:(blk + 1) * 512],
                   kc == 0, kc == 7, [Wgn, blkname(blk)], [bn])
            g, gn = gr.next()
            act(g, bk[:, :], AF.Silu, [bn], [gn])
            return g, gn

        P.barrier()
        arena.reset()
        qT = arena.alloc([S], BF16)
        kT = arena.alloc([S + 2 * KPAD], BF16)
        vb = arena.alloc([48, 256], BF16)
        acc = arena.alloc([2, S], F32)
        Wqk_r = Rot("Wqk", 2, [8, 2, 128], BF16)
        Wv_r = Rot("Wv", 2, [8, 128], BF16)
        WgC = arena.alloc([8, 256], BF16)
        rope_r = Rot("rope", 2, [2, 512], F32)
        rt_r = Rot("rt", 4, [512], F32)
        qs_r = Rot("qs", 2, [2, 512], F32)
        pT_r = Rot("pT", 5, [512], BF16)
        rd_r = Rot("rd", 2, [512], F32)
        g_r = Rot("g", 2, [512], F32)
        st_r = Rot("st", 2, [512], BF16)
        cw = {}

        def c_load(hp_, g_):
            Wqk_, Wqkn_ = Wqk_r.next()
            Wv_, Wvn_ = Wv_r.next()
            load_w(Wqk_[:, :, 0, :], wcols(l, 768 + g_ * 256 + hp_ * 128, 128), Wqkn_ + "_0")
            load_w(Wqk_[:, :, 1, :], wcols(l, 1536 + g_ * 256 + hp_ * 128, 128), Wqkn_ + "_2")
            load_w(Wv_, wcols(l, 2304 + g_ * 256 + hp_ * 128, 128), Wvn_)
            cw[(hp_, g_)] = (Wqk_, Wqkn_, Wv_, Wvn_)

        corder = [(hp_, g_) for hp_ in range(2) for g_ in range(3)]
        c_load(0, 0)
        load_w(WgC, wcols(l, 3328 + 2 * 256, 256), "WgC")
        memset("dve", kT[:, 0:KPAD], 0.0, ["kTpadL"])
        memset("dve", kT[:, KPAD + S:KPAD + S + KPAD], 0.0, ["kTpadR"])
        memset("dve", vb[:, :, :], 0.0, ["vb"])
        memset("dve", vb[:, :, 64:192], 1.0, ["vb"])
        def finalize(hp):
            for blk in range(NBLK):
                sl = slice(blk * 512, (blk + 1) * 512)
                an = "acc_b%d" % blk
                rd, rdn = rd_r.next()
                act(rd[0:64, :], acc[64:128, 0, sl], AF.Ln, [an], [rdn])
                act(rd[64:128, :], acc[0:64, 1, sl], AF.Ln, [an], [rdn])
                act(rd, rd, AF.Exp, [rdn], [rdn], scale=-1.0)
                tt("pool", acc[0:64, 0, sl], acc[0:64, 0, sl], rd[0:64, :], ALU.mult, [an, rdn], [an])
                tt("pool", acc[64:128, 0, sl], acc[64:128, 1, sl], rd[64:128, :], ALU.mult, [an, rdn], [an])
            for blk in range(NBLK):
                sl = slice(blk * 512, (blk + 1) * 512)
                gt, gtn = gate_tile(WgC, "WgC", hp, blk, g_r)
                stg, stn = st_r.next()
                tt("dve", stg, acc[:, 0, sl], gt, ALU.mult, ["acc_b%d" % blk, gtn], [stn])
                P.dma("pool", yg_d[512 + hp * 128:512 + hp * 128 + 128, sl], stg, reads=[stn], writes=["yg_%d_%d" % (4 + hp, blk)])
        fin_pending = []
        for hp in range(2):
            for g in range(3):
                dil = DIL[g]
                Lg = S // dil
                ntl = Lg // 128
                nch = ntl + 1
                ci_ = corder.index((hp, g))
                if ci_ + 1 < len(corder):
                    c_load(*corder[ci_ + 1])
                Wqk, Wqkn, Wv, Wvn = cw.pop((hp, g))
                for blk in range(NBLK):
                    rp, rpn = rope_r.next()
                    P.dma("sp", rp[:, 0, :], ropec_d[:, blk * 512:(blk + 1) * 512], writes=[rpn + "c"])
                    P.dma("sp", rp[:, 1, :], ropes_d[:, blk * 512:(blk + 1) * 512], writes=[rpn + "s"])
                    (bq, bqn), (bkk, bkn), pair = pb_pair()
                    for qk, (bk, bn) in enumerate(((bq, bqn), (bkk, bkn))):
                        for kc in range(8):
                            mm(bk[:, :], Wqk[:, kc, qk, :], hT[:, kc, blk * 512:(blk + 1) * 512],
                               kc == 0, kc == 7, [Wqkn + "_%d" % (2 * qk), blkname(blk)], [bn])
                    qs, qsn = qs_r.next()
                    for (src, dst) in ((32, 0), (0, 32), (96, 64), (64, 96)):
                        cp("act", qs[dst:dst + 32, :, :], pair[src:src + 32, :, :], [bqn, bkn], [qsn])
                    for qk, (bk, bn) in enumerate(((bq, bqn), (bkk, bkn))):
                        t1, t1n = rt_r.next()
                        t2, t2n = rt_r.next()
                        tt("dve", t1, bk[:, :], rp[:, 0, :], ALU.mult, [bn, rpn + "c"], [t1n])
                        tt("dve", t2, qs[:, qk, :], rp[:, 1, :], ALU.mult, [qsn, rpn + "s"], [t2n])
                        if qk == 0:
                            tt("pool", qT[:, blk * 512:(blk + 1) * 512], t1, t2, ALU.add, [t1n, t2n], ["qT_b%d" % blk])
                        else:
                            tt("dve", kT[:, KPAD + blk * 512:KPAD + (blk + 1) * 512], t1, t2, ALU.add, [t1n, t2n], ["kT_b%d" % blk])
                if stop == 'C1':
                    P.emit()
                    return P
                allq = ["qT_b%d" % b for b in range(NBLK)]
                allk = ["kT_b%d" % b for b in range(NBLK)] + ["kTpadL", "kTpadR"]
                allh = [blkname(b) for b in range(NBLK)]
                for r in range(dil):
                    for ci in range(nch):
                        ch = r * nch + ci
                        p0 = ci * 128 - 64
                        lo = max(p0, 0)
                        hi = min(p0 + 128, Lg)
                        npos = hi - lo
                        prow = lo - p0
                        t0 = r + dil * lo
                        bk, bn = pb()
                        for kc in range(8):
                            lhsT = hT[:, kc, t0:t0 + dil * (npos - 1) + 1:dil]
                            if prow == 0:
                                mm(bk[0:npos, 0:128], lhsT, Wv[:, kc, :], kc == 0, kc == 7, [Wvn] + allh, [bn])
                            else:
                                mm(bk[64:128, 0:128], lhsT, Wv[:, kc, :], kc == 0, kc == 7, [Wvn] + allh, [bn], tp=(0, 64))
                        dst = bass.AP(vb.tensor, vb.offset + prow * vb.ap[0][0] + ch * 256, [[vb.ap[0][0], npos], [192, 2], [1, 64]])
                        srcv = bk[prow:prow + npos, 0:128].rearrange("p (h d) -> p h d", h=2)
                        cp("act", dst, srcv, [bn], ["vb"])
                if stop == 'C2':
                    P.emit()
                    return P
                if g == 0 and fin_pending:
                    finalize(fin_pending.pop())
                tiles = [(r, ti) for r in range(dil) for ti in range(ntl)]
                LOOK = 3
                staged = {}

                def stage_a(idx):
                    r, ti = tiles[idx]
                    variant = 1 if ti == 0 else (2 if ti == ntl - 1 else 0)
                    (s0, s0n), (s1, s1n), spair = pb_pair()
                    qa = r + dil * 128 * ti
                    for kc in range(2):
                        ka = KPAD + r + dil * (128 * ti - 64 + 128 * kc)
                        for hh, (sb_, sbn_) in enumerate(((s0, s0n), (s1, s1n))):
                            rows = slice(hh * 64, hh * 64 + 64)
                            mm(sb_[:, kc * 128:(kc + 1) * 128], kT[rows, ka:ka + dil * 127 + 1:dil],
                               qT[rows, qa:qa + dil * 127 + 1:dil], True, True, allq + allk, [sbn_])
                    pT, pTn = pT_r.next()
                    act(pT.rearrange("p (h c) -> p h c", h=2), spair[:, :, 0:256], AF.Exp, [s0n, s1n], [pTn], scale=0.125)
                    tt("dve", pT, pT, masks[:, variant, :], ALU.mult, [pTn, "masks"], [pTn])
                    staged[idx] = (pT, pTn)

                def stage_b(idx):
                    r, ti = tiles[idx]
                    pT, pTn = staged.pop(idx)
                    obk, obn = pb()
                    for hh in range(2):
                        for kc in range(2):
                            ch = r * nch + ti + kc
                            bi = 2 * hh + kc
                            mm(obk[:, hh * 128:(hh + 1) * 128], vb[:, ch, hh * 128:(hh + 1) * 128],
                               pT[:, bi * 128:(bi + 1) * 128], kc == 0, kc == 1, ["vb", pTn], [obn])
                    qa = r + dil * 128 * ti
                    dst = acc[:, :, qa:qa + dil * 127 + 1:dil]
                    srco = obk[:, 0:256].rearrange("p (h t) -> p h t", h=2)
                    bset = sorted(set([(qa) // 512, (qa + dil * 127) // 512]))
                    accn = ["acc_b%d" % b for b in range(bset[0], bset[-1] + 1)]
                    if g == 0:
                        cp("dve", dst, srco, [obn], accn)
                    else:
                        tt("dve", dst, srco, dst, ALU.add, [obn] + accn, accn)

                for idx in range(len(tiles) + LOOK):
                    if idx < len(tiles):
                        stage_a(idx)
                    if idx - LOOK >= 0:
                        stage_b(idx - LOOK)
                if g == 2:
                    fin_pending.append(hp)
        while fin_pending:
            finalize(fin_pending.pop())
        if stop == 'C':
            P.emit()
            return P
        P.barrier()
        arena.reset()
        Wa = arena.alloc([8, 256], BF16)
        WgA = arena.alloc([8, 256], BF16)
        cs = arena.alloc([2, 512], BF16)
        wf = arena.alloc([2, 256], BF16)
        uaT = arena.alloc([2, S], BF16)
        PQ = arena.alloc([32, 512], BF16)
        fT = arena.alloc([2, S], BF16)
        dft_r = Rot("dft", 3, [2, 2, 512], BF16)
        sq_r = Rot("sq", 2, [512], F32)
        dmid = arena.alloc([32], BF16)
        P.dma("sp", dmid, dmid_d, writes=["dmid"])
        g_r = Rot("g", 2, [512], F32)
        st_r = Rot("st", 2, [512], BF16)
        load_w(Wa, wcols(l, 0, 256), "Wa")
        load_w(WgA, wcols(l, 3328, 256), "WgA")
        load_w(wf, w_fourier_d[l].rearrange("(c p) n -> p c n", p=128), "wf")
        P.dma("sp", cs, cs_d, writes=["cs"])
        for blk in range(NBLK):
            for c in range(2):
                bk, bn = pb()
                for kc in range(8):
                    mm(bk[:, :], Wa[:, kc, c * 128:(c + 1) * 128], hT[:, kc, blk * 512:(blk + 1) * 512],
                       kc == 0, kc == 7, ["Wa", blkname(blk)], [bn])
                cp("act" if c == 0 else "dve", uaT[:, c, blk * 512:(blk + 1) * 512], bk[:, :], [bn], ["uaT_b%d" % blk])
        for a in range(NT):
            bk, bn = pb()
            for c in range(2):
                mm(bk[:, :], uaT[:, c, a * 128:(a + 1) * 128], cs[:, c, :], c == 0, c == 1, ["uaT_b%d" % (a // 4), "cs"], [bn])
            cp("act" if a % 2 == 0 else "dve", PQ[:, a, :], bk[:, :], [bn], ["PQ_%d" % a])
        for ps_ in range(4):
            bC = [pb() for c in range(2)]
            bS = [pb() for c in range(2)]
            for a2 in range(NT // 2):
                dt_, dtn = dft_r.next()
                P.dma("sp", dt_, dft_d[ps_, a2], writes=[dtn])
                for ai in range(2):
                    a = 2 * a2 + ai
                    for c in range(2):
                        mm(bC[c][0][:, :], PQ[:, a, c * 128:(c + 1) * 128], dt_[:, ai, 0, :],
                           a == 0, a == NT - 1, ["PQ_%d" % a, dtn], [bC[c][1]])
                        mm(bS[c][0][:, :], PQ[:, a, 256 + c * 128:256 + (c + 1) * 128], dt_[:, ai, 1, :],
                           a == 0, a == NT - 1, ["PQ_%d" % a, dtn], [bS[c][1]])
            for c in range(2):
                sq, sqn = sq_r.next()
                cp("act", sq, bS[c][0][:, :], [bS[c][1]], [sqn])
                k0 = ps_ * 512
                tt("dve", fT[:, c, k0:k0 + 512], bC[c][0][:, :], sq, ALU.subtract, [bC[c][1], sqn], ["fT_b%d" % ps_])
                j0 = 1 if ps_ == 0 else 0
                fv = fT[:, c, :]
                rev = bass.AP(fv.tensor, fv.offset + (S - k0 - j0), [[fv.ap[0][0], 128], [-1, 512 - j0]])
                hin = ["fT_b%d" % (7 - ps_)] + (["fT_b%d" % (8 - ps_)] if ps_ >= 1 else [])
                tt("dve", rev, bC[c][0][:, j0:512], sq[:, j0:512], ALU.add, [bC[c][1], sqn], hin)
        bm_, bmn_ = pb()
        for c in range(2):
            for a in range(NT):
                mm(bm_[:, c:c + 1], PQ[:, a, c * 128:(c + 1) * 128], dmid[:, a:a + 1], a == 0, a == NT - 1, ["PQ_%d" % a, "dmid"], [bmn_])
        for c in range(2):
            cp("act", fT[:, c, S // 2:S // 2 + 1], bm_[:, c:c + 1], [bmn_], ["fT_b4"])
        for blk in range(NBLK):
            sl = slice(blk * 512, (blk + 1) * 512)
            for co in range(2):
                bk, bn = pb()
                for c in range(2):
                    mm(bk[:, :], wf[:, c, co * 128:(co + 1) * 128], fT[:, c, sl], c == 0, c == 1, ["wf", "fT_b%d" % blk], [bn])
                gt, gtn = gate_tile(WgA, "WgA", co, blk, g_r)
                stg, stn = st_r.next()
                tt("dve", stg, bk[:, :], gt, ALU.mult, [bn, gtn], [stn])
                P.dma("pool", yg_d[co * 128:(co + 1) * 128, sl], stg, reads=[stn], writes=["yg_%d_%d" % (co, blk)])

        if stop == 'A':
            P.emit()
            return P
        P.barrier()
        arena.reset()
        Wb = arena.alloc([8, 512], BF16)
        WgB = arena.alloc([8, 256], BF16)
        wpw = arena.alloc([2, 256], BF16)
        dg = arena.alloc([2, 31, 128], BF16)
        uT = arena.alloc([2, S + 32], BF16)
        sg_r = Rot("sg", 2, [512], F32)
        y_r = Rot("y", 3, [2, 512], F32)
        ysq_r = Rot("ysq", 3, [2, 512], F32)
        tmp_r = Rot("ctmp", 8, [512], F32)
        s_r = Rot("s", 3, [2, 512], BF16)
        g_r = Rot("g", 2, [512], F32)
        st_r = Rot("st", 2, [512], BF16)
        load_w(Wb, wcols(l, 256, 512), "Wb")
        load_w(WgB, wcols(l, 3328 + 256, 256), "WgB")
        load_w(wpw, w_pw_d[l].rearrange("(c p) n -> p c n", p=128), "wpw")
        for c in range(2):
            for k in range(31):
                ts("dve", dg[:, c, k, :], ident[:, :], vecs[:, V_CW + c * 31 + k:V_CW + c * 31 + k + 1], None, ALU.mult, None,
                   ["ident", "vecs"], ["dg_%d_%d" % (c, k)])
        memset("pool", uT[:, :, 0:16], 0.0, ["uTpadL"])
        memset("pool", uT[:, :, 16 + S:32 + S], 0.0, ["uTpadR"])
        for blk in range(NBLK):
            for c in range(2):
                bka, bna = pb()
                bkg, bng = pb()
                for kc in range(8):
                    mm(bka[:, :], Wb[:, kc, c * 128:(c + 1) * 128], hT[:, kc, blk * 512:(blk + 1) * 512],
                       kc == 0, kc == 7, ["Wb", blkname(blk)], [bna])
                for kc in range(8):
                    mm(bkg[:, :], Wb[:, kc, 256 + c * 128:256 + (c + 1) * 128], hT[:, kc, blk * 512:(blk + 1) * 512],
                       kc == 0, kc == 7, ["Wb", blkname(blk)], [bng])
                sg, sgn = sg_r.next()
                act(sg, bkg[:, :], AF.Sigmoid, [bng], [sgn])
                tt("dve", uT[:, c, 16 + blk * 512:16 + (blk + 1) * 512], bka[:, :], sg, ALU.mult, [bna, sgn], ["uT_b%d" % blk])
        cst = {}

        def b_s1(blk):
            urd = ["uT_b%d" % b for b in range(max(blk - 1, 0), min(blk + 1, NBLK - 1) + 1)] + ["uTpadL", "uTpadR"]
            y, yn = y_r.next()
            ysq, ysqn = ysq_r.next()
            for c in range(2):
                bk, bn = pb()
                for k in range(31):
                    o = 16 + blk * 512 + k - 15
                    mm(bk[:, :], dg[:, c, k, :], uT[:, c, o:o + 512], k == 0, k == 30, ["dg_%d_%d" % (c, k)] + urd, [bn])
                act(y[:, c, :], bk[:, :], AF.Identity, [bn, "vecs"], [yn], bias=vecs[:, V_CB + c:V_CB + c + 1])
                act(ysq[:, c, :], bk[:, :], AF.Square, [bn, "vecs"], [ysqn], bias=vecs[:, V_CB + c:V_CB + c + 1])
            cst[blk] = [y, yn, ysq, ysqn]

        def b_s2(blk):
            y, yn, ysq, ysqn = cst[blk]
            bm, bmn = pb()
            bs, bsn = pb()
            for c in range(2):
                mm(bm[:, :], onesm[:, :], y[:, c, :], c == 0, c == 1, ["onesm", yn], [bmn])
            for c in range(2):
                mm(bs[:, :], onesm[:, :], ysq[:, c, :], c == 0, c == 1, ["onesm", ysqn], [bsn])
            msq, msqn = tmp_r.next()
            act(msq, bm[:, :], AF.Square, [bmn], [msqn])
            var, varn = tmp_r.next()
            tt("dve", var, bs[:, :], msq, ALU.subtract, [bsn, msqn], [varn])
            ts("dve", var, var, LN_EPS, None, ALU.add, None, [varn], [varn])
            act(var, var, AF.Sqrt, [varn], [varn])
            P.op("dve", lambda e, var=var: e.reciprocal(out=var, in_=var), [varn], [varn])
            s_, sn = s_r.next()
            for c in range(2):
                d_, dn = tmp_r.next()
                tt("dve", d_, y[:, c, :], bm[:, :], ALU.subtract, [yn, bmn], [dn])
                tt("pool", d_, d_, var, ALU.mult, [dn, varn], [dn])
                act(s_[:, c, :], d_, AF.Silu, [dn, "vecs"], [sn], bias=vecs[:, V_LB + c:V_LB + c + 1], scale=vecs[:, V_LG + c:V_LG + c + 1])
            cst[blk] = [s_, sn]

        def b_s3(blk):
            sl = slice(blk * 512, (blk + 1) * 512)
            s_, sn = cst.pop(blk)
            for co in range(2):
                bk, bn = pb()
                for c in range(2):
                    mm(bk[:, :], wpw[:, c, co * 128:(co + 1) * 128], s_[:, c, :], c == 0, c == 1, ["wpw", sn], [bn])
                gt, gtn = gate_tile(WgB, "WgB", co, blk, g_r)
                stg, stn = st_r.next()
                tt("dve", stg, bk[:, :], gt, ALU.mult, [bn, gtn], [stn])
                P.dma("pool", yg_d[256 + co * 128:256 + (co + 1) * 128, sl], stg, reads=[stn], writes=["yg_%d_%d" % (2 + co, blk)])

        for i in range(NBLK + 2):
            if i < NBLK:
                b_s1(i)
            if 0 <= i - 1 < NBLK:
                b_s2(i - 1)
            if i - 2 >= 0:
                b_s3(i - 2)

        if stop == 'B':
            P.emit()
            return P
        P.barrier()
        arena.reset()
        Wd = arena.alloc([8, 256], BF16)
        WgD = arena.alloc([8, 256], BF16)
        wpb = arena.alloc([2, 128], BF16)
        ud = arena.alloc([2, S + 16], F32)
        bA = arena.alloc([S + 16], F32)
        bB = arena.alloc([S + 16], F32)
        pooled = arena.alloc([2, S], BF16)
        edge = arena.alloc([2, 16], F32)
        etmp = arena.alloc([16], F32)
        po_r = Rot("po", 2, [512], F32)
        g_r = Rot("g", 2, [512], F32)
        st_r = Rot("st", 2, [512], BF16)
        load_w(Wd, wcols(l, 3072, 256), "Wd")
        load_w(WgD, wcols(l, 3328 + 768, 256), "WgD")
        memset("pool", wpb[:, :, :], 0.0, ["wpb"])
        for gi in range(4):
            r0 = (gi % 2) * 64
            P.dma("pool", wpb[r0:r0 + 64, gi // 2, r0:r0 + 64], w_pool_d[l, gi], reads=["wpb"], writes=["wpb_%d" % gi])
        P.dma("sp", edge, edge_d, writes=["edge"])
        memset("pool", ud[:, :, 0:8], 0.0, ["udpadL"])
        memset("pool", ud[:, :, 8 + S:16 + S], 0.0, ["udpadR"])
        for c in range(2):
            for blk in range(NBLK):
                bk, bn = pb()
                for kc in range(8):
                    mm(bk[:, :], Wd[:, kc, c * 128:(c + 1) * 128], hT[:, kc, blk * 512:(blk + 1) * 512],
                       kc == 0, kc == 7, ["Wd", blkname(blk)], [bn])
                cp("act", ud[:, c, 8 + blk * 512:8 + (blk + 1) * 512], bk[:, :], [bn], ["ud_%d" % c])
        NP_ = S + 16
        for c in range(2):
            u = ud[:, c, :]
            tt("dve", bA[:, 1:NP_], u[:, 0:NP_ - 1], u[:, 1:NP_], ALU.add, ["ud_%d" % c, "udpadL", "udpadR"], ["bA"])
            tt("dve", bB[:, 2:NP_ - 1], bA[:, 1:NP_ - 2], bA[:, 3:NP_], ALU.add, ["bA"], ["bB"])
            if c == 1:
                tt("dve", bA[:, 4:NP_ - 4], bB[:, 2:NP_ - 6], bB[:, 6:NP_ - 2], ALU.add, ["bB"], ["bA"])
                tt("dve", bB[:, 8:NP_ - 8], bA[:, 4:NP_ - 12], bA[:, 12:NP_ - 4], ALU.add, ["bA"], ["bB"])
            for half in range(2):
                rows = slice(half * 64, half * 64 + 64)
                W_ = bA if half == 0 else bB
                stt("dve", pooled[rows, c, :], W_[rows, 8:8 + S], vecs[rows, V_IS + c:V_IS + c + 1], u[rows, 8:8 + S],
                    ALU.mult, ALU.subtract, ["bA", "bB", "ud_%d" % c, "vecs"], ["pooled_%d" % c])
                for e0, t0 in ((0, 0), (8, S - 8)):
                    tt("dve", etmp[rows, e0:e0 + 8], W_[rows, 8 + t0:16 + t0], edge[rows, c, e0:e0 + 8], ALU.mult,
                       ["bA", "bB", "edge"], ["etmp"])
                    tt("dve", pooled[rows, c, t0:t0 + 8], etmp[rows, e0:e0 + 8], u[rows, 8 + t0:16 + t0], ALU.subtract,
                       ["etmp", "ud_%d" % c], ["pooled_%d" % c])
        for co in range(2):
            for blk in range(NBLK):
                sl = slice(blk * 512, (blk + 1) * 512)
                bk, bn = pb()
                mm(bk[:, :], wpb[:, co, :], pooled[:, co, sl], True, True, ["wpb", "wpb_0", "wpb_1", "wpb_2", "wpb_3", "pooled_%d" % co], [bn])
                gt, gtn = gate_tile(WgD, "WgD", co, blk, g_r)
                stg, stn = st_r.next()
                po, pon = po_r.next()
                act(po, bk[:, :], AF.Copy, [bn, "vecs"], [pon], scale=vecs[:, V_PS + co:V_PS + co + 1])
                tt("pool", stg, po, gt, ALU.mult, [pon, gtn], [stn])
                P.dma("pool", yg_d[768 + co * 128:768 + (co + 1) * 128, sl], stg, reads=[stn], writes=["yg_%d_%d" % (6 + co, blk)])

        if stop == 'D':
            P.emit()
            return P
        P.barrier()
        arena.reset()
        ygT = arena.alloc([8, S], BF16)
        Wgt_r = Rot("Wgt", 3, [4, 8, 128], BF16)
        Wbr_r = Rot("Wbr", 3, [4, 2, 128], BF16)
        mst_r = Rot("mst", 2, [S], BF16)
        mg_r = Rot("mg", 2, [512], F32)
        mt_r = Rot("mtmp", 3, [512], F32)
        mac_r = Rot("macc", 2, [512], F32)
        for c8 in range(8):
            P.dma("sp", ygT[:, c8, :], yg_d[c8 * 128:(c8 + 1) * 128, :],
                  reads=["yg_%d_%d" % (c8, b) for b in range(NBLK)], writes=["ygT_%d" % c8])
        wslots = {}

        def m_load(j):
            Wgt, Wgtn = Wgt_r.next()
            Wbr, Wbrn = Wbr_r.next()
            for n in range(4):
                load_w(Wgt[:, n, :, :], w_gate_d[l, n].rearrange("(kc p) m -> p kc m", p=128)[:, :, j * 128:(j + 1) * 128], Wgtn + "_%d" % n)
                load_w(Wbr[:, n, :, :], w_branch_d[l, n].rearrange("(c p) m -> p c m", p=128)[:, :, j * 128:(j + 1) * 128], Wbrn + "_%d" % n)
            wslots[j] = (Wgt, Wgtn, Wbr, Wbrn)

        m_load(0)
        for j in range(8):
            if j + 1 < 8:
                m_load(j + 1)
            Wgt, Wgtn, Wbr, Wbrn = wslots.pop(j)
            mst, mstn = mst_r.next()
            for blk in range(NBLK):
                sl = slice(blk * 512, (blk + 1) * 512)
                macc, maccn = mac_r.next()
                for n in range(4):
                    bg, bgn = pb()
                    by, byn = pb()
                    for kc in range(8):
                        mm(bg[:, :], Wgt[:, n, kc, :], hT[:, kc, sl], kc == 0, kc == 7, [Wgtn + "_%d" % n, blkname(blk)], [bgn])
                    for c in range(2):
                        mm(by[:, :], Wbr[:, n, c, :], ygT[:, 2 * n + c, sl], c == 0, c == 1, [Wbrn + "_%d" % n, "ygT_%d" % (2 * n + c)], [byn])
                    mg, mgn = mg_r.next()
                    act(mg, bg[:, :], AF.Sigmoid, [bgn, "vecs"], [mgn], bias=vecs[:, V_BG + n * 8 + j:V_BG + n * 8 + j + 1])
                    if n == 0:
                        tt("dve", macc, by[:, :], mg, ALU.mult, [byn, mgn], [maccn])
                    else:
                        tmp, tmpn = mt_r.next()
                        tt("dve", tmp, by[:, :], mg, ALU.mult, [byn, mgn], [tmpn])
                        if n < 3:
                            tt("pool", macc, macc, tmp, ALU.add, [maccn, tmpn], [maccn])
                        else:
                            tt("pool", mst[:, sl], macc, tmp, ALU.add, [maccn, tmpn], [mstn])
            P.dma("pool", mT_d[j * 128:(j + 1) * 128, :], mst, reads=[mstn], writes=["mT_%d" % j])

        if stop == 'M':
            P.emit()
            return P
        P.barrier()
        arena.reset()
        wo = arena.alloc([8, D], BF16)
        mTb_r = Rot("mTb", 2, [8, 512], BF16)
        rots = {"x": Rot("x", 3, [D], F32), "xn": Rot("xn", 4, [D], F32), "junk": Rot("junk", 2, [D], BF16),
                "h": Rot("h", 4, [D], BF16), "ofin": Rot("ofin", 2, [D], F32)}
        load_w(wo, w_out_d[l].rearrange("(jc p) m -> p jc m", p=128), "wo")
        load_gb(l + 1)
        xsrc = x_d if l == 0 else xmid_d
        pend = {}
        cur = {}

        def e_stage1(tile):
            blk, t4 = tile // 4, tile % 4
            if t4 == 0:
                mTb, mTbn = mTb_r.next()
                P.dma("sp", mTb, mT_d.rearrange("(jc p) t -> p jc t", p=128)[:, :, blk * 512:(blk + 1) * 512],
                      reads=["mT_%d" % j for j in range(8)], writes=[mTbn])
                cur["m"] = (mTb, mTbn)
            mTb, mTbn = cur["m"]
            xt, xnm = rots["x"].next()
            rd_ = ["xmid_%d" % tile] if l > 0 else []
            P.dma("sp", xt, xsrc[tile * 128:(tile + 1) * 128, :], reads=rd_, writes=[xnm])
            hb = []
            for half in range(2):
                bk, bn = pb()
                for jc in range(8):
                    mm(bk[:, :], mTb[:, jc, t4 * 128:(t4 + 1) * 128], wo[:, jc, half * 512:(half + 1) * 512],
                       jc == 0, jc == 7, [mTbn, "wo"], [bn])
                hb.append((bk, bn))
            xn, xnn = rots["xn"].next()
            for half in range(2):
                tt("dve", xn[:, half * 512:(half + 1) * 512], xt[:, half * 512:(half + 1) * 512], hb[half][0][:, :], ALU.add,
                   [xnm, hb[half][1]], [xnn])
            if (not lastl) or dbg:
                P.dma("pool", xmid_d[tile * 128:(tile + 1) * 128, :], xn, reads=[xnn], writes=["xmid_%d" % tile])
            norm_s1a(xn, xnn, tile, rots)
            return (xn, xnn)

        xns = {}
        for idx in range(NT + 3):
            if idx < NT:
                xns[idx] = e_stage1(idx)
            if 0 <= idx - 1 < NT:
                xn, xnn = xns.pop(idx - 1)
                pend[idx - 1] = norm_s1b(xn, xnn, idx - 1, rots, lastl)
            if idx - 3 >= 0:
                hh = pend.pop(idx - 3)
                if hh is not None:
                    norm_stage2(hh, idx - 3)
    P.emit()
    return P

import ml_dtypes as _mld

_BF = _mld.bfloat16
_CONST_CACHE = {}


def _constants():
    if _CONST_CACHE:
        return _CONST_CACHE
    c = {}
    c["ident"] = np.eye(128, dtype=np.float32).astype(_BF)
    p = np.arange(128)[:, None]
    n = np.arange(128)[None, :]
    M0 = (n <= p).astype(np.float32)
    M1 = (n >= p).astype(np.float32)
    M0f = M0 * (p >= 64)
    M1l = M1 * (p < 64)
    mk = np.zeros((128, 3, 512), np.float32)
    for v, (a, b) in enumerate(((M0, M1), (M0f, M1), (M0, M1l))):
        mk[:, v, :] = np.concatenate([a, b, a, b], axis=1)
    c["masks"] = mk.astype(_BF)
    inv = (1.0 / (np.float32(10000.0) ** (np.arange(0, 64, 2, dtype=np.float32) / np.float32(64)))).astype(np.float32)
    ang = (np.arange(S, dtype=np.float32)[None, :] * inv[:, None]).astype(np.float32)
    cosv = np.cos(ang).astype(np.float32)
    sinv = np.sin(ang).astype(np.float32)
    rc = np.zeros((128, S), np.float32)
    rs = np.zeros((128, S), np.float32)
    for q in range(4):
        rc[q * 32:(q + 1) * 32] = cosv
        rs[q * 32:(q + 1) * 32] = sinv if (q % 2 == 1) else -sinv
    c["ropec"] = rc
    c["ropes"] = rs
    s_idx = np.arange(S, dtype=np.int64)[:, None]
    k_idx = np.arange(S // 2, dtype=np.int64)[None, :]
    ph = ((s_idx * k_idx) % S).astype(np.float64) * (2.0 * np.pi / S)
    Cm = (np.cos(ph) / 512.0).astype(np.float32).reshape(16, 2, 128, 4, 512)
    Sm = (np.sin(ph) / 512.0).astype(np.float32).reshape(16, 2, 128, 4, 512)
    del ph
    dft = np.empty((4, 16, 128, 2, 2, 512), dtype=_BF)
    dft[:, :, :, :, 0, :] = Cm.transpose(3, 0, 2, 1, 4).astype(_BF)
    dft[:, :, :, :, 1, :] = Sm.transpose(3, 0, 2, 1, 4).astype(_BF)
    c["dft"] = dft
    sgn = np.where((np.arange(S) % 2) == 0, 1.0, -1.0).astype(np.float32) / 512.0
    c["dmid"] = np.ascontiguousarray(sgn.reshape(32, 128).T).astype(_BF)
    cm = np.arange(64)
    phc = ((cm[:, None] * cm[None, :]) % 64).astype(np.float64) * (2.0 * np.pi / 64)
    cs = np.zeros((256, 512), np.float32)
    for gi in range(4):
        cs[gi * 64:(gi + 1) * 64, gi * 64:(gi + 1) * 64] = np.cos(phc)
        cs[gi * 64:(gi + 1) * 64, 256 + gi * 64:256 + (gi + 1) * 64] = np.sin(phc)
    c["cs"] = np.ascontiguousarray(cs.reshape(2, 128, 512).transpose(1, 0, 2)).astype(_BF)
    sizes = (2, 4, 8, 16)
    inv_s = np.zeros((128, 2), np.float32)
    edge = np.zeros((128, 2, 16), np.float32)
    for gi, sz in enumerate(sizes):
        rows = slice((gi % 2) * 64, (gi % 2) * 64 + 64)
        cch = gi // 2
        inv_s[rows, cch] = 1.0 / sz
        for e in range(16):
            t = e if e < 8 else S - 16 + e
            lo = max(t - sz // 2, 0)
            hi = min(t + sz - 1 - sz // 2, S - 1)
            edge[rows, cch, e] = 1.0 / float(hi - lo + 1)
    c["inv_s"] = inv_s
    c["edge"] = edge
    perm = np.zeros(1536, np.int64)
    for j in range(1536):
        base = (j // 64) * 64
        d = j % 64
        perm[j] = base + (d + 32 if d < 32 else d - 32)
    c["perm"] = perm
    _CONST_CACHE.update(c)
    return c


def _prep_inputs(inp):
    c = _constants()
    f = lambda a: np.ascontiguousarray(np.asarray(a, dtype=np.float32))
    w_in = f(inp["w_in"])
    L = w_in.shape[0]
    vecs = np.zeros((L, 128, NV), np.float32)

    def pc(v):
        return f(v).reshape(L, 2, 128).transpose(0, 2, 1)

    vecs[:, :, V_CB:V_CB + 2] = pc(inp["conv_b"])
    vecs[:, :, V_LG:V_LG + 2] = pc(inp["conv_ln_g"])
    vecs[:, :, V_LB:V_LB + 2] = pc(inp["conv_ln_b"])
    vecs[:, :, V_PS:V_PS + 2] = pc(inp["pool_scale"])
    bg = f(inp["b_gate"]).reshape(L, 4, 8, 128).transpose(0, 3, 1, 2).reshape(L, 128, 32)
    vecs[:, :, V_BG:V_BG + 32] = bg
    vecs[:, :, V_IS:V_IS + 2] = c["inv_s"][None]
    cw = f(inp["conv_w"]).reshape(L, 31, 2, 128).transpose(0, 3, 2, 1).reshape(L, 128, 62)
    vecs[:, :, V_CW:V_CW + 62] = cw
    gvec = np.concatenate([f(inp["norm_g"]), f(inp["final_g"])[None, :]], axis=0)
    shared = {
        "w_in": w_in, "w_fourier": f(inp["w_fourier"]), "w_pw": f(inp["w_pw"]),
        "w_pool": f(inp["w_pool"]), "w_branch": f(inp["w_branch"]), "w_gate": f(inp["w_gate"]),
        "w_out": f(inp["w_out"]), "gvec": np.ascontiguousarray(gvec), "vecs": vecs,
        "ident": c["ident"], "masks": c["masks"], "ropec": c["ropec"], "ropes": c["ropes"],
        "dft": c["dft"], "dmid": c["dmid"], "cs": c["cs"], "edge": c["edge"],
    }
    return shared


_NC_CACHE = {}


def _get_nc(n_layers=2, dbg=False, stop=None):
    key = (n_layers, dbg, stop)
    if key not in _NC_CACHE:
        nc = bass.Bass("TRN2", target_bir_lowering=False)
        build_program(nc, n_layers=n_layers, dbg=dbg, stop=stop)
        _NC_CACHE[key] = nc
    return _NC_CACHE[key]


def kernel(**inputs):
    from concourse.bass_utils import run_bass_kernel_spmd
    x = np.ascontiguousarray(np.asarray(inputs["x"], dtype=np.float32))
    B = x.shape[0]
    shared = _prep_inputs(inputs)
    nc = _get_nc(2, False)
    in_maps = []
    for b in range(B):
        m = dict(shared)
        m["x"] = x[b]
        in_maps.append(m)
    res = run_bass_kernel_spmd(nc, in_maps, core_ids=list(range(B)))
    out = np.stack([np.asarray(r["out"], dtype=np.float32) for r in res.results], axis=0)
    return out
```
